# Optimizing a Trainium2 kernel written in Bass

```python
import math
import jax, jax.numpy as jnp
from jax import lax
import numpy as np

D_MODEL = 1024
BATCH = 8
SEQ = 2048
DEPTH = 1

D_FF = 2816
HEAD_DIM = 64
NSA_HEADS = 8
NSA_KV_GROUPS = 2
NSA_HPG = NSA_HEADS // NSA_KV_GROUPS
CMP_BLOCK = 32
CMP_STRIDE = 16
CMP_HIDDEN = 256
SEL_BLOCK = 64
SEL_TOP_N = 16
SEL_Q_BLOCK = 64
WINDOW = 512
WIN_Q_BLOCK = 128
FOX_HEADS = 8
FOX_Q_BLOCK = 128
MEM_LEN = 256
MEM_HEADS = 4
MEM_HEAD_DIM = D_MODEL // MEM_HEADS
NUM_BUCKETS = 32
MAX_DISTANCE = 128
RMS_EPS = 1e-6
NEG_INF = -1e30
FORCE_BONUS = 1e4

NSA_Q = NSA_HEADS * HEAD_DIM
NSA_KV = NSA_KV_GROUPS * HEAD_DIM
FOX_W = FOX_HEADS * HEAD_DIM
IN_SPLITS = (NSA_Q, NSA_KV, NSA_KV, NSA_KV, NSA_KV, NSA_KV, NSA_KV, 3 * NSA_HEADS,
             FOX_W, FOX_W, FOX_W, FOX_HEADS, D_MODEL, D_MODEL)
IN_WIDTH = NSA_Q + 6 * NSA_KV + 3 * NSA_HEADS + 3 * FOX_W + FOX_HEADS + 2 * D_MODEL

kernel_name = 'hybrid_nsa_fox_macaron'


def rmsnorm(x, g):
    x32 = x.astype(jnp.float32)
    y = x32 * lax.rsqrt(jnp.mean(x32 * x32, axis=-1, keepdims=True) + RMS_EPS)
    return (y * g.astype(jnp.float32)).astype(x.dtype)


def swiglu(h, w_gate, w_up, w_down):
    return (jax.nn.silu(h @ w_gate) * (h @ w_up)) @ w_down


def split_cols(z, sizes):
    return jnp.split(z, np.cumsum(sizes)[:-1].tolist(), axis=-1)


def t5_bucket(dist):
    n = jnp.maximum(dist, 0)
    exact = NUM_BUCKETS // 2
    nf = jnp.maximum(n, 1).astype(jnp.float32)
    large = exact + (jnp.log(nf / exact) / math.log(MAX_DISTANCE / exact)
                     * (NUM_BUCKETS - exact)).astype(jnp.int32)
    large = jnp.minimum(large, NUM_BUCKETS - 1)
    return jnp.where(n < exact, n, large)


def masked_softmax(s, mask):
    s = jnp.where(mask, s.astype(jnp.float32), NEG_INF)
    p = jax.nn.softmax(s, axis=-1)
    return jnp.where(mask, p, 0.0)


def cmp_sel_overlap(n_cmp, n_sel):
    c0 = np.arange(n_cmp)[:, None] * CMP_STRIDE
    s0 = np.arange(n_sel)[None, :] * SEL_BLOCK
    ov = np.clip(np.minimum(c0 + CMP_BLOCK, s0 + SEL_BLOCK) - np.maximum(c0, s0), 0, None)
    return jnp.asarray(ov / CMP_STRIDE, dtype=jnp.float32)


def compress(kv, pos, w1, w2):
    b, s, g, d = kv.shape
    n_chunks = s // CMP_STRIDE
    r = CMP_BLOCK // CMP_STRIDE
    nc = n_chunks - r + 1
    chunks = kv.reshape(b, n_chunks, CMP_STRIDE, g, d)
    blocks = jnp.concatenate([chunks[:, i:i + nc] for i in range(r)], axis=2)
    blocks = blocks + pos[None, None, :, None, :]
    flat = blocks.transpose(0, 1, 3, 2, 4).reshape(b, nc, g, CMP_BLOCK * d)
    return jax.nn.gelu(flat @ w1) @ w2


def nsa_attention(q, k_c, v_c, k_s, v_s, k_w, v_w, g_nsa, rel_bias_table,
                  cmp_pos_k, cmp_pos_v, cmp_k_w1, cmp_k_w2, cmp_v_w1, cmp_v_w2):
    b, s = q.shape[:2]
    G, J, D = NSA_KV_GROUPS, NSA_HPG, HEAD_DIM
    qg = q.reshape(b, s, G, J, D) * (D ** -0.5)
    t = jnp.arange(s)
    tbl = rel_bias_table.reshape(NUM_BUCKETS, G, J)

    kc = compress(k_c, cmp_pos_k, cmp_k_w1, cmp_k_w2)
    vc = compress(v_c, cmp_pos_v, cmp_v_w1, cmp_v_w2)
    nc = kc.shape[1]
    c_end = jnp.arange(nc) * CMP_STRIDE + CMP_BLOCK - 1
    dist_c = t[:, None] - c_end[None, :]
    bias_c = tbl[t5_bucket(dist_c)].transpose(2, 3, 0, 1)
    s_c = jnp.einsum('bsgjd,bcgd->bgjsc', qg, kc) + bias_c
    p_c = masked_softmax(s_c, dist_c >= 0)
    o_c = jnp.einsum('bgjsc,bcgd->bsgjd', p_c.astype(vc.dtype), vc)

    n_sel = s // SEL_BLOCK
    top_n = min(SEL_TOP_N, n_sel)
    imp = jnp.einsum('bgjsc,cn->bgsn', p_c, cmp_sel_overlap(nc, n_sel))
    blk = jnp.arange(n_sel)[None, :]
    cur = (t // SEL_BLOCK)[:, None]
    forced = (blk == 0) | (blk == cur) | (blk == cur - 1)
    imp = jnp.where(blk * SEL_BLOCK <= t[:, None],
                    imp + jnp.where(forced, FORCE_BONUS, 0.0), NEG_INF)
    _, sel_idx = lax.top_k(imp, top_n)

    kb = k_s.reshape(b, n_sel, SEL_BLOCK, G, D).transpose(0, 3, 1, 2, 4)
    vb = v_s.reshape(b, n_sel, SEL_BLOCK, G, D).transpose(0, 3, 1, 2, 4)
    nq = s // SEL_Q_BLOCK
    q_blocks = qg.reshape(b, nq, SEL_Q_BLOCK, G, J, D).transpose(1, 0, 3, 4, 2, 5)
    idx_blocks = sel_idx.reshape(b, G, nq, SEL_Q_BLOCK, top_n).transpose(2, 0, 1, 3, 4)
    bi = jnp.arange(b)[:, None, None, None]
    gi = jnp.arange(G)[None, :, None, None]
    gi5 = jnp.arange(G)[None, :, None, None, None]
    r = jnp.arange(SEL_BLOCK)

    def sel_block(args):
        qb, ib, start = args
        kg = kb[bi, gi, ib]
        vg = vb[bi, gi, ib]
        tq = start + jnp.arange(SEL_Q_BLOCK)
        pos = ib[..., None] * SEL_BLOCK + r
        dist = tq[None, None, :, None, None] - pos
        bias = tbl[t5_bucket(dist), gi5].transpose(0, 1, 5, 2, 3, 4)
        sc = jnp.einsum('bgjqd,bgqnrd->bgjqnr', qb, kg) + bias
        flat = top_n * SEL_BLOCK
        p = masked_softmax(sc.reshape(b, G, J, SEL_Q_BLOCK, flat),
                           (dist >= 0).reshape(b, G, 1, SEL_Q_BLOCK, flat))
        p = p.reshape(b, G, J, SEL_Q_BLOCK, top_n, SEL_BLOCK).astype(vg.dtype)
        return jnp.einsum('bgjqnr,bgqnrd->bqgjd', p, vg)

    o_s = lax.map(sel_block, (q_blocks, idx_blocks, jnp.arange(nq) * SEL_Q_BLOCK))
    o_s = o_s.transpose(1, 0, 2, 3, 4, 5).reshape(b, s, G, J, D)

    nw = s // WIN_Q_BLOCK
    span = WINDOW + WIN_Q_BLOCK
    kpad = jnp.pad(k_w, ((0, 0), (WINDOW, 0), (0, 0), (0, 0)))
    vpad = jnp.pad(v_w, ((0, 0), (WINDOW, 0), (0, 0), (0, 0)))
    win_idx = (jnp.arange(nw) * WIN_Q_BLOCK)[:, None] + jnp.arange(span)[None, :]
    kwin = kpad[:, win_idx]
    vwin = vpad[:, win_idx]
    qw = qg.reshape(b, nw, WIN_Q_BLOCK, G, J, D)
    dist_w = jnp.arange(WIN_Q_BLOCK)[:, None] + WINDOW - jnp.arange(span)[None, :]
    bias_w = tbl[t5_bucket(dist_w)].transpose(2, 3, 0, 1)[:, :, None]
    mask_w = ((dist_w >= 0) & (dist_w < WINDOW))[None] & (win_idx >= WINDOW)[:, None, :]
    s_w = jnp.einsum('bnqgjd,bnkgd->bgjnqk', qw, kwin) + bias_w
    p_w = masked_softmax(s_w, mask_w)
    o_w = jnp.einsum('bgjnqk,bnkgd->bnqgjd', p_w.astype(vwin.dtype), vwin).reshape(b, s, G, J, D)

    g = jax.nn.sigmoid(g_nsa).reshape(b, s, 3, G, J)[..., None]
    o = g[:, :, 0] * o_c + g[:, :, 1] * o_s + g[:, :, 2] * o_w
    return o.reshape(b, s, NSA_Q)


def forgetting_attention(q, k, v, f_logit):
    b, s, h, d = q.shape
    c = jnp.cumsum(jax.nn.log_sigmoid(f_logit.astype(jnp.float32)), axis=1).transpose(0, 2, 1)
    nb = s // FOX_Q_BLOCK
    qb = (q * (d ** -0.5)).reshape(b, nb, FOX_Q_BLOCK, h, d).transpose(1, 0, 3, 2, 4)
    cb = c.reshape(b, h, nb, FOX_Q_BLOCK).transpose(2, 0, 1, 3)
    kpos = jnp.arange(s)

    def block(args):
        qi, ci, start = args
        sc = jnp.einsum('bhqd,bkhd->bhqk', qi, k).astype(jnp.float32) + ci[..., None] - c[:, :, None, :]
        mask = (start + jnp.arange(FOX_Q_BLOCK))[:, None] >= kpos[None, :]
        p = masked_softmax(sc, mask)
        return jnp.einsum('bhqk,bkhd->bqhd', p.astype(v.dtype), v)

    o = lax.map(block, (qb, cb, jnp.arange(nb) * FOX_Q_BLOCK))
    return o.transpose(1, 0, 2, 3, 4).reshape(b, s, h * d)


def hybrid_mixer(h, w_in, b_forget, rel_bias_table, cmp_pos_k, cmp_pos_v, cmp_k_w1, cmp_k_w2,
                 cmp_v_w1, cmp_v_w2, w_up_nsa, w_up_fox, w_out):
    b, s, _ = h.shape
    (q_n, k_c, v_c, k_s, v_s, k_w, v_w, g_nsa, q_f, k_f, v_f, f_logit,
     gate_a, gate_b) = split_cols(h @ w_in, IN_SPLITS)
    kvh = lambda z: z.reshape(b, s, NSA_KV_GROUPS, HEAD_DIM)
    o_nsa = nsa_attention(q_n, kvh(k_c), kvh(v_c), kvh(k_s), kvh(v_s), kvh(k_w), kvh(v_w), g_nsa,
                          rel_bias_table, cmp_pos_k, cmp_pos_v, cmp_k_w1, cmp_k_w2, cmp_v_w1, cmp_v_w2)
    fh = lambda z: z.reshape(b, s, FOX_HEADS, HEAD_DIM)
    o_fox = forgetting_attention(fh(q_f), fh(k_f), fh(v_f), f_logit + b_forget)
    y = jax.nn.sigmoid(gate_a) * (o_nsa @ w_up_nsa) + jax.nn.sigmoid(gate_b) * (o_fox @ w_up_fox)
    return y @ w_out


def memory_cross_attention(h, mem_n, w_q, w_kv, w_o):
    b, s, _ = h.shape
    q = (h @ w_q).reshape(b, s, MEM_HEADS, MEM_HEAD_DIM) * (MEM_HEAD_DIM ** -0.5)
    k, v = jnp.split(mem_n @ w_kv, 2, axis=-1)
    k = k.reshape(b, -1, MEM_HEADS, MEM_HEAD_DIM)
    v = v.reshape(b, -1, MEM_HEADS, MEM_HEAD_DIM)
    p = jax.nn.softmax(jnp.einsum('bshd,bmhd->bhsm', q, k).astype(jnp.float32), axis=-1)
    o = jnp.einsum('bhsm,bmhd->bshd', p.astype(v.dtype), v).reshape(b, s, D_MODEL)
    return o @ w_o


def setup_inputs(seed: int = 0) -> dict:
    key = jax.random.key(seed)
    ks = iter(jax.random.split(key, 40))
    nrm = lambda shape, scale: jax.random.normal(next(ks), shape, jnp.float32) * scale
    gain = lambda shape: 1.0 + nrm(shape, 0.02)
    L = DEPTH
    return {
        'x': nrm((BATCH, SEQ, D_MODEL), 1.0),
        'mem': nrm((BATCH, MEM_LEN, D_MODEL), 1.0),
        'rel_bias_table': nrm((NUM_BUCKETS, NSA_HEADS), 0.5),
        'ffn1_norm': gain((L, D_MODEL)),
        'ffn1_w_gate': nrm((L, D_MODEL, D_FF), D_MODEL ** -0.5),
        'ffn1_w_up': nrm((L, D_MODEL, D_FF), D_MODEL ** -0.5),
        'ffn1_w_down': nrm((L, D_FF, D_MODEL), D_FF ** -0.5),
        'mix_norm': gain((L, D_MODEL)),
        'mix_w_in': nrm((L, D_MODEL, IN_WIDTH), D_MODEL ** -0.5),
        'mix_b_forget': 2.0 + nrm((L, FOX_HEADS), 0.5),
        'cmp_pos_k': nrm((L, CMP_BLOCK, HEAD_DIM), 0.1),
        'cmp_pos_v': nrm((L, CMP_BLOCK, HEAD_DIM), 0.1),
        'cmp_k_w1': nrm((L, CMP_BLOCK * HEAD_DIM, CMP_HIDDEN), (CMP_BLOCK * HEAD_DIM) ** -0.5),
        'cmp_k_w2': nrm((L, CMP_HIDDEN, HEAD_DIM), 2.0 * CMP_HIDDEN ** -0.5),
        'cmp_v_w1': nrm((L, CMP_BLOCK * HEAD_DIM, CMP_HIDDEN), (CMP_BLOCK * HEAD_DIM) ** -0.5),
        'cmp_v_w2': nrm((L, CMP_HIDDEN, HEAD_DIM), 2.0 * CMP_HIDDEN ** -0.5),
        'w_up_nsa': nrm((L, NSA_Q, D_MODEL), NSA_Q ** -0.5),
        'w_up_fox': nrm((L, FOX_W, D_MODEL), FOX_W ** -0.5),
        'mix_w_out': nrm((L, D_MODEL, D_MODEL), D_MODEL ** -0.5),
        'mem_q_norm': gain((L, D_MODEL)),
        'mem_kv_norm': gain((L, D_MODEL)),
        'mem_w_q': nrm((L, D_MODEL, D_MODEL), D_MODEL ** -0.5),
        'mem_w_kv': nrm((L, D_MODEL, 2 * D_MODEL), D_MODEL ** -0.5),
        'mem_w_o': nrm((L, D_MODEL, D_MODEL), D_MODEL ** -0.5),
        'ffn2_norm': gain((L, D_MODEL)),
        'ffn2_w_gate': nrm((L, D_MODEL, D_FF), D_MODEL ** -0.5),
        'ffn2_w_up': nrm((L, D_MODEL, D_FF), D_MODEL ** -0.5),
        'ffn2_w_down': nrm((L, D_FF, D_MODEL), D_FF ** -0.5),
        'final_norm': gain((D_MODEL,)),
    }


def reference(x, mem, rel_bias_table, ffn1_norm, ffn1_w_gate, ffn1_w_up, ffn1_w_down,
              mix_norm, mix_w_in, mix_b_forget, cmp_pos_k, cmp_pos_v, cmp_k_w1, cmp_k_w2,
              cmp_v_w1, cmp_v_w2, w_up_nsa, w_up_fox, mix_w_out, mem_q_norm, mem_kv_norm,
              mem_w_q, mem_w_kv, mem_w_o, ffn2_norm, ffn2_w_gate, ffn2_w_up, ffn2_w_down,
              final_norm):
    for l in range(DEPTH):
        x = x + 0.5 * swiglu(rmsnorm(x, ffn1_norm[l]), ffn1_w_gate[l], ffn1_w_up[l], ffn1_w_down[l])
        x = x + hybrid_mixer(rmsnorm(x, mix_norm[l]), mix_w_in[l], mix_b_forget[l], rel_bias_table,
                             cmp_pos_k[l], cmp_pos_v[l], cmp_k_w1[l], cmp_k_w2[l], cmp_v_w1[l],
                             cmp_v_w2[l], w_up_nsa[l], w_up_fox[l], mix_w_out[l])
        x = x + memory_cross_attention(rmsnorm(x, mem_q_norm[l]), rmsnorm(mem, mem_kv_norm[l]),
                                       mem_w_q[l], mem_w_kv[l], mem_w_o[l])
        x = x + 0.5 * swiglu(rmsnorm(x, ffn2_norm[l]), ffn2_w_gate[l], ffn2_w_up[l], ffn2_w_down[l])
    return rmsnorm(x, final_norm)
```

```python
import math
from contextlib import ExitStack
import numpy as np
import ml_dtypes
import concourse.bass as bass
import concourse.mybir as mybir
from concourse.bass_utils import run_bass_kernel_spmd

F32 = mybir.dt.float32
BF16 = mybir.dt.bfloat16
AF = mybir.ActivationFunctionType
ALU = mybir.AluOpType
AX = mybir.AxisListType

S = 2048
D = 1024
KC = 8
DFF = 2816
NF = 22
FGROUPS = [(0, 6), (6, 6), (12, 5), (17, 5)]
INW = 4896
MEM = 256
NEG = -1.0e30
EPS = 1e-6


class Tok:
    __slots__ = ("sem", "val")

    def __init__(self, sem, val):
        self.sem = sem
        self.val = val


class Buf:
    __slots__ = ("lw", "rd", "excl")

    def __init__(self):
        self.lw = None
        self.rd = {}
        self.excl = False


class Eng:
    def __init__(self, h, sem, is_pe=False):
        self.h = h
        self.sem = sem
        self.count = 0
        self.seen = {}
        self.is_pe = is_pe
        self.dsems = []
        self.dcnt = []
        self.dnext = 0


class Tile:
    def __init__(self, t, nb=1, init=None):
        self.t = t
        self.b = [Buf() for _ in range(nb)]
        if init:
            for b in self.b:
                b.rd = dict(init)


class Phase(ExitStack):
    def __init__(self, kb):
        super().__init__()
        self.kb = kb
        self.tiles = []

    def __exit__(self, *a):
        ft = self.kb.free_toks
        for t in self.tiles:
            for b in t.b:
                for tok in [b.lw] + list(b.rd.values()):
                    if tok is None:
                        continue
                    k = id(tok.sem)
                    if k not in ft or ft[k].val < tok.val:
                        ft[k] = tok
        return super().__exit__(*a)


class KB:
    def __init__(self, nc, es):
        self.nc = nc
        self.es = es
        sem = lambda n: es.enter_context(nc.semaphore(n))
        self.pe = Eng(nc.tensor, sem("s_pe"), True)
        self.act = Eng(nc.scalar, sem("s_act"))
        self.dve = Eng(nc.vector, sem("s_dve"))
        self.pool = Eng(nc.gpsimd, sem("s_pool"))
        self.sp = Eng(nc.sync, sem("s_sp"))
        for q, nm, n in ((self.sp, "dsp", 16), (self.pool, "dpl", 24)):
            q.dsems = [sem(f"{nm}{i}") for i in range(n)]
            q.dcnt = [0] * n
        self.banks = [Tile(es.enter_context(nc.psum_tensor(f"pb{i}", [128, 512], F32))) for i in range(8)]
        for t in self.banks:
            t.b[0].excl = True
        self.nalloc = 0
        self.free_toks = {}

    def sb(self, es, shape, dtype, nb=1, name=None):
        self.nalloc += 1
        t = es.enter_context(self.nc.sbuf_tensor(f"{name or 't'}_{self.nalloc}", list(shape), dtype))
        tl = Tile(t, nb, self.free_toks)
        if hasattr(es, "tiles"):
            es.tiles.append(tl)
        return tl

    def wait(self, eng, tok, raw):
        if tok is None:
            return
        if tok.sem is eng.sem and eng.is_pe:
            return
        k = id(tok.sem)
        if eng.seen.get(k, 0) >= tok.val:
            return
        eng.h.wait_ge(tok.sem, tok.val)
        eng.seen[k] = tok.val

    def _deps(self, eng, rd, wr):
        for b in rd:
            self.wait(eng, b.lw, True)
            if b.excl:
                for t in b.rd.values():
                    if t.sem is not eng.sem:
                        self.wait(eng, t, False)
        for b in wr:
            self.wait(eng, b.lw, False)
            for t in b.rd.values():
                self.wait(eng, t, False)

    def _commit(self, tok, rd, wr):
        k = id(tok.sem)
        for b in rd:
            b.rd[k] = tok
        for b in wr:
            b.lw = tok
            b.rd = {}

    def op(self, eng, fn, rd=(), wr=()):
        self._deps(eng, rd, wr)
        inst = fn()
        eng.count += 1
        inst.then_inc(eng.sem, 1)
        tok = Tok(eng.sem, eng.count)
        self._commit(tok, rd, wr)
        return tok

    def dma(self, q, out, in_, rd=(), wr=()):
        self._deps(q, rd, wr)
        i = q.dnext
        q.dnext = (i + 1) % len(q.dsems)
        s = q.dsems[i]
        if q.dcnt[i] > 0:
            self.wait(q, Tok(s, 16 * q.dcnt[i]), True)
        inst = q.h.dma_start(out=out, in_=in_)
        inst.then_inc(s, 16)
        q.dcnt[i] += 1
        tok = Tok(s, 16 * q.dcnt[i])
        self._commit(tok, rd, wr)
        return tok

    def barrier(self, bufs):
        pass

    def mm(self, out, lhsT, rhs, start, stop, rd, wr):
        return self.op(self.pe, lambda: self.nc.tensor.matmul(out, lhsT=lhsT, rhs=rhs, start=start, stop=stop), rd, wr)

    def tr(self, out, in_, ident, rd, wr):
        return self.op(self.pe, lambda: self.nc.tensor.transpose(out, in_, ident), rd, wr)

    def actv(self, out, in_, func, rd, wr, bias=0.0, scale=1.0, accum_out=None):
        if accum_out is not None:
            return self.op(self.act, lambda: self.nc.scalar.activation(out=out, in_=in_, func=func, bias=bias, scale=scale, accum_out=accum_out), rd, wr)
        return self.op(self.act, lambda: self.nc.scalar.activation(out=out, in_=in_, func=func, bias=bias, scale=scale), rd, wr)

    def tt(self, out, in0, in1, op, rd, wr, eng=None):
        eng = eng or self.dve
        return self.op(eng, lambda: eng.h.tensor_tensor(out=out, in0=in0, in1=in1, op=op), rd, wr)

    def ts(self, out, in0, s1, s2, op0, op1, rd, wr, eng=None):
        eng = eng or self.dve
        if s2 is None:
            return self.op(eng, lambda: eng.h.tensor_scalar(out=out, in0=in0, scalar1=s1, scalar2=None, op0=op0), rd, wr)
        return self.op(eng, lambda: eng.h.tensor_scalar(out=out, in0=in0, scalar1=s1, scalar2=s2, op0=op0, op1=op1), rd, wr)

    def stt(self, out, in0, scalar, in1, op0, op1, rd, wr, eng=None):
        eng = eng or self.dve
        return self.op(eng, lambda: eng.h.scalar_tensor_tensor(out=out, in0=in0, scalar=scalar, in1=in1, op0=op0, op1=op1), rd, wr)

    def cp(self, out, in_, rd, wr, eng=None):
        eng = eng or self.dve
        if eng is self.act:
            return self.actv(out, in_, AF.Copy, rd, wr)
        return self.op(eng, lambda: eng.h.tensor_copy(out=out, in_=in_), rd, wr)

    def memset(self, ap, val, wr, eng=None):
        eng = eng or self.dve
        return self.op(eng, lambda: eng.h.memset(ap, val), (), wr)


class Ring:
    def __init__(self, kb, es, shape, dtype, n, name, nb=1):
        self.tiles = [kb.sb(es, shape, dtype, nb, f"{name}{i}") for i in range(n)]
        self.i = 0

    def next(self):
        t = self.tiles[self.i]
        self.i = (self.i + 1) % len(self.tiles)
        return t


def build_program(stage=9, dbg=False):
    nc = bass.Bass("TRN2", target_bir_lowering=False)
    dram = lambda n, shp, dt=F32, kind="ExternalInput": nc.dram_tensor(n, list(shp), dt, kind=kind).ap()
    x_d = dram("x", [S, D])
    mem_d = dram("mem", [MEM, D])
    gains_d = dram("gains", [128, 5, KC])
    gbc_d = dram("gbc", [128, 2, D])
    ident_d = dram("ident", [128, 128])
    w_d = {}
    for nm in ("ffn1", "ffn2"):
        w_d[nm + "_g"] = dram(nm + "_w_gate", [D, DFF])
        w_d[nm + "_u"] = dram(nm + "_w_up", [D, DFF])
        w_d[nm + "_d"] = dram(nm + "_w_down", [DFF, D])
    win_d = dram("w_in", [D, INW])
    bdiag_d = dram("bdiag", [8, 128, 256])
    bwf_d = dram("bwf", [8, 128, 128])
    tbl31_d = dram("tbl31", [128, 8])
    bcmp_d = dram("bcmp", [8, 4, 128, 512])
    ovaug_d = dram("ovaug", [128, 33])
    impm_d = dram("impm", [128, 2, 16, 32])
    esel_d = dram("esel", [32, S])
    tri_d = dram("tri", [128, 128])
    bfg_d = dram("bforget", [8, 1])
    posT_d = dram("posT", [2, 64, 32])
    w1_d = [dram("cmp_k_w1", [2048, 256]), dram("cmp_v_w1", [2048, 256])]
    w2_d = [dram("cmp_k_w2", [256, 64]), dram("cmp_v_w2", [256, 64])]
    wupa_d = dram("w_up_nsa", [512, D])
    wupb_d = dram("w_up_fox", [512, D])
    wout_d = dram("mix_w_out", [D, D])
    wq_d = dram("mem_w_q", [D, D])
    wkv_d = dram("mem_w_kv", [D, 2 * D])
    wo_d = dram("mem_w_o", [D, D])
    out_d = dram("out", [S, D], kind="ExternalOutput")

    es = ExitStack()
    dump_toks = []

    def dump(name, tile, ap, dt):
        if not dbg:
            return
        d_ = nc.dram_tensor("dbg_" + name, list(ap.shape), dt, kind="ExternalOutput").ap()
        dump_toks.append(kb.dma(kb.sp, d_, ap, tile.b, ()))

    with es:
        kb = KB(nc, es)
        PE, ACT, DVE, POOL, SP = kb.pe, kb.act, kb.dve, kb.pool, kb.sp
        bank = kb.banks

        ident = kb.sb(es, [128, 128], F32, name="ident")
        kb.dma(SP, ident.t[:], ident_d, (), ident.b)
        gains = kb.sb(es, [128, 5, KC], F32, name="gains")
        kb.dma(SP, gains.t[:], gains_d, (), gains.b)
        ones_bf = kb.sb(es, [128, 128], BF16, name="ones")
        kb.memset(ones_bf.t[:], 1.0, ones_bf.b)

        xT = kb.sb(es, [128, KC, S], F32, nb=KC * 4, name="xT")
        xb = lambda kc, tt: xT.b[kc * 4 + tt]

        with Phase(kb) as ph:
            xst = Ring(kb, ph, [128, D], F32, 4, "xst")
            k = 0
            for tb in range(16):
                xs = xst.next()
                kb.dma(SP, xs.t[:], x_d[tb * 128:(tb + 1) * 128, :], (), xs.b)
                for half in range(2):
                    pb = bank[6 + (k % 2)]
                    k += 1
                    for j in range(4):
                        kc = half * 4 + j
                        kb.tr(pb.t[:, j * 128:(j + 1) * 128], xs.t[:, kc * 128:(kc + 1) * 128], ident.t[:], xs.b + ident.b, pb.b)
                    dst = xT.t[:, half * 4:half * 4 + 4, tb * 128:(tb + 1) * 128]
                    src = pb.t[:].rearrange("p (j t) -> p j t", j=4)
                    wr = [xb(half * 4 + j, tb // 4) for j in range(4)]
                    kb.cp(dst, src, pb.b, wr, eng=(DVE if half == 0 else ACT))

        def rmsnorm_fm(ph, gi, hT, rstd=None):
            sqr = Ring(kb, ph, [128, KC, 512], BF16, 2, "sq", nb=2)
            rsr = None if rstd is not None else Ring(kb, ph, [128, 512], F32, 2, "rstd")
            for tt in range(4):
                cs = slice(tt * 512, (tt + 1) * 512)
                sq = sqr.next()
                kb.actv(sq.t[:, 0:5, :], xT.t[:, 0:5, cs], AF.Square, [xb(kc, tt) for kc in range(5)], [sq.b[0]])
                kb.tt(sq.t[:, 5:8, :], xT.t[:, 5:8, cs], xT.t[:, 5:8, cs], ALU.mult, [xb(kc, tt) for kc in range(5, 8)], [sq.b[1]], eng=POOL)
                pb = bank[6 + (tt % 2)]
                for kc in range(KC):
                    kb.mm(pb.t[:], ones_bf.t[:], sq.t[:, kc, :], kc == 0, kc == KC - 1, [sq.b[0 if kc < 5 else 1]] + ones_bf.b, pb.b)
                if rstd is not None:
                    rs_ap, rs_b = rstd.t[:, cs], [rstd.b[tt]]
                else:
                    rs = rsr.next()
                    rs_ap, rs_b = rs.t[:], rs.b
                kb.actv(rs_ap, pb.t[:], AF.Ln, pb.b, rs_b, bias=EPS, scale=1.0 / D)
                kb.actv(rs_ap, rs_ap, AF.Exp, rs_b, rs_b, scale=-0.5)
                rmsnorm_apply(gi, hT, tt, rs_ap, rs_b)

        def rmsnorm_apply(gi, hT, tt, rs_ap, rs_b):
            cs = slice(tt * 512, (tt + 1) * 512)
            for kc in range(KC):
                kb.stt(hT.t[:, kc, cs], xT.t[:, kc, cs], gains.t[:, gi, kc:kc + 1], rs_ap, ALU.mult, ALU.mult,
                       [xb(kc, tt)] + rs_b + gains.b, [hT.b[kc * 4 + tt]])

        def ffn_prefetch(ph, wg_d, wu_d, wd_d):
            pre = {}
            pre["wgr"] = Ring(kb, ph, [128, KC, 128], BF16, 3, "wg")
            pre["wur"] = Ring(kb, ph, [128, KC, 128], BF16, 3, "wu")
            pre["wdr"] = Ring(kb, ph, [128, 6, D], BF16, 2, "wd")
            wgv = wg_d.rearrange("(k p) f -> p k f", p=128)
            wuv = wu_d.rearrange("(k p) f -> p k f", p=128)
            f0, nf = FGROUPS[0]
            wd = pre["wdr"].next()
            kb.dma(POOL, wd.t[:, 0:nf, :], wd_d[f0 * 128:(f0 + nf) * 128, :].rearrange("(f p) d -> p f d", p=128), (), wd.b)
            pre["wd0"] = wd
            for f in range(3):
                wg = pre["wgr"].next()
                wu = pre["wur"].next()
                kb.dma(POOL, wg.t[:], wgv[:, :, f * 128:(f + 1) * 128], (), wg.b)
                kb.dma(POOL, wu.t[:], wuv[:, :, f * 128:(f + 1) * 128], (), wu.b)
                pre[f] = (wg, wu)
            return pre

        def ffn(ph, hT, wg_d, wu_d, wd_d, pre):
            wgr, wur, wdr = pre["wgr"], pre["wur"], pre["wdr"]
            aT = kb.sb(ph, [128, 6, S], BF16, nb=6 * 4, name="aT")
            sgr = Ring(kb, ph, [128, 512], F32, 2, "sg")
            wgv = wg_d.rearrange("(k p) f -> p k f", p=128)
            wuv = wu_d.rearrange("(k p) f -> p k f", p=128)
            nb = 0
            for gi_, (f0, nf) in enumerate(FGROUPS):
                if gi_ == 0:
                    wd = pre["wd0"]
                else:
                    wd = wdr.next()
                    kb.dma(POOL, wd.t[:, 0:nf, :], wd_d[f0 * 128:(f0 + nf) * 128, :].rearrange("(f p) d -> p f d", p=128), (), wd.b)
                for fi in range(nf):
                    f = f0 + fi
                    if f in pre:
                        wg, wu = pre[f]
                    else:
                        wg = wgr.next()
                        wu = wur.next()
                        kb.dma(POOL, wg.t[:], wgv[:, :, f * 128:(f + 1) * 128], (), wg.b)
                        kb.dma(POOL, wu.t[:], wuv[:, :, f * 128:(f + 1) * 128], (), wu.b)
                    for tt in range(4):
                        cs = slice(tt * 512, (tt + 1) * 512)
                        pg = bank[(nb % 2) * 2]
                        pu = bank[(nb % 2) * 2 + 1]
                        nb += 1
                        for kc in range(KC):
                            kb.mm(pg.t[:], wg.t[:, kc, :], hT.t[:, kc, cs], kc == 0, kc == KC - 1, wg.b + [hT.b[kc * 4 + tt]], pg.b)
                        for kc in range(KC):
                            kb.mm(pu.t[:], wu.t[:, kc, :], hT.t[:, kc, cs], kc == 0, kc == KC - 1, wu.b + [hT.b[kc * 4 + tt]], pu.b)
                        sg = sgr.next()
                        kb.actv(sg.t[:], pg.t[:], AF.Silu, pg.b, sg.b)
                        kb.tt(aT.t[:, fi, cs], sg.t[:], pu.t[:], ALU.mult, sg.b + pu.b, [aT.b[fi * 4 + tt]])
                k = 0
                for dj in range(KC):
                    for tt in range(4):
                        cs = slice(tt * 512, (tt + 1) * 512)
                        pb = bank[4 + (k % 2)]
                        k += 1
                        for fi in range(nf):
                            kb.mm(pb.t[:], wd.t[:, fi, dj * 128:(dj + 1) * 128], aT.t[:, fi, cs], fi == 0, fi == nf - 1,
                                  wd.b + [aT.b[fi * 4 + tt]], pb.b)
                        kb.stt(xT.t[:, dj, cs], pb.t[:], 0.5, xT.t[:, dj, cs], ALU.mult, ALU.add,
                               pb.b + [xb(dj, tt)], [xb(dj, tt)])

        if stage >= 1:
            with Phase(kb) as ph:
                hT = kb.sb(ph, [128, KC, S], BF16, nb=KC * 4, name="hT")
                pre = ffn_prefetch(ph, w_d["ffn1_g"], w_d["ffn1_u"], w_d["ffn1_d"])
                rmsnorm_fm(ph, 0, hT)
                dump("hT", hT, hT.t[:], BF16)
                ffn(ph, hT, w_d["ffn1_g"], w_d["ffn1_u"], w_d["ffn1_d"], pre)
                dump("xT1", xT, xT.t[:], F32)


        winv = win_d.rearrange("(k p) f -> p k f", p=128)
        evk = [0]

        def evac(out, in_, rd, wr, scale=None):
            evk[0] += 1
            if evk[0] % 2 == 0:
                return kb.actv(out, in_, AF.Copy, rd, wr, scale=(1.0 if scale is None else scale))
            if scale is None:
                return kb.cp(out, in_, rd, wr)
            return kb.ts(out, in_, scale, None, ALU.mult, None, rd, wr)

        def wload(ring, c0, n, dup=False):
            w = ring.next()
            kb.dma(POOL, w.t[:, :, 0:n], winv[:, :, c0:c0 + n], (), w.b)
            if dup:
                kb.dma(POOL, w.t[:, :, n:2 * n], winv[:, :, c0:c0 + n], (), w.b)
            return w

        pjk = [0]

        def proj_fm(w, M, hT, fn):
            for tt in range(4):
                cs = slice(tt * 512, (tt + 1) * 512)
                pb = bank[6 + (pjk[0] % 2)]
                pjk[0] += 1
                for kc in range(KC):
                    kb.mm(pb.t[0:M, :], w.t[:, kc, 0:M], hT.t[:, kc, cs], kc == 0, kc == KC - 1, w.b + [hT.b[kc * 4 + tt]], pb.b)
                fn(tt, cs, pb)

        def proj_tm(w, N, hT, fn):
            for tb in range(16):
                pb = bank[6 + (pjk[0] % 2)]
                pjk[0] += 1
                for kc in range(KC):
                    kb.mm(pb.t[:, 0:N], hT.t[:, kc, tb * 128:(tb + 1) * 128], w.t[:, kc, 0:N], kc == 0, kc == KC - 1,
                          w.b + [hT.b[kc * 4 + tb // 4]], pb.b)
                fn(tb, pb)

        accb = [[Buf(), Buf()] for _ in range(4)]
        sk = [0]
        apar = [0]

        class Pipe:
            def __init__(self):
                self.q = []
                self.step = 0
                self.seq = 0

            def add(self, delay, fn, prio=1):
                self.q.append((self.step + delay, prio, self.seq, fn))
                self.seq += 1

            def tick(self):
                self.step += 1
                due = sorted([x for x in self.q if x[0] <= self.step])
                self.q = [x for x in self.q if x[0] > self.step]
                for x in due:
                    x[3]()

            def flush(self):
                while self.q:
                    self.tick()

        pipe = Pipe()
        NS = 2
        acck = [0]

        def attend(pr, chunks_fn, qk_fn, seg_fn, v_fn, fin_fn, krows, nv, tmpr):
            packed = 4 * nv <= 512
            ns = 4 if packed else 2
            for qt in range(4):
                chs = chunks_fn(qt)
                if not chs:
                    continue
                abanks = []
                if packed:
                    ab_ = bank[4 + acck[0] % 3]
                    acck[0] += 1
                    abanks = [ab_] * 4
                else:
                    for qb in range(4):
                        abanks.append(bank[2 + acck[0] % 5])
                        acck[0] += 1
                contrib = {qb: [i for i, (kc, lo, hi) in enumerate(chs) if lo <= qb * 128 and (qb + 1) * 128 <= hi] for qb in range(4)}
                lastc = max(max(v_) for v_ in contrib.values() if v_)
                started = [False]
                for i, (kc, lo, hi) in enumerate(chs):
                    pipe.tick()
                    sbk = bank[sk[0] % ns]
                    sk[0] += 1
                    qk_fn(sbk, kc, qt, lo, hi)
                    p = pr.next()

                    def stage_b(sbk=sbk, p=p, kc=kc, qt=qt, lo=lo, hi=hi):
                        for seg in seg_fn(kc, qt, lo, hi):
                            kind, c0, c1 = seg[0], seg[1], seg[2]
                            pbs = p.b[c0 // 128:(c1 + 127) // 128]
                            if kind == "const":
                                kb.actv(p.t[0:krows, c0:c1], sbk.t[0:krows, c0:c1], AF.Exp, sbk.b + seg[4], pbs, bias=seg[3])
                            elif kind == "tile":
                                tm = tmpr.next()
                                kb.tt(tm.t[0:krows, 0:c1 - c0], sbk.t[0:krows, c0:c1], seg[3], ALU.add, sbk.b + seg[4], tm.b)
                                kb.actv(p.t[0:krows, c0:c1], tm.t[0:krows, 0:c1 - c0], AF.Exp, tm.b, pbs)
                            elif kind == "mul":
                                kb.tt(p.t[0:krows, c0:c1], p.t[0:krows, c0:c1], seg[3], ALU.mult, pbs + seg[4], pbs)

                    def stage_c(i=i, kc=kc, p=p, qt=qt, contrib=contrib, abanks=abanks, started=started, lastc=lastc):
                        for qb in range(4):
                            if i in contrib[qb]:
                                first = contrib[qb][0] == i
                                last = contrib[qb][-1] == i
                                ab = abanks[qb]
                                v_ap, v_b = v_fn(kc)
                                if packed:
                                    reg = ab.t[:, qb * nv:(qb + 1) * nv]
                                    st = not started[0]
                                    started[0] = True
                                    kb.op(PE, lambda: nc.tensor.matmul(reg, lhsT=p.t[0:krows, qb * 128:(qb + 1) * 128], rhs=v_ap, start=st,
                                                                       stop=(i == lastc and last), skip_group_check=True),
                                          [p.b[qb]] + v_b, ab.b)
                                else:
                                    reg = ab.t[:, 0:nv]
                                    kb.mm(reg, p.t[0:krows, qb * 128:(qb + 1) * 128], v_ap, first, last, [p.b[qb]] + v_b, ab.b)
                                    if last:
                                        pipe.add(1, lambda qt=qt, qb=qb, reg=reg, ab=ab: fin_fn(qt, qb, reg, ab.b), prio=0)
                        if packed and i == lastc:
                            ab = abanks[0]
                            pipe.add(1, lambda qt=qt, ab=ab: fin_fn(qt, None, ab.t[:, 0:4 * nv].rearrange("p (q v) -> p q v", q=4), ab.b), prio=0)

                    pipe.add(1, stage_b, prio=1)
                    pipe.add(3, stage_c, prio=2)

        def transpose_o(oacc, dst, j):
            for t4 in range(4):
                pb = bank[6 + (pjk[0] % 2)]
                pjk[0] += 1
                for k_ in range(4):
                    tb = t4 * 4 + k_
                    kb.tr(pb.t[:, k_ * 128:(k_ + 1) * 128], oacc.t[:, tb, :, :].rearrange("p a b -> p (a b)"), ident.t[:],
                          [oacc.b[tb]] + ident.b, pb.b)
                evac(dst.t[:, j, t4 * 512:(t4 + 1) * 512], pb.t[:], pb.b, [dst.b[j]])

        def mixer():
            with Phase(kb) as pm:
                qno = kb.sb(pm, [128, 4, S], BF16, nb=4, name="qno")
                qfo = kb.sb(pm, [128, 4, S], BF16, nb=4, name="qfo")
                rstd_mix = kb.sb(pm, [128, S], F32, nb=4, name="rstdmix")
                tbl31 = kb.sb(pm, [128, 8], F32, name="tbl31")
                kb.dma(SP, tbl31.t[:], tbl31_d, (), tbl31.b)
                tri = kb.sb(pm, [128, 128], BF16, name="tri")
                kb.dma(POOL, tri.t[:], tri_d, (), tri.b)

                if stage >= 2:
                  with Phase(kb) as pn:
                    ksa = [kb.sb(pn, [128, S], BF16, name="ksa") for _ in range(2)]
                    kwT = [kb.sb(pn, [128, S], BF16, name="kwT") for _ in range(2)]
                    vs_tm = kb.sb(pn, [128, 16, 2, 65], BF16, nb=16, name="vs")
                    vw_tm = kb.sb(pn, [128, 16, 2, 65], BF16, nb=16, name="vw")
                    kcmpT = kb.sb(pn, [128, 2, 128], BF16, nb=2, name="kcmp")
                    vc_tm = kb.sb(pn, [128, 2, 65], BF16, nb=2, name="vc")
                    gs = kb.sb(pn, [128, 16, 24], F32, nb=16, name="gs")
                    ovaug = kb.sb(pn, [128, 33], BF16, name="ovaug")
                    kb.dma(POOL, ovaug.t[:], ovaug_d, (), ovaug.b)
                    kb.memset(vs_tm.t[:, :, :, 64:65], 1.0, vs_tm.b)
                    kb.memset(vw_tm.t[:, :, :, 64:65], 1.0, vw_tm.b)
                    kb.memset(vc_tm.t[:, :, 64:65], 1.0, vc_tm.b)
                    for g in range(2):
                        kb.memset(ksa[g].t[64:128, :], 0.0, ksa[g].b)
                        kb.memset(kwT[g].t[64:128, :], 0.0, kwT[g].b)
                        kb.dma(POOL, ksa[g].t[64:96, :], esel_d, (), ksa[g].b)
                    with Phase(kb) as pa:
                        h2T = kb.sb(pa, [128, KC, S], BF16, nb=KC * 4, name="h2T")
                        with Phase(kb) as pr_:
                            rmsnorm_fm(pr_, 1, h2T, rstd=rstd_mix)
                        wr = Ring(kb, pa, [128, KC, 128], BF16, 3, "wi")
                        kcT = kb.sb(pa, [128, 16, 128], BF16, name="kcT")
                        vcT = kb.sb(pa, [128, 16, 128], BF16, name="vcT")
                        w = wload(wr, 512, 128)
                        proj_fm(w, 128, h2T, lambda tt, cs, pb: evac(kcT.t[:, :, tt * 32:(tt + 1) * 32], pb.t[:].rearrange("p (c r) -> p r c", r=16), pb.b, kcT.b))
                        w = wload(wr, 640, 128)
                        proj_fm(w, 128, h2T, lambda tt, cs, pb: evac(vcT.t[:, :, tt * 32:(tt + 1) * 32], pb.t[:].rearrange("p (c r) -> p r c", r=16), pb.b, vcT.b))
                        with Phase(kb) as pc:
                            w1 = kb.sb(pc, [128, 32, 256], BF16, name="w1")
                            w2 = kb.sb(pc, [128, 2, 128], BF16, name="w2")
                            posT = kb.sb(pc, [128, 32], BF16, name="posT")
                            hid = kb.sb(pc, [128, 2, 128], BF16, nb=2, name="hid")
                            bsb = kb.sb(pc, [128, 1], F32, name="bsb")
                            zr = Ring(kb, pc, [128, 4, 128], F32, 2, "z")
                            for kv in range(2):
                                src = kcT if kv == 0 else vcT
                                w1v = w1_d[kv].rearrange("(i d) h -> d i h", d=64)
                                kb.dma(POOL, w1.t[0:64], w1v, (), w1.b)
                                kb.dma(POOL, w1.t[64:128], w1v, (), w1.b)
                                w2v = w2_d[kv].rearrange("(c p) d -> p c d", p=128)
                                kb.dma(POOL, w2.t[:, :, 0:64], w2v, (), w2.b)
                                kb.dma(POOL, w2.t[:, :, 64:128], w2v, (), w2.b)
                                kb.dma(POOL, posT.t[0:64, :], posT_d[kv], (), posT.b)
                                kb.dma(POOL, posT.t[64:128, :], posT_d[kv], (), posT.b)
                                for g in range(2):
                                    r0 = g * 64
                                    for hc in range(2):
                                        hs = slice(hc * 128, (hc + 1) * 128)
                                        pbb = bank[7]
                                        for i in range(32):
                                            kb.mm(pbb.t[:, 0:1], w1.t[r0:r0 + 64, i, hs], posT.t[r0:r0 + 64, i:i + 1], i == 0, i == 31,
                                                  w1.b + posT.b, pbb.b)
                                        kb.cp(bsb.t[:], pbb.t[:, 0:1], pbb.b, bsb.b)
                                        pbh = bank[6]
                                        for i in range(32):
                                            kb.mm(pbh.t[:, 0:127], w1.t[r0:r0 + 64, i, hs], src.t[r0:r0 + 64, i % 16, i // 16:i // 16 + 127], i == 0, i == 31,
                                                  w1.b + src.b, pbh.b)
                                        z = zr.next()
                                        kb.actv(z.t[:, 0, 0:127], pbh.t[:, 0:127], AF.Identity, pbh.b + bsb.b, z.b, bias=bsb.t[:, 0:1])
                                        kb.tt(z.t[:, 1, 0:127], z.t[:, 0, 0:127], z.t[:, 0, 0:127], ALU.mult, z.b, z.b)
                                        kb.ts(z.t[:, 1, 0:127], z.t[:, 1, 0:127], 0.044715, 1.0, ALU.mult, ALU.add, z.b, z.b)
                                        kb.tt(z.t[:, 2, 0:127], z.t[:, 1, 0:127], z.t[:, 0, 0:127], ALU.mult, z.b, z.b)
                                        kb.actv(z.t[:, 3, 0:127], z.t[:, 2, 0:127], AF.Sigmoid, z.b, z.b, scale=1.5957691216057308)
                                        kb.tt(hid.t[:, hc, 0:127], z.t[:, 3, 0:127], z.t[:, 0, 0:127], ALU.mult, z.b, [hid.b[hc]])
                                    pbo = bank[7]
                                    if kv == 0:
                                        for hc in range(2):
                                            kb.mm(pbo.t[:, 0:127], w2.t[:, hc, :], hid.t[:, hc, 0:127], hc == 0, hc == 1, w2.b + [hid.b[hc]], pbo.b)
                                        evac(kcmpT.t[:, g, 0:127], pbo.t[:, 0:127], pbo.b, [kcmpT.b[g]])
                                    else:
                                        for hc in range(2):
                                            kb.mm(pbo.t[0:127, 0:64], hid.t[:, hc, 0:127], w2.t[:, hc, 0:64], hc == 0, hc == 1, w2.b + [hid.b[hc]], pbo.b)
                                        evac(vc_tm.t[0:127, g, 0:64], pbo.t[0:127, 0:64], pbo.b, [vc_tm.b[g]])
                        for j in range(4):
                            w = wload(wr, 128 * j, 128)
                            proj_fm(w, 128, h2T, lambda tt, cs, pb, j=j: evac(qno.t[:, j, cs], pb.t[:], pb.b, [qno.b[j]], scale=0.125))
                        for g in range(2):
                            w = wload(wr, 768 + 64 * g, 64)
                            proj_fm(w, 64, h2T, lambda tt, cs, pb, g=g: evac(ksa[g].t[0:64, cs], pb.t[0:64, :], pb.b, ksa[g].b))
                            w = wload(wr, 1024 + 64 * g, 64)
                            proj_fm(w, 64, h2T, lambda tt, cs, pb, g=g: evac(kwT[g].t[0:64, cs], pb.t[0:64, :], pb.b, kwT[g].b))
                        w = wload(wr, 896, 128)
                        proj_tm(w, 128, h2T, lambda tb, pb: evac(vs_tm.t[:, tb, :, 0:64], pb.t[:, 0:128].rearrange("p (g d) -> p g d", g=2), pb.b, [vs_tm.b[tb]]))
                        w = wload(wr, 1152, 128)
                        proj_tm(w, 128, h2T, lambda tb, pb: evac(vw_tm.t[:, tb, :, 0:64], pb.t[:, 0:128].rearrange("p (g d) -> p g d", g=2), pb.b, [vw_tm.b[tb]]))
                        w = wload(wr, 1280, 24)
                        proj_tm(w, 24, h2T, lambda tb, pb: kb.actv(gs.t[:, tb, :], pb.t[:, 0:24], AF.Sigmoid, pb.b, [gs.b[tb]]))
                        if dbg:
                            dump("qno", qno, qno.t[:], BF16)
                            dump("kcmpT", kcmpT, kcmpT.t[:, :, 0:127], BF16)
                            dump("vc_tm", vc_tm, vc_tm.t[0:127], BF16)
                            dump("gs", gs, gs.t[:], F32)
                            dump("ksa0", ksa[0], ksa[0].t[:], BF16)
                            dump("vs_tm", vs_tm, vs_tm.t[:], BF16)
                    with Phase(kb) as pb_:
                      if stage >= 2.2:
                        qsa_g = [Ring(kb, pb_, [128, S], BF16, 2, "qsa"), Ring(kb, pb_, [128, S], BF16, 2, "qsb")]
                        for r_ in qsa_g:
                            for t_ in r_.tiles:
                                kb.memset(t_.t[64:128, :], 0.0, t_.b)
                        oacc = kb.sb(pb_, [128, 16, 2, 64], F32, nb=16, name="oacc")
                        impacc_g = [kb.sb(pb_, [128, 16, 32], F32, nb=16, name="impacc") for _ in range(2)]
                        impm = kb.sb(pb_, [128, 2, 16, 32], F32, name="impm")
                        kb.dma(SP, impm.t[:], impm_d, (), impm.b)
                        pr = Ring(kb, pb_, [128, 512], BF16, 8, "P", nb=4)
                        tmpr = Ring(kb, pb_, [128, 512], F32, 3, "tmp")
                        bdr = Ring(kb, pb_, [128, 256], F32, 2, "bd")
                        bwr = Ring(kb, pb_, [128, 128], F32, 2, "bw")
                        bcr = Ring(kb, pb_, [128, 512], F32, 2, "bc")
                        smr = Ring(kb, pb_, [128, 8], F32, 4, "sm")
                        o65 = Ring(kb, pb_, [128, 64], F32, 2, "o65")
                        m8r = Ring(kb, pb_, [128, 16], F32, 2, "m8")
                        ost = kb.sb(pb_, [128, 16, 3, 65], F32, nb=16, name="ost")
                        impst = kb.sb(pb_, [128, 16, 33], F32, nb=16, name="impst")
                        rdr = Ring(kb, pb_, [128, 16, 4], F32, 2, "rd")
                        vv = Ring(kb, pb_, [128, 4, 32], F32, 2, "vv")

                        def cmp_chunks(qt):
                            return [(0, 0, 512)]

                        def causal_chunks(qt):
                            return [(kc, max(0, 128 * kc - 512 * qt), 512) for kc in range(4 * qt + 4)]

                        wcnt = [0]

                        def win_chunks(qt):
                            out = []
                            for kc in range(max(0, 4 * qt - 4), 4 * qt + 4):
                                off = 512 * qt - 128 * kc
                                out.append((kc, max(0, -off), min(512, 640 - off)))
                            import os
                            wc = os.environ.get("WCUT", "")
                            if wc == "diag":
                                out = out[:1]
                            elif wc == "upper":
                                out = [o for o in out if o[0] >= 4 * qt]
                            elif wc == "lower":
                                out = [o for o in out if o[0] <= 4 * qt]
                            elif wc.startswith("n"):
                                n_ = int(wc[1:])
                                out = [o for o in out if o[0] <= 4 * qt]
                                keep = []
                                for o in out:
                                    if wcnt[0] < n_:
                                        keep.append(o)
                                        wcnt[0] += 1
                                out = keep
                            elif wc.startswith("off"):
                                k_ = int(wc[3:]) // 128
                                out = [o for o in out if o[0] == 4 * qt or o[0] == 4 * qt - k_]
                            return out

                        def group_part(g, part):
                            impacc = impacc_g[g]
                            qsa = qsa_g[g]
                            if part == 1:
                                for hj in range(4):
                                    h = g * 4 + hj
                                    j, half = h // 2, h % 2
                                    r0 = half * 64
                                    bct = {}

                                    def qk_cmp(sbk, kc, qt, lo, hi, g=g, j=j, r0=r0):
                                        kb.mm(sbk.t[0:127, 0:512], kcmpT.t[r0:r0 + 64, g, 0:127], qno.t[r0:r0 + 64, j, qt * 512:(qt + 1) * 512], True, True,
                                              [kcmpT.b[g], qno.b[j]], sbk.b)

                                    def seg_cmp(kc, qt, lo, hi, h=h):
                                        bc = bcr.next()
                                        kb.dma(SP, bc.t[:], bcmp_d[h, qt], (), bc.b)
                                        return [("tile", 0, 512, bc.t[0:127, :], bc.b)]

                                    def fin_imp(qt, qb, reg, rb, hj=hj):
                                        kb.cp(impst.t[:, 4 * qt:4 * qt + 4, :], reg, rb, impst.b[4 * qt:4 * qt + 4])

                                    attend(pr, cmp_chunks, qk_cmp, seg_cmp, lambda kc: (ovaug.t[0:127, 0:33], ovaug.b), fin_imp, 127, 33, tmpr)

                                    def norm_imp(hj=hj):
                                        rd = rdr.next()
                                        kb.ts(rd.t[:, :, 0:1], impst.t[:, :, 32:33], 1e-30, None, ALU.max, None, impst.b, rd.b)
                                        kb.op(DVE, lambda: nc.vector.reciprocal(out=rd.t[:, :, 0:1], in_=rd.t[:, :, 0:1]), rd.b, rd.b)
                                        rb_ = rd.t[:, :, 0:1].to_broadcast([128, 16, 32])
                                        if hj == 0:
                                            kb.tt(impacc.t[:], impst.t[:, :, 0:32], rb_, ALU.mult, impst.b + rd.b, impacc.b)
                                        else:
                                            kb.tt(impst.t[:, :, 0:32], impst.t[:, :, 0:32], rb_, ALU.mult, impst.b + rd.b, impst.b)
                                            kb.tt(impacc.t[:], impacc.t[:], impst.t[:, :, 0:32], ALU.add, impst.b + impacc.b, impacc.b)

                                    pipe.add(5, norm_imp, prio=-1)
                                pipe.flush()
                                return
                            if part == 2:
                                if stage < 2.22:
                                    return
                                for tb in range(16):
                                    def mask_tb(g=g, tb=tb):
                                        v = vv.next()
                                        kb.tt(v.t[:, 1, :], impacc_g[g].t[:, tb, :], impm.t[:, 0, tb, :], ALU.mult, [impacc_g[g].b[tb]] + impm.b, v.b)
                                        kb.tt(v.t[:, 2, :], v.t[:, 1, :], impm.t[:, 1, tb, :], ALU.add, v.b + impm.b, v.b)
                                        m8 = m8r.next()
                                        kb.op(DVE, lambda: nc.vector.max(out=m8.t[:, 0:8], in_=v.t[:, 2, :]), v.b, m8.b)
                                        kb.op(DVE, lambda: nc.vector.match_replace(out=v.t[:, 3, :], in_to_replace=m8.t[:, 0:8], in_values=v.t[:, 2, :],
                                                                                   imm_value=-3.0e38), v.b + m8.b, v.b)
                                        kb.op(DVE, lambda: nc.vector.max(out=m8.t[:, 8:16], in_=v.t[:, 3, :]), v.b, m8.b)
                                        kb.ts(v.t[:, 0, :], v.t[:, 2, :], m8.t[:, 15:16], -30000.0, ALU.is_lt, ALU.mult, v.b + m8.b, v.b)
                                        pbt = bank[7]
                                        kb.tr(pbt.t[:, 0:128], v.t[:].rearrange("p a b -> p (a b)"), ident.t[:], v.b + ident.b, pbt.b)
                                        for t_ in qsa_g[g].tiles:
                                            kb.cp(t_.t[64:96, tb * 128:(tb + 1) * 128], pbt.t[0:32, 0:128], pbt.b, t_.b)
                                    pipe.add(1 + tb * (1 if g == 0 else 10), mask_tb, prio=3)
                                return
                            if stage < 2.23:
                                return
                            def prep_nsa(h):
                                bd = bdr.next()
                                kb.dma(SP, bd.t[:], bdiag_d[h], (), bd.b)
                                bw = bwr.next()
                                kb.dma(SP, bw.t[:], bwf_d[h], (), bw.b)
                                qa = qsa.next()
                                kb.cp(qa.t[0:64, :], qno.t[(h % 2) * 64:(h % 2) * 64 + 64, h // 2, :], [qno.b[h // 2]], qa.b)
                                return bd, bw, qa

                            preps = {g * 4: prep_nsa(g * 4)}
                            for pj in range(2):
                                j = g * 2 + pj
                                for half in range(2):
                                    h = 2 * j + half
                                    r0 = half * 64
                                    bd, bw, qa = preps.pop(h)
                                    if h + 1 < g * 4 + 4 and half == 0:
                                        preps[h + 1] = prep_nsa(h + 1)

                                    def mkfin(br, first, h=h, half=half):
                                        def fin(qt, qb, reg, rb):
                                            kb.cp(ost.t[:, 4 * qt:4 * qt + 4, br, :], reg, rb, ost.b[4 * qt:4 * qt + 4])
                                        return fin

                                    def norm_nsa(h=h, half=half):
                                        rd = rdr.next()
                                        kb.ts(rd.t[:, :, 0:3], ost.t[:, :, :, 64], 1e-30, None, ALU.max, None, ost.b, rd.b)
                                        kb.op(DVE, lambda: nc.vector.reciprocal(out=rd.t[:, :, 0:3], in_=rd.t[:, :, 0:3]), rd.b, rd.b)
                                        kb.tt(rd.t[:, :, 0:3], rd.t[:, :, 0:3], gs.t[:, :, h:h + 17:8], ALU.mult, rd.b + gs.b, rd.b)
                                        dst = oacc.t[:, :, half, :]
                                        for br in range(3):
                                            cb = rd.t[:, :, br:br + 1].to_broadcast([128, 16, 64])
                                            if br == 0:
                                                kb.tt(dst, ost.t[:, :, 0, 0:64], cb, ALU.mult, ost.b + rd.b, oacc.b)
                                            else:
                                                kb.tt(ost.t[:, :, br, 0:64], ost.t[:, :, br, 0:64], cb, ALU.mult, ost.b + rd.b, ost.b)
                                                kb.tt(dst, dst, ost.t[:, :, br, 0:64], ALU.add, ost.b + oacc.b, oacc.b)

                                    def qk_cmp(sbk, kc, qt, lo, hi, g=g, j=j, r0=r0):
                                        kb.mm(sbk.t[0:127, 0:512], kcmpT.t[r0:r0 + 64, g, 0:127], qno.t[r0:r0 + 64, j, qt * 512:(qt + 1) * 512], True, True,
                                              [kcmpT.b[g], qno.b[j]], sbk.b)

                                    def seg_cmp(kc, qt, lo, hi, h=h):
                                        bc = bcr.next()
                                        kb.dma(SP, bc.t[:], bcmp_d[h, qt], (), bc.b)
                                        return [("tile", 0, 512, bc.t[0:127, :], bc.b)]

                                    attend(pr, cmp_chunks, qk_cmp, seg_cmp, lambda kc, g=g: (vc_tm.t[0:127, g, :], [vc_tm.b[g]]), mkfin(0, True), 127, 65, tmpr)

                                    def qk_sel(sbk, kc, qt, lo, hi, g=g, qa=qa):
                                        kb.mm(sbk.t[:, lo:hi], ksa[g].t[:, kc * 128:(kc + 1) * 128], qa.t[:, qt * 512 + lo:qt * 512 + hi], True, True,
                                              ksa[g].b + qa.b, sbk.b)

                                    def seg_diag(kc, qt, lo, hi, h=h, bd=bd, bw=bw):
                                        off = 512 * qt - 128 * kc
                                        segs = []
                                        for (m0, m1, kind) in ((0, 256, "bd"), (256, 512, "c"), (512, 640, "bw")):
                                            c0, c1 = max(lo, m0 - off), min(hi, m1 - off)
                                            if c1 <= c0:
                                                continue
                                            if kind == "bd":
                                                segs.append(("tile", c0, c1, bd.t[:, c0 + off:c1 + off], bd.b))
                                            elif kind == "c":
                                                segs.append(("const", c0, c1, tbl31.t[:, h:h + 1], tbl31.b))
                                            else:
                                                segs.append(("tile", c0, c1, bw.t[:, c0 + off - 512:c1 + off - 512], bw.b))
                                        return segs

                                    def seg_sel(kc, qt, lo, hi, h=h, bd=bd):
                                        off = 512 * qt - 128 * kc
                                        segs = []
                                        c0, c1 = max(lo, -off), min(hi, 256 - off)
                                        if c1 > c0:
                                            segs.append(("tile", c0, c1, bd.t[:, c0 + off:c1 + off], bd.b))
                                        c0 = max(lo, 256 - off)
                                        if hi > c0:
                                            segs.append(("const", c0, hi, tbl31.t[:, h:h + 1], tbl31.b))
                                        return segs

                                    if stage >= 2.24:
                                      attend(pr, causal_chunks, qk_sel, seg_sel, lambda kc, g=g: (vs_tm.t[:, kc, g, :], [vs_tm.b[kc]]), mkfin(1, False), 128, 65, tmpr)

                                    def qk_win(sbk, kc, qt, lo, hi, g=g, qa=qa):
                                        kb.mm(sbk.t[:, lo:hi], kwT[g].t[:, kc * 128:(kc + 1) * 128], qa.t[:, qt * 512 + lo:qt * 512 + hi], True, True,
                                              kwT[g].b + qa.b, sbk.b)

                                    if stage >= 2.25:
                                      attend(pr, win_chunks, qk_win, seg_diag, lambda kc, g=g: (vw_tm.t[:, kc, g, :], [vw_tm.b[kc]]), mkfin(2, False), 128, 65, tmpr)
                                    pipe.add(5, norm_nsa, prio=-1)
                                if dbg and j == 0:
                                    pipe.flush()
                                    dump("oacc0", oacc, oacc.t[:], F32)
                                if pj == 0:
                                    preps[2 * j + 2] = prep_nsa(2 * j + 2)
                                pipe.flush()
                                if stage >= 2.26:
                                    transpose_o(oacc, qno, j)


                        group_part(0, 1)
                        group_part(0, 2)
                        group_part(1, 1)
                        group_part(1, 2)
                        group_part(0, 3)
                        pipe.flush()
                        group_part(1, 3)
                        pipe.flush()
                if stage >= 2.3:
                  with Phase(kb) as pf:
                    kfT = kb.sb(pf, [128, 4, S], BF16, nb=4, name="kfT")
                    vf_tm = kb.sb(pf, [128, 16, 8, 65], BF16, nb=16, name="vf")
                    negc_tm = kb.sb(pf, [128, 16, 8], F32, name="negc")
                    c3 = kb.sb(pf, [8, 3, S], BF16, name="c3")
                    kb.memset(vf_tm.t[:, :, :, 64:65], 1.0, vf_tm.b)
                    with Phase(kb) as pa:
                        h2T = kb.sb(pa, [128, KC, S], BF16, nb=KC * 4, name="h2Tf")
                        for tt_ in range(4):
                            rmsnorm_apply(1, h2T, tt_, rstd_mix.t[:, tt_ * 512:(tt_ + 1) * 512], [rstd_mix.b[tt_]])
                        wr = Ring(kb, pa, [128, KC, 128], BF16, 3, "wi")
                        w = wload(wr, 2840, 8)
                        bfg = kb.sb(pa, [8, 2], F32, name="bfg")
                        kb.dma(SP, bfg.t[:, 0:1], bfg_d, (), bfg.b)
                        kb.ts(bfg.t[:, 1:2], bfg.t[:, 0:1], -1.0, None, ALU.mult, None, bfg.b, bfg.b)
                        ones8 = kb.sb(pa, [8, 512], F32, name="ones8")
                        kb.memset(ones8.t[:], 1.0, ones8.b)
                        etr = Ring(kb, pa, [8, 512], F32, 2, "et")
                        ncr = Ring(kb, pa, [8, 512], F32, 2, "nct")
                        r1r = Ring(kb, pa, [8, 512], F32, 2, "r1")
                        prev = [None]

                        def fgate(tt, cs, pb):
                            e = etr.next()
                            kb.actv(e.t[:], pb.t[0:8, :], AF.Exp, pb.b + bfg.b, e.b, bias=bfg.t[:, 1:2], scale=-1.0)
                            kb.actv(e.t[:], e.t[:], AF.Ln, e.b, e.b, bias=1.0)
                            nct = ncr.next()
                            init = 0.0 if prev[0] is None else prev[0].t[:, 511:512]
                            rdp = [] if prev[0] is None else prev[0].b
                            kb.op(DVE, lambda: nc.vector.tensor_tensor_scan(out=nct.t[:], data0=ones8.t[:], data1=e.t[:], initial=init,
                                                                            op0=ALU.mult, op1=ALU.add), e.b + ones8.b + rdp, nct.b)
                            prev[0] = nct
                            r1 = r1r.next()
                            kb.ts(c3.t[:, 0, cs], nct.t[:], -1.0, None, ALU.mult, None, nct.b, c3.b)
                            kb.stt(r1.t[:], nct.t[:], -1.0, c3.t[:, 0, cs], ALU.mult, ALU.subtract, nct.b + c3.b, r1.b)
                            kb.cp(c3.t[:, 1, cs], r1.t[:], r1.b, c3.b)
                            kb.tt(r1.t[:], r1.t[:], c3.t[:, 1, cs], ALU.subtract, r1.b + c3.b, r1.b)
                            kb.cp(c3.t[:, 2, cs], r1.t[:], r1.b, c3.b)
                            pbt = bank[6 + (pjk[0] % 2)]
                            pjk[0] += 1
                            for k_ in range(4):
                                kb.tr(pbt.t[:, k_ * 8:(k_ + 1) * 8], nct.t[0:8, k_ * 128:(k_ + 1) * 128], ident.t[0:8, 0:8], nct.b + ident.b, pbt.b)
                            evac(negc_tm.t[:, 4 * tt:4 * tt + 4, :], pbt.t[:, 0:32].rearrange("p (a b) -> p a b", a=4), pbt.b, negc_tm.b)

                        proj_fm(w, 8, h2T, fgate)
                        for j in range(4):
                            w = wload(wr, 1304 + 128 * j, 128)
                            proj_fm(w, 128, h2T, lambda tt, cs, pb, j=j: evac(qfo.t[:, j, cs], pb.t[:], pb.b, [qfo.b[j]], scale=0.125))
                        for j in range(4):
                            w = wload(wr, 1816 + 128 * j, 128)
                            proj_fm(w, 128, h2T, lambda tt, cs, pb, j=j: evac(kfT.t[:, j, cs], pb.t[:], pb.b, [kfT.b[j]]))
                        for j in range(4):
                            w = wload(wr, 2328 + 128 * j, 128)
                            proj_tm(w, 128, h2T, lambda tb, pb, j=j: evac(vf_tm.t[:, tb, 2 * j:2 * j + 2, 0:64],
                                                                         pb.t[:, 0:128].rearrange("p (g d) -> p g d", g=2), pb.b, [vf_tm.b[tb]]))
                        if dbg:
                            dump("c3", c3, c3.t[:], BF16)
                            dump("negc", negc_tm, negc_tm.t[:], F32)
                    with Phase(kb) as pb_:
                      if stage >= 2.35:
                        qfa = Ring(kb, pb_, [128, S], BF16, 2, "qfa")
                        kfa = Ring(kb, pb_, [128, S], BF16, 2, "kfa")
                        for t_ in qfa.tiles + kfa.tiles:
                            kb.memset(t_.t[64:128, :], 0.0, t_.b)
                        for t_ in kfa.tiles:
                            kb.memset(t_.t[64:67, :], 1.0, t_.b)
                        oacc = kb.sb(pb_, [128, 16, 2, 64], F32, nb=16, name="oaccf")
                        pr = Ring(kb, pb_, [128, 512], BF16, 8, "Pf", nb=4)
                        smr = Ring(kb, pb_, [128, 8], F32, 4, "smf")
                        ostf = kb.sb(pb_, [128, 16, 65], F32, nb=16, name="ostf")
                        rdf = Ring(kb, pb_, [128, 16, 2], F32, 2, "rdf")

                        def causal_chunks(qt):
                            return [(kc, max(0, 128 * kc - 512 * qt), 512) for kc in range(4 * qt + 4)]

                        def prep_fox(h):
                            r0_, j_ = (h % 2) * 64, h // 2
                            qa = qfa.next()
                            kb.cp(qa.t[0:64, :], qfo.t[r0_:r0_ + 64, j_, :], [qfo.b[j_]], qa.b)
                            for i in range(3):
                                kb.dma(SP, qa.t[64 + i:65 + i, :], c3.t[h:h + 1, i, :], c3.b, qa.b)
                            ka = kfa.next()
                            kb.cp(ka.t[0:64, :], kfT.t[r0_:r0_ + 64, j_, :], [kfT.b[j_]], ka.b)
                            return qa, ka

                        fpre = {0: prep_fox(0)}
                        for j in range(4):
                            for half in range(2):
                                h = 2 * j + half
                                r0 = half * 64
                                qa, ka = fpre.pop(h)
                                if h + 1 < 8:
                                    fpre[h + 1] = prep_fox(h + 1)

                                def qk_fox(sbk, kc, qt, lo, hi, qa=qa, ka=ka):
                                    kb.mm(sbk.t[:, lo:hi], ka.t[:, kc * 128:(kc + 1) * 128], qa.t[:, qt * 512 + lo:qt * 512 + hi], True, True,
                                          ka.b + qa.b, sbk.b)

                                def seg_fox(kc, qt, lo, hi, h=h):
                                    off = 512 * qt - 128 * kc
                                    segs = [("const", lo, hi, negc_tm.t[:, kc, h:h + 1], negc_tm.b)]
                                    if off <= 0:
                                        segs.append(("mul", -off, -off + 128, tri.t[:], tri.b))
                                    return segs

                                def fin_fox(qt, qb, reg, rb, half=half):
                                    kb.cp(ostf.t[:, 4 * qt:4 * qt + 4, :], reg, rb, ostf.b[4 * qt:4 * qt + 4])

                                def norm_fox(half=half):
                                    rd = rdf.next()
                                    kb.op(DVE, lambda: nc.vector.reciprocal(out=rd.t[:, :, 0:1], in_=ostf.t[:, :, 64:65]), ostf.b, rd.b)
                                    kb.tt(oacc.t[:, :, half, :], ostf.t[:, :, 0:64], rd.t[:, :, 0:1].to_broadcast([128, 16, 64]), ALU.mult,
                                          ostf.b + rd.b, oacc.b)

                                attend(pr, causal_chunks, qk_fox, seg_fox, lambda kc, h=h: (vf_tm.t[:, kc, h, :], [vf_tm.b[kc]]), fin_fox, 128, 65, None)
                                pipe.add(5, norm_fox, prio=-1)
                            if dbg and j == 0:
                                dump("oaccf0", oacc, oacc.t[:], F32)
                            pipe.flush()
                            transpose_o(oacc, qfo, j)

                with Phase(kb) as pg:
                  if stage >= 2.4:
                    h2T = kb.sb(pg, [128, KC, S], BF16, nb=KC * 4, name="h2Tg")
                    for tt_ in range(4):
                        rmsnorm_apply(1, h2T, tt_, rstd_mix.t[:, tt_ * 512:(tt_ + 1) * 512], [rstd_mix.b[tt_]])
                    yT = kb.sb(pg, [128, KC, S], BF16, nb=KC * 4, name="yT")
                    wga = Ring(kb, pg, [128, KC, 128], BF16, 2, "wga")
                    wgb = Ring(kb, pg, [128, KC, 128], BF16, 2, "wgb")
                    wua = Ring(kb, pg, [128, 4, 128], BF16, 2, "wua")
                    wub = Ring(kb, pg, [128, 4, 128], BF16, 2, "wub")
                    sgr = Ring(kb, pg, [128, 512], F32, 3, "sgg")
                    y1r = Ring(kb, pg, [128, 512], F32, 2, "y1")
                    wupav = wupa_d.rearrange("(k p) f -> p k f", p=128)
                    wupbv = wupb_d.rearrange("(k p) f -> p k f", p=128)
                    for dj in range(KC):
                        ds_ = slice(dj * 128, (dj + 1) * 128)
                        ga = wload(wga, 2848 + 128 * dj, 128)
                        gb = wload(wgb, 3872 + 128 * dj, 128)
                        ua = wua.next()
                        kb.dma(POOL, ua.t[:], wupav[:, :, ds_], (), ua.b)
                        ub = wub.next()
                        kb.dma(POOL, ub.t[:], wupbv[:, :, ds_], (), ub.b)
                        for tt in range(4):
                            cs = slice(tt * 512, (tt + 1) * 512)
                            for kc in range(KC):
                                kb.mm(bank[0].t[:], ga.t[:, kc, :], h2T.t[:, kc, cs], kc == 0, kc == KC - 1, ga.b + [h2T.b[kc * 4 + tt]], bank[0].b)
                            sa = sgr.next()
                            kb.actv(sa.t[:], bank[0].t[:], AF.Sigmoid, bank[0].b, sa.b)
                            for k_ in range(4):
                                kb.mm(bank[1].t[:], ua.t[:, k_, :], qno.t[:, k_, cs], k_ == 0, k_ == 3, ua.b + [qno.b[k_]], bank[1].b)
                            y1 = y1r.next()
                            kb.tt(y1.t[:], sa.t[:], bank[1].t[:], ALU.mult, sa.b + bank[1].b, y1.b)
                            for kc in range(KC):
                                kb.mm(bank[2].t[:], gb.t[:, kc, :], h2T.t[:, kc, cs], kc == 0, kc == KC - 1, gb.b + [h2T.b[kc * 4 + tt]], bank[2].b)
                            sb_ = sgr.next()
                            kb.actv(sb_.t[:], bank[2].t[:], AF.Sigmoid, bank[2].b, sb_.b)
                            for k_ in range(4):
                                kb.mm(bank[3].t[:], ub.t[:, k_, :], qfo.t[:, k_, cs], k_ == 0, k_ == 3, ub.b + [qfo.b[k_]], bank[3].b)
                            kb.tt(sb_.t[:], sb_.t[:], bank[3].t[:], ALU.mult, sb_.b + bank[3].b, sb_.b)
                            kb.tt(yT.t[:, dj, cs], y1.t[:], sb_.t[:], ALU.add, y1.b + sb_.b, [yT.b[dj * 4 + tt]])
                    if dbg:
                        dump("yT", yT, yT.t[:], BF16)
                    wov = wout_d.rearrange("(k p) f -> p k f", p=128)
                    k = 0
                    for dj in range(KC):
                        wo = wga.next()
                        kb.dma(POOL, wo.t[:], wov[:, :, dj * 128:(dj + 1) * 128], (), wo.b)
                        for tt in range(4):
                            cs = slice(tt * 512, (tt + 1) * 512)
                            pb = bank[4 + (k % 2)]
                            k += 1
                            for kc in range(KC):
                                kb.mm(pb.t[:], wo.t[:, kc, :], yT.t[:, kc, cs], kc == 0, kc == KC - 1, wo.b + [yT.b[kc * 4 + tt]], pb.b)
                            kb.tt(xT.t[:, dj, cs], xT.t[:, dj, cs], pb.t[:], ALU.add, pb.b + [xb(dj, tt)], [xb(dj, tt)])
                if dbg:
                    dump("xT2", xT, xT.t[:], F32)

        def memattn():
            with Phase(kb) as pm:
                h3T = kb.sb(pm, [128, KC, S], BF16, nb=KC * 4, name="h3T")
                with Phase(kb) as pr_:
                    rmsnorm_fm(pr_, 2, h3T)
                omT = kb.sb(pm, [128, KC, S], BF16, nb=KC, name="omT")
                mem_nT = kb.sb(pm, [128, KC, MEM], BF16, name="memnT")
                kmT = kb.sb(pm, [128, KC, MEM], BF16, nb=KC, name="kmT")
                vm = kb.sb(pm, [128, 2, 4, 257], BF16, nb=2, name="vm")
                kb.memset(vm.t[:, :, :, 256:257], 1.0, vm.b)
                wr = Ring(kb, pm, [128, KC, 128], BF16, 5, "wm")
                with Phase(kb) as p1:
                    mt = kb.sb(p1, [128, 2, D], F32, name="mt")
                    kb.dma(SP, mt.t[:], mem_d.rearrange("(b p) d -> p b d", p=128), (), mt.b)
                    gkv = kb.sb(p1, [128, D], F32, name="gkv")
                    kb.dma(SP, gkv.t[:], gbc_d[:, 0, :], (), gkv.b)
                    sqj = kb.sb(p1, [128, D], F32, name="sqjm")
                    ss = kb.sb(p1, [128, 4], F32, name="ssm")
                    mn = kb.sb(p1, [128, 2, D], F32, name="mn")
                    for b in range(2):
                        kb.tt(sqj.t[:], mt.t[:, b, :], mt.t[:, b, :], ALU.mult, mt.b, sqj.b)
                        kb.op(DVE, lambda: nc.vector.reduce_sum(out=ss.t[:, b:b + 1], in_=sqj.t[:], axis=AX.X), sqj.b, ss.b)
                        kb.actv(ss.t[:, 2 + b:3 + b], ss.t[:, b:b + 1], AF.Sqrt, ss.b, ss.b, bias=EPS, scale=1.0 / D)
                        kb.op(DVE, lambda: nc.vector.reciprocal(out=ss.t[:, 2 + b:3 + b], in_=ss.t[:, 2 + b:3 + b]), ss.b, ss.b)
                        kb.stt(mn.t[:, b, :], mt.t[:, b, :], ss.t[:, 2 + b:3 + b], gkv.t[:], ALU.mult, ALU.mult, mt.b + ss.b + gkv.b, mn.b)
                        for h4 in range(2):
                            pb = bank[6 + (pjk[0] % 2)]
                            pjk[0] += 1
                            for k_ in range(4):
                                kc = h4 * 4 + k_
                                kb.tr(pb.t[:, k_ * 128:(k_ + 1) * 128], mn.t[:, b, kc * 128:(kc + 1) * 128], ident.t[:], mn.b + ident.b, pb.b)
                            evac(mem_nT.t[:, h4 * 4:h4 * 4 + 4, b * 128:(b + 1) * 128], pb.t[:].rearrange("p (j t) -> p j t", j=4), pb.b, mem_nT.b)
                wkvv = wkv_d.rearrange("(k p) f -> p k f", p=128)
                wqv = wq_d.rearrange("(k p) f -> p k f", p=128)
                wov = wo_d.rearrange("(k p) f -> p k f", p=128)
                for hc in range(KC):
                    w = wr.next()
                    kb.dma(POOL, w.t[:], wkvv[:, :, hc * 128:(hc + 1) * 128], (), w.b)
                    pb = bank[6 + (pjk[0] % 2)]
                    pjk[0] += 1
                    for kc in range(KC):
                        kb.mm(pb.t[:, 0:MEM], w.t[:, kc, :], mem_nT.t[:, kc, :], kc == 0, kc == KC - 1, w.b + mem_nT.b, pb.b)
                    evac(kmT.t[:, hc, :], pb.t[:, 0:MEM], pb.b, [kmT.b[hc]])
                for hv in range(4):
                    for c in range(2):
                        w = wr.next()
                        c0 = D + hv * 256 + c * 128
                        kb.dma(POOL, w.t[:], wkvv[:, :, c0:c0 + 128], (), w.b)
                        for b in range(2):
                            pb = bank[6 + (pjk[0] % 2)]
                            pjk[0] += 1
                            for kc in range(KC):
                                kb.mm(pb.t[:, 0:128], mem_nT.t[:, kc, b * 128:(b + 1) * 128], w.t[:, kc, :], kc == 0, kc == KC - 1, w.b + mem_nT.b, pb.b)
                            evac(vm.t[:, b, hv, c * 128:(c + 1) * 128], pb.t[:, 0:128], pb.b, [vm.b[b]])
                qmr = Ring(kb, pm, [128, 2, S], BF16, 2, "qm")
                oma = kb.sb(pm, [128, 16, 256], F32, nb=16, name="oma")
                pr = Ring(kb, pm, [128, 512], BF16, 6, "Pm", nb=4)
                smr = Ring(kb, pm, [128, 8], F32, 4, "smm")
                for hv in range(4):
                    qm = qmr.next()
                    for c in range(2):
                        w = wr.next()
                        c0 = hv * 256 + c * 128
                        kb.dma(POOL, w.t[:], wqv[:, :, c0:c0 + 128], (), w.b)
                        proj_fm(w, 128, h3T, lambda tt, cs, pb, c=c, qm=qm: evac(qm.t[:, c, cs], pb.t[:], pb.b, qm.b, scale=0.0625))

                    def qk_mem(sbk, kc, qt, lo, hi, hv=hv, qm=qm):
                        for c in range(2):
                            kb.mm(sbk.t[:, 0:512], kmT.t[:, hv * 2 + c, kc * 128:(kc + 1) * 128], qm.t[:, c, qt * 512:(qt + 1) * 512], c == 0, c == 1,
                                  [kmT.b[hv * 2 + c]] + qm.b, sbk.b)

                    def fin_mem(qt, qb, reg, rb):
                        tb = 4 * qt + qb
                        sm = smr.next()
                        kb.op(DVE, lambda: nc.vector.reciprocal(out=sm.t[:, 1:2], in_=reg[:, 256:257]), rb, sm.b)
                        kb.ts(oma.t[:, tb, :], reg[:, 0:256], sm.t[:, 1:2], None, ALU.mult, None, rb + sm.b, [oma.b[tb]])

                    attend(pr, lambda qt: [(0, 0, 512), (1, 0, 512)], qk_mem, lambda kc, qt, lo, hi: [("const", 0, 512, 0.0, [])],
                           lambda kc, hv=hv: (vm.t[:, kc, hv, :], [vm.b[kc]]), fin_mem, 128, 257, None)
                    pipe.flush()
                    for c in range(2):
                        for t4 in range(4):
                            pb = bank[6 + (pjk[0] % 2)]
                            pjk[0] += 1
                            for k_ in range(4):
                                tb = t4 * 4 + k_
                                kb.tr(pb.t[:, k_ * 128:(k_ + 1) * 128], oma.t[:, tb, c * 128:(c + 1) * 128], ident.t[:], [oma.b[tb]] + ident.b, pb.b)
                            evac(omT.t[:, hv * 2 + c, t4 * 512:(t4 + 1) * 512], pb.t[:], pb.b, [omT.b[hv * 2 + c]])
                k = 0
                for dj in range(KC):
                    wo = wr.next()
                    kb.dma(POOL, wo.t[:], wov[:, :, dj * 128:(dj + 1) * 128], (), wo.b)
                    for tt in range(4):
                        cs = slice(tt * 512, (tt + 1) * 512)
                        pb = bank[(k % 2)]
                        k += 1
                        for kc in range(KC):
                            kb.mm(pb.t[:], wo.t[:, kc, :], omT.t[:, kc, cs], kc == 0, kc == KC - 1, wo.b + [omT.b[kc]], pb.b)
                        kb.tt(xT.t[:, dj, cs], xT.t[:, dj, cs], pb.t[:], ALU.add, pb.b + [xb(dj, tt)], [xb(dj, tt)])

        if stage >= 2:
            mixer()
        if stage >= 3:
            memattn()
            if dbg:
                dump("xT3", xT, xT.t[:], F32)

        if stage >= 4:
            with Phase(kb) as ph:
                hT = kb.sb(ph, [128, KC, S], BF16, nb=KC * 4, name="hT2")
                pre = ffn_prefetch(ph, w_d["ffn2_g"], w_d["ffn2_u"], w_d["ffn2_d"])
                rmsnorm_fm(ph, 3, hT)
                ffn(ph, hT, w_d["ffn2_g"], w_d["ffn2_u"], w_d["ffn2_d"], pre)

        with Phase(kb) as ph:
            gbc = kb.sb(ph, [128, D], F32, name="gfin")
            kb.dma(SP, gbc.t[:], gbc_d[:, 1, :], (), gbc.b)
            xor_ = Ring(kb, ph, [128, D], F32, 4, "xo")
            sqj = kb.sb(ph, [128, D], F32, name="sqj")
            ssr = Ring(kb, ph, [128, 2], F32, 2, "ss")
            outs = []
            for tb in range(16):
                xo = xor_.next()
                for half in range(2):
                    pb = bank[6 + half]
                    for j in range(4):
                        kc = half * 4 + j
                        kb.tr(pb.t[:, j * 128:(j + 1) * 128], xT.t[:, kc, tb * 128:(tb + 1) * 128], ident.t[:],
                              [xb(kc, tb // 4)] + ident.b, pb.b)
                    kb.cp(xo.t[:, half * 512:(half + 1) * 512], pb.t[:], pb.b, xo.b, eng=(DVE if half == 0 else ACT))
                ss = ssr.next()
                kb.memset(ss.t[:, 0:1], 0.0, ss.b)
                kb.actv(sqj.t[:], xo.t[:], AF.Square, xo.b + ss.b, sqj.b + ss.b, accum_out=ss.t[:, 0:1])
                kb.actv(ss.t[:, 1:2], ss.t[:, 0:1], AF.Sqrt, ss.b, ss.b, bias=EPS, scale=1.0 / D)
                kb.op(DVE, lambda: nc.vector.reciprocal(out=ss.t[:, 1:2], in_=ss.t[:, 1:2]), ss.b, ss.b)
                kb.stt(xo.t[:], xo.t[:], ss.t[:, 1:2], gbc.t[:], ALU.mult, ALU.mult, xo.b + ss.b + gbc.b, xo.b)
                outs.append(kb.dma(SP, out_d[tb * 128:(tb + 1) * 128, :], xo.t[:], xo.b, ()))
            for t in outs + dump_toks:
                kb.wait(SP, t, True)
            import os
            if os.environ.get("ENGCOUNTS"):
                print("ENGCOUNTS pe", PE.count, "act", ACT.count, "dve", DVE.count, "pool", POOL.count, "sp", SP.count,
                      "dma_sp", sum(SP.dcnt), "dma_pool", sum(POOL.dcnt))
    return nc


_CACHE = {}


def _prep_shared(inp):
    sq = lambda a: np.ascontiguousarray(np.asarray(a, dtype=np.float32))
    fm = lambda g: np.asarray(g, np.float32).reshape(KC, 128).T
    gains = np.zeros((128, 5, KC), np.float32)
    gains[:, 0] = fm(inp["ffn1_norm"][0])
    gains[:, 1] = fm(inp["mix_norm"][0])
    gains[:, 2] = fm(inp["mem_q_norm"][0])
    gains[:, 3] = fm(inp["ffn2_norm"][0])
    gbc = np.zeros((128, 2, D), np.float32)
    gbc[:, 0] = np.asarray(inp["mem_kv_norm"][0], np.float32)[None, :]
    gbc[:, 1] = np.asarray(inp["final_norm"], np.float32)[None, :]
    sh = {
        "gains": gains,
        "gbc": gbc,
        "ident": np.eye(128, dtype=np.float32),
    }
    tbl = np.asarray(inp["rel_bias_table"], np.float32)

    def bucket(n):
        n = np.maximum(n, 0)
        nf = np.maximum(n, 1).astype(np.float32)
        large = 16 + (np.log(nf / np.float32(16)) / np.float32(math.log(128 / 16)) * np.float32(16)).astype(np.int32)
        large = np.minimum(large, 31)
        return np.where(n < 16, n, large)

    p_ = np.arange(128)[:, None]
    m_ = np.arange(256)[None, :]
    dist = m_ - p_
    bk = bucket(dist)
    bdiag = np.empty((8, 128, 256), np.float32)
    bwf = np.empty((8, 128, 128), np.float32)
    bcmp = np.empty((8, 4, 128, 512), np.float32)
    c_ = np.arange(128)[:, None]
    for h in range(8):
        bdiag[h] = np.where(dist >= 0, tbl[bk, h], np.float32(NEG))
        bwf[h] = np.where(p_ > np.arange(128)[None, :], tbl[31, h], np.float32(NEG))
        for qt in range(4):
            dc = qt * 512 + np.arange(512)[None, :] - (16 * c_ + 31)
            v = np.where(dc >= 0, tbl[bucket(dc), h], np.float32(NEG))
            v[127, :] = NEG
            bcmp[h, qt] = v
    sh["bdiag"] = bdiag
    sh["bwf"] = bwf
    sh["bcmp"] = bcmp
    sh["tbl31"] = np.ascontiguousarray(np.broadcast_to(tbl[31][None, :], (128, 8)))
    c0 = np.arange(127)[:, None] * 16
    s0 = np.arange(32)[None, :] * 64
    ov = np.clip(np.minimum(c0 + 32, s0 + 64) - np.maximum(c0, s0), 0, None) / 16
    ovaug = np.zeros((128, 33), np.float32)
    ovaug[:127, :32] = ov
    ovaug[:127, 32] = 1.0
    sh["ovaug"] = ovaug
    t_ = np.arange(S)[:, None]
    blk = np.arange(32)[None, :]
    cur = t_ // 64
    forced = (blk == 0) | (blk == cur) | (blk == cur - 1)
    valid = blk * 64 <= t_
    vm = valid.astype(np.float32)
    add2 = np.where(valid, np.where(forced, np.float32(1e4), np.float32(0.0)), np.float32(NEG)).astype(np.float32)
    impm = np.stack([vm, add2], 0).reshape(2, 16, 128, 32).transpose(2, 0, 1, 3)
    sh["impm"] = np.ascontiguousarray(impm)
    sh["esel"] = (np.arange(S)[None, :] // 64 == np.arange(32)[:, None]).astype(np.float32)
    sh["tri"] = (np.arange(128)[None, :] >= np.arange(128)[:, None]).astype(np.float32)
    sh["bforget"] = np.asarray(inp["mix_b_forget"][0], np.float32).reshape(8, 1)
    sh["posT"] = np.ascontiguousarray(np.stack([np.asarray(inp["cmp_pos_k"][0], np.float32).T, np.asarray(inp["cmp_pos_v"][0], np.float32).T], 0))
    for nm in ("cmp_k_w1", "cmp_v_w1", "cmp_k_w2", "cmp_v_w2", "w_up_nsa", "w_up_fox", "mix_w_out", "mem_w_q", "mem_w_kv", "mem_w_o"):
        sh[nm] = sq(inp[nm][0])
    sh["w_in"] = sq(inp["mix_w_in"][0])
    for nm in ("ffn1", "ffn2"):
        sh[nm + "_w_gate"] = sq(inp[nm + "_w_gate"][0])
        sh[nm + "_w_up"] = sq(inp[nm + "_w_up"][0])
        sh[nm + "_w_down"] = sq(inp[nm + "_w_down"][0])
    return sh


def kernel(_stage=9, _ncores=8, _dbg=False, **inp):
    key = ("prog", _stage, _dbg)
    if key not in _CACHE:
        _CACHE[key] = build_program(_stage, _dbg)
    nc = _CACHE[key]
    sh = _prep_shared(inp)
    x = np.asarray(inp["x"], np.float32)
    mem = np.asarray(inp["mem"], np.float32)
    in_maps = []
    for b in range(_ncores):
        m = dict(sh)
        m["x"] = np.ascontiguousarray(x[b])
        m["mem"] = np.ascontiguousarray(mem[b])
        in_maps.append(m)
    res = run_bass_kernel_spmd(nc, in_maps, core_ids=list(range(_ncores)))
    out = np.stack([np.asarray(r["out"], np.float32) for r in res.results], axis=0)
    if _dbg:
        return out, res.results
    return out
```

```python
import math
from contextlib import ExitStack
import numpy as np
import ml_dtypes
import concourse.bass as bass
import concourse.mybir as mybir
from concourse.bass_utils import run_bass_kernel_spmd

F32 = mybir.dt.float32
BF16 = mybir.dt.bfloat16
AF = mybir.ActivationFunctionType
ALU = mybir.AluOpType
AX = mybir.AxisListType

S = 2048
D = 1024
KC = 8
DFF = 2816
NF = 22
FGROUPS = [(0, 6), (6, 6), (12, 5), (17, 5)]
INW = 4896
MEM = 256
NEG = -1.0e30
EPS = 1e-6


class Tok:
    __slots__ = ("sem", "val")

    def __init__(self, sem, val):
        self.sem = sem
        self.val = val


class Buf:
    __slots__ = ("lw", "rd", "excl")

    def __init__(self):
        self.lw = None
        self.rd = {}
        self.excl = False


class Eng:
    def __init__(self, h, sem, is_pe=False):
        self.h = h
        self.sem = sem
        self.count = 0
        self.seen = {}
        self.is_pe = is_pe
        self.dsems = []
        self.dcnt = []
        self.dnext = 0


class Tile:
    def __init__(self, t, nb=1, init=None):
        self.t = t
        self.b = [Buf() for _ in range(nb)]
        if init:
            for b in self.b:
                b.rd = dict(init)


class Phase(ExitStack):
    def __init__(self, kb):
        super().__init__()
        self.kb = kb
        self.tiles = []

    def __exit__(self, *a):
        ft = self.kb.free_toks
        for t in self.tiles:
            for b in t.b:
                for tok in [b.lw] + list(b.rd.values()):
                    if tok is None:
                        continue
                    k = id(tok.sem)
                    if k not in ft or ft[k].val < tok.val:
                        ft[k] = tok
        return super().__exit__(*a)


class KB:
    def __init__(self, nc, es):
        self.nc = nc
        self.es = es
        sem = lambda n: es.enter_context(nc.semaphore(n))
        self.pe = Eng(nc.tensor, sem("s_pe"), True)
        self.act = Eng(nc.scalar, sem("s_act"))
        self.dve = Eng(nc.vector, sem("s_dve"))
        self.pool = Eng(nc.gpsimd, sem("s_pool"))
        self.sp = Eng(nc.sync, sem("s_sp"))
        for q, nm, n in ((self.sp, "dsp", 16), (self.pool, "dpl", 24)):
            q.dsems = [sem(f"{nm}{i}") for i in range(n)]
            q.dcnt = [0] * n
        self.banks = [Tile(es.enter_context(nc.psum_tensor(f"pb{i}", [128, 512], F32))) for i in range(8)]
        for t in self.banks:
            t.b[0].excl = True
        self.nalloc = 0
        self.free_toks = {}

    def sb(self, es, shape, dtype, nb=1, name=None):
        self.nalloc += 1
        t = es.enter_context(self.nc.sbuf_tensor(f"{name or 't'}_{self.nalloc}", list(shape), dtype))
        tl = Tile(t, nb, self.free_toks)
        if hasattr(es, "tiles"):
            es.tiles.append(tl)
        return tl

    def wait(self, eng, tok, raw):
        if tok is None:
            return
        if tok.sem is eng.sem and eng.is_pe:
            return
        k = id(tok.sem)
        if eng.seen.get(k, 0) >= tok.val:
            return
        eng.h.wait_ge(tok.sem, tok.val)
        eng.seen[k] = tok.val

    def _deps(self, eng, rd, wr):
        for b in rd:
            self.wait(eng, b.lw, True)
            if b.excl:
                for t in b.rd.values():
                    if t.sem is not eng.sem:
                        self.wait(eng, t, False)
        for b in wr:
            self.wait(eng, b.lw, False)
            for t in b.rd.values():
                self.wait(eng, t, False)

    def _commit(self, tok, rd, wr):
        k = id(tok.sem)
        for b in rd:
            b.rd[k] = tok
        for b in wr:
            b.lw = tok
            b.rd = {}

    def op(self, eng, fn, rd=(), wr=()):
        self._deps(eng, rd, wr)
        inst = fn()
        eng.count += 1
        inst.then_inc(eng.sem, 1)
        tok = Tok(eng.sem, eng.count)
        self._commit(tok, rd, wr)
        return tok

    def dma(self, q, out, in_, rd=(), wr=()):
        self._deps(q, rd, wr)
        i = q.dnext
        q.dnext = (i + 1) % len(q.dsems)
        s = q.dsems[i]
        if q.dcnt[i] > 0:
            self.wait(q, Tok(s, 16 * q.dcnt[i]), True)
        inst = q.h.dma_start(out=out, in_=in_)
        inst.then_inc(s, 16)
        q.dcnt[i] += 1
        tok = Tok(s, 16 * q.dcnt[i])
        self._commit(tok, rd, wr)
        return tok

    def barrier(self, bufs):
        pass

    def mm(self, out, lhsT, rhs, start, stop, rd, wr):
        return self.op(self.pe, lambda: self.nc.tensor.matmul(out, lhsT=lhsT, rhs=rhs, start=start, stop=stop), rd, wr)

    def tr(self, out, in_, ident, rd, wr):
        return self.op(self.pe, lambda: self.nc.tensor.transpose(out, in_, ident), rd, wr)

    def actv(self, out, in_, func, rd, wr, bias=0.0, scale=1.0, accum_out=None):
        if accum_out is not None:
            return self.op(self.act, lambda: self.nc.scalar.activation(out=out, in_=in_, func=func, bias=bias, scale=scale, accum_out=accum_out), rd, wr)
        return self.op(self.act, lambda: self.nc.scalar.activation(out=out, in_=in_, func=func, bias=bias, scale=scale), rd, wr)

    def tt(self, out, in0, in1, op, rd, wr, eng=None):
        eng = eng or self.dve
        return self.op(eng, lambda: eng.h.tensor_tensor(out=out, in0=in0, in1=in1, op=op), rd, wr)

    def ts(self, out, in0, s1, s2, op0, op1, rd, wr, eng=None):
        eng = eng or self.dve
        if s2 is None:
            return self.op(eng, lambda: eng.h.tensor_scalar(out=out, in0=in0, scalar1=s1, scalar2=None, op0=op0), rd, wr)
        return self.op(eng, lambda: eng.h.tensor_scalar(out=out, in0=in0, scalar1=s1, scalar2=s2, op0=op0, op1=op1), rd, wr)

    def stt(self, out, in0, scalar, in1, op0, op1, rd, wr, eng=None):
        eng = eng or self.dve
        return self.op(eng, lambda: eng.h.scalar_tensor_tensor(out=out, in0=in0, scalar=scalar, in1=in1, op0=op0, op1=op1), rd, wr)

    def cp(self, out, in_, rd, wr, eng=None):
        eng = eng or self.dve
        if eng is self.act:
            return self.actv(out, in_, AF.Copy, rd, wr)
        return self.op(eng, lambda: eng.h.tensor_copy(out=out, in_=in_), rd, wr)

    def memset(self, ap, val, wr, eng=None):
        eng = eng or self.dve
        return self.op(eng, lambda: eng.h.memset(ap, val), (), wr)


class Ring:
    def __init__(self, kb, es, shape, dtype, n, name, nb=1):
        self.tiles = [kb.sb(es, shape, dtype, nb, f"{name}{i}") for i in range(n)]
        self.i = 0

    def next(self):
        t = self.tiles[self.i]
        self.i = (self.i + 1) % len(self.tiles)
        return t


def build_program(stage=9, dbg=False):
    nc = bass.Bass("TRN2", target_bir_lowering=False)
    dram = lambda n, shp, dt=F32, kind="ExternalInput": nc.dram_tensor(n, list(shp), dt, kind=kind).ap()
    x_d = dram("x", [S, D])
    mem_d = dram("mem", [MEM, D])
    gains_d = dram("gains", [128, 5, KC])
    gbc_d = dram("gbc", [128, 2, D])
    ident_d = dram("ident", [128, 128])
    w_d = {}
    for nm in ("ffn1", "ffn2"):
        w_d[nm + "_g"] = dram(nm + "_w_gate", [D, DFF])
        w_d[nm + "_u"] = dram(nm + "_w_up", [D, DFF])
        w_d[nm + "_d"] = dram(nm + "_w_down", [DFF, D])
    win_d = dram("w_in", [D, INW])
    bdiag_d = dram("bdiag", [8, 128, 256])
    bwf_d = dram("bwf", [8, 128, 128])
    tbl31_d = dram("tbl31", [128, 8])
    bcmp_d = dram("bcmp", [8, 4, 128, 512])
    ovaug_d = dram("ovaug", [128, 33])
    impm_d = dram("impm", [128, 2, 16, 32])
    esel_d = dram("esel", [32, S])
    tri_d = dram("tri", [128, 128])
    bfg_d = dram("bforget", [8, 1])
    posT_d = dram("posT", [2, 64, 32])
    w1_d = [dram("cmp_k_w1", [2048, 256]), dram("cmp_v_w1", [2048, 256])]
    w2_d = [dram("cmp_k_w2", [256, 64]), dram("cmp_v_w2", [256, 64])]
    wupa_d = dram("w_up_nsa", [512, D])
    wupb_d = dram("w_up_fox", [512, D])
    wout_d = dram("mix_w_out", [D, D])
    wq_d = dram("mem_w_q", [D, D])
    wkv_d = dram("mem_w_kv", [D, 2 * D])
    wo_d = dram("mem_w_o", [D, D])
    out_d = dram("out", [S, D], kind="ExternalOutput")

    es = ExitStack()
    dump_toks = []

    def dump(name, tile, ap, dt):
        if not dbg:
            return
        d_ = nc.dram_tensor("dbg_" + name, list(ap.shape), dt, kind="ExternalOutput").ap()
        dump_toks.append(kb.dma(kb.sp, d_, ap, tile.b, ()))

    with es:
        kb = KB(nc, es)
        PE, ACT, DVE, POOL, SP = kb.pe, kb.act, kb.dve, kb.pool, kb.sp
        bank = kb.banks

        ident = kb.sb(es, [128, 128], F32, name="ident")
        kb.dma(SP, ident.t[:], ident_d, (), ident.b)
        gains = kb.sb(es, [128, 5, KC], F32, name="gains")
        kb.dma(SP, gains.t[:], gains_d, (), gains.b)
        ones_bf = kb.sb(es, [128, 128], BF16, name="ones")
        kb.memset(ones_bf.t[:], 1.0, ones_bf.b)

        xT = kb.sb(es, [128, KC, S], F32, nb=KC * 4, name="xT")
        xb = lambda kc, tt: xT.b[kc * 4 + tt]

        with Phase(kb) as ph:
            xst = Ring(kb, ph, [128, D], F32, 4, "xst")
            k = 0
            for tb in range(16):
                xs = xst.next()
                kb.dma(SP, xs.t[:], x_d[tb * 128:(tb + 1) * 128, :], (), xs.b)
                for half in range(2):
                    pb = bank[6 + (k % 2)]
                    k += 1
                    for j in range(4):
                        kc = half * 4 + j
                        kb.tr(pb.t[:, j * 128:(j + 1) * 128], xs.t[:, kc * 128:(kc + 1) * 128], ident.t[:], xs.b + ident.b, pb.b)
                    dst = xT.t[:, half * 4:half * 4 + 4, tb * 128:(tb + 1) * 128]
                    src = pb.t[:].rearrange("p (j t) -> p j t", j=4)
                    wr = [xb(half * 4 + j, tb // 4) for j in range(4)]
                    kb.cp(dst, src, pb.b, wr, eng=(DVE if half == 0 else ACT))

        def rmsnorm_fm(ph, gi, hT, rstd=None):
            sqr = Ring(kb, ph, [128, KC, 512], BF16, 2, "sq", nb=2)
            rsr = None if rstd is not None else Ring(kb, ph, [128, 512], F32, 2, "rstd")
            for tt in range(4):
                cs = slice(tt * 512, (tt + 1) * 512)
                sq = sqr.next()
                kb.actv(sq.t[:, 0:5, :], xT.t[:, 0:5, cs], AF.Square, [xb(kc, tt) for kc in range(5)], [sq.b[0]])
                kb.tt(sq.t[:, 5:8, :], xT.t[:, 5:8, cs], xT.t[:, 5:8, cs], ALU.mult, [xb(kc, tt) for kc in range(5, 8)], [sq.b[1]], eng=POOL)
                pb = bank[6 + (tt % 2)]
                for kc in range(KC):
                    kb.mm(pb.t[:], ones_bf.t[:], sq.t[:, kc, :], kc == 0, kc == KC - 1, [sq.b[0 if kc < 5 else 1]] + ones_bf.b, pb.b)
                if rstd is not None:
                    rs_ap, rs_b = rstd.t[:, cs], [rstd.b[tt]]
                else:
                    rs = rsr.next()
                    rs_ap, rs_b = rs.t[:], rs.b
                kb.actv(rs_ap, pb.t[:], AF.Ln, pb.b, rs_b, bias=EPS, scale=1.0 / D)
                kb.actv(rs_ap, rs_ap, AF.Exp, rs_b, rs_b, scale=-0.5)
                rmsnorm_apply(gi, hT, tt, rs_ap, rs_b)

        def rmsnorm_apply(gi, hT, tt, rs_ap, rs_b):
            cs = slice(tt * 512, (tt + 1) * 512)
            for kc in range(KC):
                kb.stt(hT.t[:, kc, cs], xT.t[:, kc, cs], gains.t[:, gi, kc:kc + 1], rs_ap, ALU.mult, ALU.mult,
                       [xb(kc, tt)] + rs_b + gains.b, [hT.b[kc * 4 + tt]])

        def ffn_prefetch(ph, wg_d, wu_d, wd_d):
            pre = {}
            pre["wgr"] = Ring(kb, ph, [128, KC, 128], BF16, 3, "wg")
            pre["wur"] = Ring(kb, ph, [128, KC, 128], BF16, 3, "wu")
            pre["wdr"] = Ring(kb, ph, [128, 6, D], BF16, 2, "wd")
            wgv = wg_d.rearrange("(k p) f -> p k f", p=128)
            wuv = wu_d.rearrange("(k p) f -> p k f", p=128)
            f0, nf = FGROUPS[0]
            wd = pre["wdr"].next()
            kb.dma(POOL, wd.t[:, 0:nf, :], wd_d[f0 * 128:(f0 + nf) * 128, :].rearrange("(f p) d -> p f d", p=128), (), wd.b)
            pre["wd0"] = wd
            for f in range(3):
                wg = pre["wgr"].next()
                wu = pre["wur"].next()
                kb.dma(POOL, wg.t[:], wgv[:, :, f * 128:(f + 1) * 128], (), wg.b)
                kb.dma(POOL, wu.t[:], wuv[:, :, f * 128:(f + 1) * 128], (), wu.b)
                pre[f] = (wg, wu)
            return pre

        def ffn(ph, hT, wg_d, wu_d, wd_d, pre):
            wgr, wur, wdr = pre["wgr"], pre["wur"], pre["wdr"]
            aT = kb.sb(ph, [128, 6, S], BF16, nb=6 * 4, name="aT")
            sgr = Ring(kb, ph, [128, 512], F32, 2, "sg")
            wgv = wg_d.rearrange("(k p) f -> p k f", p=128)
            wuv = wu_d.rearrange("(k p) f -> p k f", p=128)
            nb = 0
            for gi_, (f0, nf) in enumerate(FGROUPS):
                if gi_ == 0:
                    wd = pre["wd0"]
                else:
                    wd = wdr.next()
                    kb.dma(POOL, wd.t[:, 0:nf, :], wd_d[f0 * 128:(f0 + nf) * 128, :].rearrange("(f p) d -> p f d", p=128), (), wd.b)
                for fi in range(nf):
                    f = f0 + fi
                    if f in pre:
                        wg, wu = pre[f]
                    else:
                        wg = wgr.next()
                        wu = wur.next()
                        kb.dma(POOL, wg.t[:], wgv[:, :, f * 128:(f + 1) * 128], (), wg.b)
                        kb.dma(POOL, wu.t[:], wuv[:, :, f * 128:(f + 1) * 128], (), wu.b)
                    for tt in range(4):
                        cs = slice(tt * 512, (tt + 1) * 512)
                        pg = bank[(nb % 2) * 2]
                        pu = bank[(nb % 2) * 2 + 1]
                        nb += 1
                        for kc in range(KC):
                            kb.mm(pg.t[:], wg.t[:, kc, :], hT.t[:, kc, cs], kc == 0, kc == KC - 1, wg.b + [hT.b[kc * 4 + tt]], pg.b)
                        for kc in range(KC):
                            kb.mm(pu.t[:], wu.t[:, kc, :], hT.t[:, kc, cs], kc == 0, kc == KC - 1, wu.b + [hT.b[kc * 4 + tt]], pu.b)
                        sg = sgr.next()
                        kb.actv(sg.t[:], pg.t[:], AF.Silu, pg.b, sg.b)
                        kb.tt(aT.t[:, fi, cs], sg.t[:], pu.t[:], ALU.mult, sg.b + pu.b, [aT.b[fi * 4 + tt]])
                k = 0
                for dj in range(KC):
                    for tt in range(4):
                        cs = slice(tt * 512, (tt + 1) * 512)
                        pb = bank[4 + (k % 2)]
                        k += 1
                        for fi in range(nf):
                            kb.mm(pb.t[:], wd.t[:, fi, dj * 128:(dj + 1) * 128], aT.t[:, fi, cs], fi == 0, fi == nf - 1,
                                  wd.b + [aT.b[fi * 4 + tt]], pb.b)
                        kb.stt(xT.t[:, dj, cs], pb.t[:], 0.5, xT.t[:, dj, cs], ALU.mult, ALU.add,
                               pb.b + [xb(dj, tt)], [xb(dj, tt)])

        if stage >= 1:
            with Phase(kb) as ph:
                hT = kb.sb(ph, [128, KC, S], BF16, nb=KC * 4, name="hT")
                pre = ffn_prefetch(ph, w_d["ffn1_g"], w_d["ffn1_u"], w_d["ffn1_d"])
                rmsnorm_fm(ph, 0, hT)
                dump("hT", hT, hT.t[:], BF16)
                ffn(ph, hT, w_d["ffn1_g"], w_d["ffn1_u"], w_d["ffn1_d"], pre)
                dump("xT1", xT, xT.t[:], F32)


        winv = win_d.rearrange("(k p) f -> p k f", p=128)
        evk = [0]

        def evac(out, in_, rd, wr, scale=None):
            evk[0] += 1
            if evk[0] % 2 == 0:
                return kb.actv(out, in_, AF.Copy, rd, wr, scale=(1.0 if scale is None else scale))
            if scale is None:
                return kb.cp(out, in_, rd, wr)
            return kb.ts(out, in_, scale, None, ALU.mult, None, rd, wr)

        def wload(ring, c0, n, dup=False):
            w = ring.next()
            kb.dma(POOL, w.t[:, :, 0:n], winv[:, :, c0:c0 + n], (), w.b)
            if dup:
                kb.dma(POOL, w.t[:, :, n:2 * n], winv[:, :, c0:c0 + n], (), w.b)
            return w

        pjk = [0]

        def proj_fm(w, M, hT, fn):
            for tt in range(4):
                cs = slice(tt * 512, (tt + 1) * 512)
                pb = bank[6 + (pjk[0] % 2)]
                pjk[0] += 1
                for kc in range(KC):
                    kb.mm(pb.t[0:M, :], w.t[:, kc, 0:M], hT.t[:, kc, cs], kc == 0, kc == KC - 1, w.b + [hT.b[kc * 4 + tt]], pb.b)
                fn(tt, cs, pb)

        def proj_tm(w, N, hT, fn):
            for tb in range(16):
                pb = bank[6 + (pjk[0] % 2)]
                pjk[0] += 1
                for kc in range(KC):
                    kb.mm(pb.t[:, 0:N], hT.t[:, kc, tb * 128:(tb + 1) * 128], w.t[:, kc, 0:N], kc == 0, kc == KC - 1,
                          w.b + [hT.b[kc * 4 + tb // 4]], pb.b)
                fn(tb, pb)

        accb = [[Buf(), Buf()] for _ in range(4)]
        sk = [0]
        apar = [0]

        class Pipe:
            def __init__(self):
                self.q = []
                self.step = 0
                self.seq = 0

            def add(self, delay, fn, prio=1):
                self.q.append((self.step + delay, prio, self.seq, fn))
                self.seq += 1

            def tick(self):
                self.step += 1
                due = sorted([x for x in self.q if x[0] <= self.step])
                self.q = [x for x in self.q if x[0] > self.step]
                for x in due:
                    x[3]()

            def flush(self):
                while self.q:
                    self.tick()

        pipe = Pipe()
        NS = 2
        acck = [0]

        def attend(pr, chunks_fn, qk_fn, seg_fn, v_fn, fin_fn, krows, nv, tmpr):
            packed = 4 * nv <= 512
            ns = 4 if packed else 2
            for qt in range(4):
                chs = chunks_fn(qt)
                if not chs:
                    continue
                abanks = []
                if packed:
                    ab_ = bank[4 + acck[0] % 3]
                    acck[0] += 1
                    abanks = [ab_] * 4
                else:
                    for qb in range(4):
                        abanks.append(bank[2 + acck[0] % 5])
                        acck[0] += 1
                contrib = {qb: [i for i, (kc, lo, hi) in enumerate(chs) if lo <= qb * 128 and (qb + 1) * 128 <= hi] for qb in range(4)}
                lastc = max(max(v_) for v_ in contrib.values() if v_)
                started = [False]
                for i, (kc, lo, hi) in enumerate(chs):
                    pipe.tick()
                    sbk = bank[sk[0] % ns]
                    sk[0] += 1
                    qk_fn(sbk, kc, qt, lo, hi)
                    p = pr.next()

                    def stage_b(sbk=sbk, p=p, kc=kc, qt=qt, lo=lo, hi=hi):
                        for seg in seg_fn(kc, qt, lo, hi):
                            kind, c0, c1 = seg[0], seg[1], seg[2]
                            pbs = p.b[c0 // 128:(c1 + 127) // 128]
                            if kind == "const":
                                kb.actv(p.t[0:krows, c0:c1], sbk.t[0:krows, c0:c1], AF.Exp, sbk.b + seg[4], pbs, bias=seg[3])
                            elif kind == "tile":
                                tm = tmpr.next()
                                kb.tt(tm.t[0:krows, 0:c1 - c0], sbk.t[0:krows, c0:c1], seg[3], ALU.add, sbk.b + seg[4], tm.b)
                                kb.actv(p.t[0:krows, c0:c1], tm.t[0:krows, 0:c1 - c0], AF.Exp, tm.b, pbs)
                            elif kind == "mul":
                                kb.tt(p.t[0:krows, c0:c1], p.t[0:krows, c0:c1], seg[3], ALU.mult, pbs + seg[4], pbs)

                    def stage_c(i=i, kc=kc, p=p, qt=qt, contrib=contrib, abanks=abanks, started=started, lastc=lastc):
                        for qb in range(4):
                            if i in contrib[qb]:
                                first = contrib[qb][0] == i
                                last = contrib[qb][-1] == i
                                ab = abanks[qb]
                                v_ap, v_b = v_fn(kc)
                                if packed:
                                    reg = ab.t[:, qb * nv:(qb + 1) * nv]
                                    st = not started[0]
                                    started[0] = True
                                    kb.op(PE, lambda: nc.tensor.matmul(reg, lhsT=p.t[0:krows, qb * 128:(qb + 1) * 128], rhs=v_ap, start=st,
                                                                       stop=(i == lastc and last), skip_group_check=True),
                                          [p.b[qb]] + v_b, ab.b)
                                else:
                                    reg = ab.t[:, 0:nv]
                                    kb.mm(reg, p.t[0:krows, qb * 128:(qb + 1) * 128], v_ap, first, last, [p.b[qb]] + v_b, ab.b)
                                    if last:
                                        pipe.add(1, lambda qt=qt, qb=qb, reg=reg, ab=ab: fin_fn(qt, qb, reg, ab.b), prio=0)
                        if packed and i == lastc:
                            ab = abanks[0]
                            pipe.add(1, lambda qt=qt, ab=ab: fin_fn(qt, None, ab.t[:, 0:4 * nv].rearrange("p (q v) -> p q v", q=4), ab.b), prio=0)

                    pipe.add(1, stage_b, prio=1)
                    pipe.add(4, stage_c, prio=2)

        def transpose_o(oacc, dst, j):
            for t4 in range(4):
                pb = bank[6 + (pjk[0] % 2)]
                pjk[0] += 1
                for k_ in range(4):
                    tb = t4 * 4 + k_
                    kb.tr(pb.t[:, k_ * 128:(k_ + 1) * 128], oacc.t[:, tb, :, :].rearrange("p a b -> p (a b)"), ident.t[:],
                          [oacc.b[tb]] + ident.b, pb.b)
                evac(dst.t[:, j, t4 * 512:(t4 + 1) * 512], pb.t[:], pb.b, [dst.b[j]])

        def mixer():
            with Phase(kb) as pm:
                qno = kb.sb(pm, [128, 4, S], BF16, nb=4, name="qno")
                qfo = kb.sb(pm, [128, 4, S], BF16, nb=4, name="qfo")
                rstd_mix = kb.sb(pm, [128, S], F32, nb=4, name="rstdmix")
                tbl31 = kb.sb(pm, [128, 8], F32, name="tbl31")
                kb.dma(SP, tbl31.t[:], tbl31_d, (), tbl31.b)
                tri = kb.sb(pm, [128, 128], BF16, name="tri")
                kb.dma(POOL, tri.t[:], tri_d, (), tri.b)

                if stage >= 2:
                  with Phase(kb) as pn:
                    ksa = [kb.sb(pn, [128, S], BF16, name="ksa") for _ in range(2)]
                    kwT = [kb.sb(pn, [128, S], BF16, name="kwT") for _ in range(2)]
                    vs_tm = kb.sb(pn, [128, 16, 2, 65], BF16, nb=16, name="vs")
                    vw_tm = kb.sb(pn, [128, 16, 2, 65], BF16, nb=16, name="vw")
                    kcmpT = kb.sb(pn, [128, 2, 128], BF16, nb=2, name="kcmp")
                    vc_tm = kb.sb(pn, [128, 2, 65], BF16, nb=2, name="vc")
                    gs = kb.sb(pn, [128, 16, 24], F32, nb=16, name="gs")
                    ovaug = kb.sb(pn, [128, 33], BF16, name="ovaug")
                    kb.dma(POOL, ovaug.t[:], ovaug_d, (), ovaug.b)
                    kb.memset(vs_tm.t[:, :, :, 64:65], 1.0, vs_tm.b)
                    kb.memset(vw_tm.t[:, :, :, 64:65], 1.0, vw_tm.b)
                    kb.memset(vc_tm.t[:, :, 64:65], 1.0, vc_tm.b)
                    for g in range(2):
                        kb.memset(ksa[g].t[64:128, :], 0.0, ksa[g].b)
                        kb.memset(kwT[g].t[64:128, :], 0.0, kwT[g].b)
                        kb.dma(POOL, ksa[g].t[64:96, :], esel_d, (), ksa[g].b)
                    with Phase(kb) as pa:
                        h2T = kb.sb(pa, [128, KC, S], BF16, nb=KC * 4, name="h2T")
                        with Phase(kb) as pr_:
                            rmsnorm_fm(pr_, 1, h2T, rstd=rstd_mix)
                        wr = Ring(kb, pa, [128, KC, 128], BF16, 3, "wi")
                        kcT = kb.sb(pa, [128, 16, 128], BF16, name="kcT")
                        vcT = kb.sb(pa, [128, 16, 128], BF16, name="vcT")
                        for j in range(4):
                            w = wload(wr, 128 * j, 128)
                            proj_fm(w, 128, h2T, lambda tt, cs, pb, j=j: evac(qno.t[:, j, cs], pb.t[:], pb.b, [qno.b[j]], scale=0.125))
                        for g in range(2):
                            w = wload(wr, 768 + 64 * g, 64)
                            proj_fm(w, 64, h2T, lambda tt, cs, pb, g=g: evac(ksa[g].t[0:64, cs], pb.t[0:64, :], pb.b, ksa[g].b))
                            w = wload(wr, 1024 + 64 * g, 64)
                            proj_fm(w, 64, h2T, lambda tt, cs, pb, g=g: evac(kwT[g].t[0:64, cs], pb.t[0:64, :], pb.b, kwT[g].b))
                        w = wload(wr, 512, 128)
                        proj_fm(w, 128, h2T, lambda tt, cs, pb: evac(kcT.t[:, :, tt * 32:(tt + 1) * 32], pb.t[:].rearrange("p (c r) -> p r c", r=16), pb.b, kcT.b))
                        w = wload(wr, 640, 128)
                        proj_fm(w, 128, h2T, lambda tt, cs, pb: evac(vcT.t[:, :, tt * 32:(tt + 1) * 32], pb.t[:].rearrange("p (c r) -> p r c", r=16), pb.b, vcT.b))
                        w = wload(wr, 896, 128)
                        proj_tm(w, 128, h2T, lambda tb, pb: evac(vs_tm.t[:, tb, :, 0:64], pb.t[:, 0:128].rearrange("p (g d) -> p g d", g=2), pb.b, [vs_tm.b[tb]]))
                        w = wload(wr, 1152, 128)
                        proj_tm(w, 128, h2T, lambda tb, pb: evac(vw_tm.t[:, tb, :, 0:64], pb.t[:, 0:128].rearrange("p (g d) -> p g d", g=2), pb.b, [vw_tm.b[tb]]))
                        w = wload(wr, 1280, 24)
                        proj_tm(w, 24, h2T, lambda tb, pb: kb.actv(gs.t[:, tb, :], pb.t[:, 0:24], AF.Sigmoid, pb.b, [gs.b[tb]]))
                        with Phase(kb) as pc:
                            w1 = kb.sb(pc, [128, 32, 256], BF16, name="w1")
                            w2 = kb.sb(pc, [128, 2, 128], BF16, name="w2")
                            posT = kb.sb(pc, [128, 32], BF16, name="posT")
                            hid = kb.sb(pc, [128, 2, 128], BF16, nb=2, name="hid")
                            bsb = kb.sb(pc, [128, 1], F32, name="bsb")
                            zr = Ring(kb, pc, [128, 4, 128], F32, 2, "z")
                            for kv in range(2):
                                src = kcT if kv == 0 else vcT
                                w1v = w1_d[kv].rearrange("(i d) h -> d i h", d=64)
                                kb.dma(POOL, w1.t[0:64], w1v, (), w1.b)
                                kb.dma(POOL, w1.t[64:128], w1v, (), w1.b)
                                w2v = w2_d[kv].rearrange("(c p) d -> p c d", p=128)
                                kb.dma(POOL, w2.t[:, :, 0:64], w2v, (), w2.b)
                                kb.dma(POOL, w2.t[:, :, 64:128], w2v, (), w2.b)
                                kb.dma(POOL, posT.t[0:64, :], posT_d[kv], (), posT.b)
                                kb.dma(POOL, posT.t[64:128, :], posT_d[kv], (), posT.b)
                                for g in range(2):
                                    r0 = g * 64
                                    for hc in range(2):
                                        hs = slice(hc * 128, (hc + 1) * 128)
                                        pbb = bank[7]
                                        for i in range(32):
                                            kb.mm(pbb.t[:, 0:1], w1.t[r0:r0 + 64, i, hs], posT.t[r0:r0 + 64, i:i + 1], i == 0, i == 31,
                                                  w1.b + posT.b, pbb.b)
                                        kb.cp(bsb.t[:], pbb.t[:, 0:1], pbb.b, bsb.b)
                                        pbh = bank[6]
                                        for i in range(32):
                                            kb.mm(pbh.t[:, 0:127], w1.t[r0:r0 + 64, i, hs], src.t[r0:r0 + 64, i % 16, i // 16:i // 16 + 127], i == 0, i == 31,
                                                  w1.b + src.b, pbh.b)
                                        z = zr.next()
                                        kb.actv(z.t[:, 0, 0:127], pbh.t[:, 0:127], AF.Identity, pbh.b + bsb.b, z.b, bias=bsb.t[:, 0:1])
                                        kb.tt(z.t[:, 1, 0:127], z.t[:, 0, 0:127], z.t[:, 0, 0:127], ALU.mult, z.b, z.b)
                                        kb.ts(z.t[:, 1, 0:127], z.t[:, 1, 0:127], 0.044715, 1.0, ALU.mult, ALU.add, z.b, z.b)
                                        kb.tt(z.t[:, 2, 0:127], z.t[:, 1, 0:127], z.t[:, 0, 0:127], ALU.mult, z.b, z.b)
                                        kb.actv(z.t[:, 3, 0:127], z.t[:, 2, 0:127], AF.Sigmoid, z.b, z.b, scale=1.5957691216057308)
                                        kb.tt(hid.t[:, hc, 0:127], z.t[:, 3, 0:127], z.t[:, 0, 0:127], ALU.mult, z.b, [hid.b[hc]])
                                    pbo = bank[7]
                                    if kv == 0:
                                        for hc in range(2):
                                            kb.mm(pbo.t[:, 0:127], w2.t[:, hc, :], hid.t[:, hc, 0:127], hc == 0, hc == 1, w2.b + [hid.b[hc]], pbo.b)
                                        evac(kcmpT.t[:, g, 0:127], pbo.t[:, 0:127], pbo.b, [kcmpT.b[g]])
                                    else:
                                        for hc in range(2):
                                            kb.mm(pbo.t[0:127, 0:64], hid.t[:, hc, 0:127], w2.t[:, hc, 0:64], hc == 0, hc == 1, w2.b + [hid.b[hc]], pbo.b)
                                        evac(vc_tm.t[0:127, g, 0:64], pbo.t[0:127, 0:64], pbo.b, [vc_tm.b[g]])
                        if dbg:
                            dump("qno", qno, qno.t[:], BF16)
                            dump("kcmpT", kcmpT, kcmpT.t[:, :, 0:127], BF16)
                            dump("vc_tm", vc_tm, vc_tm.t[0:127], BF16)
                            dump("gs", gs, gs.t[:], F32)
                            dump("ksa0", ksa[0], ksa[0].t[:], BF16)
                            dump("vs_tm", vs_tm, vs_tm.t[:], BF16)
                    with Phase(kb) as pb_:
                      if stage >= 2.2:
                        qsa_g = [Ring(kb, pb_, [128, S], BF16, 2, "qsa"), Ring(kb, pb_, [128, S], BF16, 2, "qsb")]
                        for r_ in qsa_g:
                            for t_ in r_.tiles:
                                kb.memset(t_.t[64:128, :], 0.0, t_.b)
                        oacc = kb.sb(pb_, [128, 16, 2, 64], F32, nb=16, name="oacc")
                        impacc_g = [kb.sb(pb_, [128, 16, 32], F32, nb=16, name="impacc") for _ in range(2)]
                        impm = kb.sb(pb_, [128, 2, 16, 32], F32, name="impm")
                        kb.dma(SP, impm.t[:], impm_d, (), impm.b)
                        pr = Ring(kb, pb_, [128, 512], BF16, 10, "P", nb=4)
                        tmpr = Ring(kb, pb_, [128, 512], F32, 3, "tmp")
                        bdr = Ring(kb, pb_, [128, 256], F32, 2, "bd")
                        bwr = Ring(kb, pb_, [128, 128], F32, 2, "bw")
                        bcr = Ring(kb, pb_, [128, 512], F32, 2, "bc")
                        smr = Ring(kb, pb_, [128, 8], F32, 4, "sm")
                        o65 = Ring(kb, pb_, [128, 64], F32, 2, "o65")
                        m8r = Ring(kb, pb_, [128, 16], F32, 2, "m8")
                        ost = kb.sb(pb_, [128, 16, 3, 65], F32, nb=16, name="ost")
                        impst = kb.sb(pb_, [128, 16, 33], F32, nb=16, name="impst")
                        rdr = Ring(kb, pb_, [128, 16, 4], F32, 2, "rd")
                        vv = Ring(kb, pb_, [128, 4, 32], F32, 2, "vv")

                        def cmp_chunks(qt):
                            return [(0, 0, 512)]

                        def causal_chunks(qt):
                            return [(kc, max(0, 128 * kc - 512 * qt), 512) for kc in range(4 * qt + 4)]

                        wcnt = [0]

                        def win_chunks(qt):
                            out = []
                            for kc in range(max(0, 4 * qt - 4), 4 * qt + 4):
                                off = 512 * qt - 128 * kc
                                out.append((kc, max(0, -off), min(512, 640 - off)))
                            import os
                            wc = os.environ.get("WCUT", "")
                            if wc == "diag":
                                out = out[:1]
                            elif wc == "upper":
                                out = [o for o in out if o[0] >= 4 * qt]
                            elif wc == "lower":
                                out = [o for o in out if o[0] <= 4 * qt]
                            elif wc.startswith("n"):
                                n_ = int(wc[1:])
                                out = [o for o in out if o[0] <= 4 * qt]
                                keep = []
                                for o in out:
                                    if wcnt[0] < n_:
                                        keep.append(o)
                                        wcnt[0] += 1
                                out = keep
                            elif wc.startswith("off"):
                                k_ = int(wc[3:]) // 128
                                out = [o for o in out if o[0] == 4 * qt or o[0] == 4 * qt - k_]
                            return out

                        def group_part(g, part):
                            impacc = impacc_g[g]
                            qsa = qsa_g[g]
                            if part == 1:
                                for hj in range(4):
                                    h = g * 4 + hj
                                    j, half = h // 2, h % 2
                                    r0 = half * 64
                                    bct = {}

                                    def qk_cmp(sbk, kc, qt, lo, hi, g=g, j=j, r0=r0):
                                        kb.mm(sbk.t[0:127, 0:512], kcmpT.t[r0:r0 + 64, g, 0:127], qno.t[r0:r0 + 64, j, qt * 512:(qt + 1) * 512], True, True,
                                              [kcmpT.b[g], qno.b[j]], sbk.b)

                                    def seg_cmp(kc, qt, lo, hi, h=h):
                                        bc = bcr.next()
                                        kb.dma(SP, bc.t[:], bcmp_d[h, qt], (), bc.b)
                                        return [("tile", 0, 512, bc.t[0:127, :], bc.b)]

                                    def fin_imp(qt, qb, reg, rb, hj=hj):
                                        kb.cp(impst.t[:, 4 * qt:4 * qt + 4, :], reg, rb, impst.b[4 * qt:4 * qt + 4])

                                    attend(pr, cmp_chunks, qk_cmp, seg_cmp, lambda kc: (ovaug.t[0:127, 0:33], ovaug.b), fin_imp, 127, 33, tmpr)

                                    def norm_imp(hj=hj):
                                        rd = rdr.next()
                                        kb.ts(rd.t[:, :, 0:1], impst.t[:, :, 32:33], 1e-30, None, ALU.max, None, impst.b, rd.b)
                                        kb.op(DVE, lambda: nc.vector.reciprocal(out=rd.t[:, :, 0:1], in_=rd.t[:, :, 0:1]), rd.b, rd.b)
                                        rb_ = rd.t[:, :, 0:1].to_broadcast([128, 16, 32])
                                        if hj == 0:
                                            kb.tt(impacc.t[:], impst.t[:, :, 0:32], rb_, ALU.mult, impst.b + rd.b, impacc.b)
                                        else:
                                            kb.tt(impst.t[:, :, 0:32], impst.t[:, :, 0:32], rb_, ALU.mult, impst.b + rd.b, impst.b)
                                            kb.tt(impacc.t[:], impacc.t[:], impst.t[:, :, 0:32], ALU.add, impst.b + impacc.b, impacc.b)

                                    pipe.add(6, norm_imp, prio=-1)
                                pipe.flush()
                                return
                            if part == 2:
                                if stage < 2.22:
                                    return
                                for tb in range(16):
                                    def mask_tb(g=g, tb=tb):
                                        v = vv.next()
                                        kb.tt(v.t[:, 1, :], impacc_g[g].t[:, tb, :], impm.t[:, 0, tb, :], ALU.mult, [impacc_g[g].b[tb]] + impm.b, v.b)
                                        kb.tt(v.t[:, 2, :], v.t[:, 1, :], impm.t[:, 1, tb, :], ALU.add, v.b + impm.b, v.b)
                                        m8 = m8r.next()
                                        kb.op(DVE, lambda: nc.vector.max(out=m8.t[:, 0:8], in_=v.t[:, 2, :]), v.b, m8.b)
                                        kb.op(DVE, lambda: nc.vector.match_replace(out=v.t[:, 3, :], in_to_replace=m8.t[:, 0:8], in_values=v.t[:, 2, :],
                                                                                   imm_value=-3.0e38), v.b + m8.b, v.b)
                                        kb.op(DVE, lambda: nc.vector.max(out=m8.t[:, 8:16], in_=v.t[:, 3, :]), v.b, m8.b)
                                        kb.ts(v.t[:, 0, :], v.t[:, 2, :], m8.t[:, 15:16], -30000.0, ALU.is_lt, ALU.mult, v.b + m8.b, v.b)
                                        pbt = bank[7]
                                        kb.tr(pbt.t[:, 0:128], v.t[:].rearrange("p a b -> p (a b)"), ident.t[:], v.b + ident.b, pbt.b)
                                        for t_ in qsa_g[g].tiles:
                                            kb.cp(t_.t[64:96, tb * 128:(tb + 1) * 128], pbt.t[0:32, 0:128], pbt.b, t_.b)
                                    pipe.add(1 + tb * (1 if g == 0 else 10), mask_tb, prio=3)
                                return
                            if stage < 2.23:
                                return
                            def prep_nsa(h):
                                bd = bdr.next()
                                kb.dma(SP, bd.t[:], bdiag_d[h], (), bd.b)
                                bw = bwr.next()
                                kb.dma(SP, bw.t[:], bwf_d[h], (), bw.b)
                                qa = qsa.next()
                                kb.cp(qa.t[0:64, :], qno.t[(h % 2) * 64:(h % 2) * 64 + 64, h // 2, :], [qno.b[h // 2]], qa.b)
                                return bd, bw, qa

                            preps = {g * 4: prep_nsa(g * 4)}
                            for pj in range(2):
                                j = g * 2 + pj
                                for half in range(2):
                                    h = 2 * j + half
                                    r0 = half * 64
                                    bd, bw, qa = preps.pop(h)
                                    if h + 1 < g * 4 + 4 and half == 0:
                                        preps[h + 1] = prep_nsa(h + 1)

                                    def mkfin(br, first, h=h, half=half):
                                        def fin(qt, qb, reg, rb):
                                            kb.cp(ost.t[:, 4 * qt:4 * qt + 4, br, :], reg, rb, ost.b[4 * qt:4 * qt + 4])
                                        return fin

                                    def norm_nsa(h=h, half=half):
                                        rd = rdr.next()
                                        kb.ts(rd.t[:, :, 0:3], ost.t[:, :, :, 64], 1e-30, None, ALU.max, None, ost.b, rd.b)
                                        kb.op(DVE, lambda: nc.vector.reciprocal(out=rd.t[:, :, 0:3], in_=rd.t[:, :, 0:3]), rd.b, rd.b)
                                        kb.tt(rd.t[:, :, 0:3], rd.t[:, :, 0:3], gs.t[:, :, h:h + 17:8], ALU.mult, rd.b + gs.b, rd.b)
                                        dst = oacc.t[:, :, half, :]
                                        for br in range(3):
                                            cb = rd.t[:, :, br:br + 1].to_broadcast([128, 16, 64])
                                            if br == 0:
                                                kb.tt(dst, ost.t[:, :, 0, 0:64], cb, ALU.mult, ost.b + rd.b, oacc.b)
                                            else:
                                                kb.tt(ost.t[:, :, br, 0:64], ost.t[:, :, br, 0:64], cb, ALU.mult, ost.b + rd.b, ost.b)
                                                kb.tt(dst, dst, ost.t[:, :, br, 0:64], ALU.add, ost.b + oacc.b, oacc.b)

                                    def qk_cmp(sbk, kc, qt, lo, hi, g=g, j=j, r0=r0):
                                        kb.mm(sbk.t[0:127, 0:512], kcmpT.t[r0:r0 + 64, g, 0:127], qno.t[r0:r0 + 64, j, qt * 512:(qt + 1) * 512], True, True,
                                              [kcmpT.b[g], qno.b[j]], sbk.b)

                                    def seg_cmp(kc, qt, lo, hi, h=h):
                                        bc = bcr.next()
                                        kb.dma(SP, bc.t[:], bcmp_d[h, qt], (), bc.b)
                                        return [("tile", 0, 512, bc.t[0:127, :], bc.b)]

                                    attend(pr, cmp_chunks, qk_cmp, seg_cmp, lambda kc, g=g: (vc_tm.t[0:127, g, :], [vc_tm.b[g]]), mkfin(0, True), 127, 65, tmpr)

                                    def qk_sel(sbk, kc, qt, lo, hi, g=g, qa=qa):
                                        kb.mm(sbk.t[:, lo:hi], ksa[g].t[:, kc * 128:(kc + 1) * 128], qa.t[:, qt * 512 + lo:qt * 512 + hi], True, True,
                                              ksa[g].b + qa.b, sbk.b)

                                    def seg_diag(kc, qt, lo, hi, h=h, bd=bd, bw=bw):
                                        off = 512 * qt - 128 * kc
                                        segs = []
                                        for (m0, m1, kind) in ((0, 256, "bd"), (256, 512, "c"), (512, 640, "bw")):
                                            c0, c1 = max(lo, m0 - off), min(hi, m1 - off)
                                            if c1 <= c0:
                                                continue
                                            if kind == "bd":
                                                segs.append(("tile", c0, c1, bd.t[:, c0 + off:c1 + off], bd.b))
                                            elif kind == "c":
                                                segs.append(("const", c0, c1, tbl31.t[:, h:h + 1], tbl31.b))
                                            else:
                                                segs.append(("tile", c0, c1, bw.t[:, c0 + off - 512:c1 + off - 512], bw.b))
                                        return segs

                                    def seg_sel(kc, qt, lo, hi, h=h, bd=bd):
                                        off = 512 * qt - 128 * kc
                                        segs = []
                                        c0, c1 = max(lo, -off), min(hi, 256 - off)
                                        if c1 > c0:
                                            segs.append(("tile", c0, c1, bd.t[:, c0 + off:c1 + off], bd.b))
                                        c0 = max(lo, 256 - off)
                                        if hi > c0:
                                            segs.append(("const", c0, hi, tbl31.t[:, h:h + 1], tbl31.b))
                                        return segs

                                    if stage >= 2.24:
                                      attend(pr, causal_chunks, qk_sel, seg_sel, lambda kc, g=g: (vs_tm.t[:, kc, g, :], [vs_tm.b[kc]]), mkfin(1, False), 128, 65, tmpr)

                                    def qk_win(sbk, kc, qt, lo, hi, g=g, qa=qa):
                                        kb.mm(sbk.t[:, lo:hi], kwT[g].t[:, kc * 128:(kc + 1) * 128], qa.t[:, qt * 512 + lo:qt * 512 + hi], True, True,
                                              kwT[g].b + qa.b, sbk.b)

                                    if stage >= 2.25:
                                      attend(pr, win_chunks, qk_win, seg_diag, lambda kc, g=g: (vw_tm.t[:, kc, g, :], [vw_tm.b[kc]]), mkfin(2, False), 128, 65, tmpr)
                                    pipe.add(6, norm_nsa, prio=-1)
                                if dbg and j == 0:
                                    pipe.flush()
                                    dump("oacc0", oacc, oacc.t[:], F32)
                                if pj == 0:
                                    preps[2 * j + 2] = prep_nsa(2 * j + 2)
                                pipe.flush()
                                if stage >= 2.26:
                                    transpose_o(oacc, qno, j)


                        group_part(0, 1)
                        group_part(0, 2)
                        group_part(1, 1)
                        group_part(1, 2)
                        group_part(0, 3)
                        pipe.flush()
                        group_part(1, 3)
                        pipe.flush()
                if stage >= 2.3:
                  with Phase(kb) as pf:
                    kfT = kb.sb(pf, [128, 4, S], BF16, nb=4, name="kfT")
                    vf_tm = kb.sb(pf, [128, 16, 8, 65], BF16, nb=16, name="vf")
                    negc_tm = kb.sb(pf, [128, 16, 8], F32, name="negc")
                    c3 = kb.sb(pf, [8, 3, S], BF16, name="c3")
                    kb.memset(vf_tm.t[:, :, :, 64:65], 1.0, vf_tm.b)
                    with Phase(kb) as pa:
                        h2T = kb.sb(pa, [128, KC, S], BF16, nb=KC * 4, name="h2Tf")
                        for tt_ in range(4):
                            rmsnorm_apply(1, h2T, tt_, rstd_mix.t[:, tt_ * 512:(tt_ + 1) * 512], [rstd_mix.b[tt_]])
                        wr = Ring(kb, pa, [128, KC, 128], BF16, 3, "wi")
                        w = wload(wr, 2840, 8)
                        bfg = kb.sb(pa, [8, 2], F32, name="bfg")
                        kb.dma(SP, bfg.t[:, 0:1], bfg_d, (), bfg.b)
                        kb.ts(bfg.t[:, 1:2], bfg.t[:, 0:1], -1.0, None, ALU.mult, None, bfg.b, bfg.b)
                        ones8 = kb.sb(pa, [8, 512], F32, name="ones8")
                        kb.memset(ones8.t[:], 1.0, ones8.b)
                        etr = Ring(kb, pa, [8, 512], F32, 2, "et")
                        ncr = Ring(kb, pa, [8, 512], F32, 2, "nct")
                        r1r = Ring(kb, pa, [8, 512], F32, 2, "r1")
                        prev = [None]

                        def fgate(tt, cs, pb):
                            e = etr.next()
                            kb.actv(e.t[:], pb.t[0:8, :], AF.Exp, pb.b + bfg.b, e.b, bias=bfg.t[:, 1:2], scale=-1.0)
                            kb.actv(e.t[:], e.t[:], AF.Ln, e.b, e.b, bias=1.0)
                            nct = ncr.next()
                            init = 0.0 if prev[0] is None else prev[0].t[:, 511:512]
                            rdp = [] if prev[0] is None else prev[0].b
                            kb.op(DVE, lambda: nc.vector.tensor_tensor_scan(out=nct.t[:], data0=ones8.t[:], data1=e.t[:], initial=init,
                                                                            op0=ALU.mult, op1=ALU.add), e.b + ones8.b + rdp, nct.b)
                            prev[0] = nct
                            r1 = r1r.next()
                            kb.ts(c3.t[:, 0, cs], nct.t[:], -1.0, None, ALU.mult, None, nct.b, c3.b)
                            kb.stt(r1.t[:], nct.t[:], -1.0, c3.t[:, 0, cs], ALU.mult, ALU.subtract, nct.b + c3.b, r1.b)
                            kb.cp(c3.t[:, 1, cs], r1.t[:], r1.b, c3.b)
                            kb.tt(r1.t[:], r1.t[:], c3.t[:, 1, cs], ALU.subtract, r1.b + c3.b, r1.b)
                            kb.cp(c3.t[:, 2, cs], r1.t[:], r1.b, c3.b)
                            pbt = bank[6 + (pjk[0] % 2)]
                            pjk[0] += 1
                            for k_ in range(4):
                                kb.tr(pbt.t[:, k_ * 8:(k_ + 1) * 8], nct.t[0:8, k_ * 128:(k_ + 1) * 128], ident.t[0:8, 0:8], nct.b + ident.b, pbt.b)
                            evac(negc_tm.t[:, 4 * tt:4 * tt + 4, :], pbt.t[:, 0:32].rearrange("p (a b) -> p a b", a=4), pbt.b, negc_tm.b)

                        proj_fm(w, 8, h2T, fgate)
                        for j in range(4):
                            w = wload(wr, 1304 + 128 * j, 128)
                            proj_fm(w, 128, h2T, lambda tt, cs, pb, j=j: evac(qfo.t[:, j, cs], pb.t[:], pb.b, [qfo.b[j]], scale=0.125))
                        for j in range(4):
                            w = wload(wr, 1816 + 128 * j, 128)
                            proj_fm(w, 128, h2T, lambda tt, cs, pb, j=j: evac(kfT.t[:, j, cs], pb.t[:], pb.b, [kfT.b[j]]))
                        for j in range(4):
                            w = wload(wr, 2328 + 128 * j, 128)
                            proj_tm(w, 128, h2T, lambda tb, pb, j=j: evac(vf_tm.t[:, tb, 2 * j:2 * j + 2, 0:64],
                                                                         pb.t[:, 0:128].rearrange("p (g d) -> p g d", g=2), pb.b, [vf_tm.b[tb]]))
                        if dbg:
                            dump("c3", c3, c3.t[:], BF16)
                            dump("negc", negc_tm, negc_tm.t[:], F32)
                    with Phase(kb) as pb_:
                      if stage >= 2.35:
                        qfa = Ring(kb, pb_, [128, S], BF16, 2, "qfa")
                        kfa = Ring(kb, pb_, [128, S], BF16, 2, "kfa")
                        for t_ in qfa.tiles + kfa.tiles:
                            kb.memset(t_.t[64:128, :], 0.0, t_.b)
                        for t_ in kfa.tiles:
                            kb.memset(t_.t[64:67, :], 1.0, t_.b)
                        oacc = kb.sb(pb_, [128, 16, 2, 64], F32, nb=16, name="oaccf")
                        pr = Ring(kb, pb_, [128, 512], BF16, 10, "Pf", nb=4)
                        smr = Ring(kb, pb_, [128, 8], F32, 4, "smf")
                        ostf = kb.sb(pb_, [128, 16, 65], F32, nb=16, name="ostf")
                        rdf = Ring(kb, pb_, [128, 16, 2], F32, 2, "rdf")

                        def causal_chunks(qt):
                            return [(kc, max(0, 128 * kc - 512 * qt), 512) for kc in range(4 * qt + 4)]

                        def prep_fox(h):
                            r0_, j_ = (h % 2) * 64, h // 2
                            qa = qfa.next()
                            kb.cp(qa.t[0:64, :], qfo.t[r0_:r0_ + 64, j_, :], [qfo.b[j_]], qa.b)
                            for i in range(3):
                                kb.dma(SP, qa.t[64 + i:65 + i, :], c3.t[h:h + 1, i, :], c3.b, qa.b)
                            ka = kfa.next()
                            kb.cp(ka.t[0:64, :], kfT.t[r0_:r0_ + 64, j_, :], [kfT.b[j_]], ka.b)
                            return qa, ka

                        fpre = {0: prep_fox(0)}
                        for j in range(4):
                            for half in range(2):
                                h = 2 * j + half
                                r0 = half * 64
                                qa, ka = fpre.pop(h)
                                if h + 1 < 8:
                                    fpre[h + 1] = prep_fox(h + 1)

                                def qk_fox(sbk, kc, qt, lo, hi, qa=qa, ka=ka):
                                    kb.mm(sbk.t[:, lo:hi], ka.t[:, kc * 128:(kc + 1) * 128], qa.t[:, qt * 512 + lo:qt * 512 + hi], True, True,
                                          ka.b + qa.b, sbk.b)

                                def seg_fox(kc, qt, lo, hi, h=h):
                                    off = 512 * qt - 128 * kc
                                    segs = [("const", lo, hi, negc_tm.t[:, kc, h:h + 1], negc_tm.b)]
                                    if off <= 0:
                                        segs.append(("mul", -off, -off + 128, tri.t[:], tri.b))
                                    return segs

                                def fin_fox(qt, qb, reg, rb, half=half):
                                    kb.cp(ostf.t[:, 4 * qt:4 * qt + 4, :], reg, rb, ostf.b[4 * qt:4 * qt + 4])

                                def norm_fox(half=half):
                                    rd = rdf.next()
                                    kb.op(DVE, lambda: nc.vector.reciprocal(out=rd.t[:, :, 0:1], in_=ostf.t[:, :, 64:65]), ostf.b, rd.b)
                                    kb.tt(oacc.t[:, :, half, :], ostf.t[:, :, 0:64], rd.t[:, :, 0:1].to_broadcast([128, 16, 64]), ALU.mult,
                                          ostf.b + rd.b, oacc.b)

                                attend(pr, causal_chunks, qk_fox, seg_fox, lambda kc, h=h: (vf_tm.t[:, kc, h, :], [vf_tm.b[kc]]), fin_fox, 128, 65, None)
                                pipe.add(6, norm_fox, prio=-1)
                            if dbg and j == 0:
                                dump("oaccf0", oacc, oacc.t[:], F32)
                            pipe.flush()
                            transpose_o(oacc, qfo, j)

                with Phase(kb) as pg:
                  if stage >= 2.4:
                    h2T = kb.sb(pg, [128, KC, S], BF16, nb=KC * 4, name="h2Tg")
                    for tt_ in range(4):
                        rmsnorm_apply(1, h2T, tt_, rstd_mix.t[:, tt_ * 512:(tt_ + 1) * 512], [rstd_mix.b[tt_]])
                    yT = kb.sb(pg, [128, KC, S], BF16, nb=KC * 4, name="yT")
                    wga = Ring(kb, pg, [128, KC, 128], BF16, 2, "wga")
                    wgb = Ring(kb, pg, [128, KC, 128], BF16, 2, "wgb")
                    wua = Ring(kb, pg, [128, 4, 128], BF16, 2, "wua")
                    wub = Ring(kb, pg, [128, 4, 128], BF16, 2, "wub")
                    sgr = Ring(kb, pg, [128, 512], F32, 3, "sgg")
                    y1r = Ring(kb, pg, [128, 512], F32, 2, "y1")
                    wupav = wupa_d.rearrange("(k p) f -> p k f", p=128)
                    wupbv = wupb_d.rearrange("(k p) f -> p k f", p=128)
                    for dj in range(KC):
                        ds_ = slice(dj * 128, (dj + 1) * 128)
                        ga = wload(wga, 2848 + 128 * dj, 128)
                        gb = wload(wgb, 3872 + 128 * dj, 128)
                        ua = wua.next()
                        kb.dma(POOL, ua.t[:], wupav[:, :, ds_], (), ua.b)
                        ub = wub.next()
                        kb.dma(POOL, ub.t[:], wupbv[:, :, ds_], (), ub.b)
                        for tt in range(4):
                            cs = slice(tt * 512, (tt + 1) * 512)
                            for kc in range(KC):
                                kb.mm(bank[0].t[:], ga.t[:, kc, :], h2T.t[:, kc, cs], kc == 0, kc == KC - 1, ga.b + [h2T.b[kc * 4 + tt]], bank[0].b)
                            sa = sgr.next()
                            kb.actv(sa.t[:], bank[0].t[:], AF.Sigmoid, bank[0].b, sa.b)
                            for k_ in range(4):
                                kb.mm(bank[1].t[:], ua.t[:, k_, :], qno.t[:, k_, cs], k_ == 0, k_ == 3, ua.b + [qno.b[k_]], bank[1].b)
                            y1 = y1r.next()
                            kb.tt(y1.t[:], sa.t[:], bank[1].t[:], ALU.mult, sa.b + bank[1].b, y1.b)
                            for kc in range(KC):
                                kb.mm(bank[2].t[:], gb.t[:, kc, :], h2T.t[:, kc, cs], kc == 0, kc == KC - 1, gb.b + [h2T.b[kc * 4 + tt]], bank[2].b)
                            sb_ = sgr.next()
                            kb.actv(sb_.t[:], bank[2].t[:], AF.Sigmoid, bank[2].b, sb_.b)
                            for k_ in range(4):
                                kb.mm(bank[3].t[:], ub.t[:, k_, :], qfo.t[:, k_, cs], k_ == 0, k_ == 3, ub.b + [qfo.b[k_]], bank[3].b)
                            kb.tt(sb_.t[:], sb_.t[:], bank[3].t[:], ALU.mult, sb_.b + bank[3].b, sb_.b)
                            kb.tt(yT.t[:, dj, cs], y1.t[:], sb_.t[:], ALU.add, y1.b + sb_.b, [yT.b[dj * 4 + tt]])
                    if dbg:
                        dump("yT", yT, yT.t[:], BF16)
                    wov = wout_d.rearrange("(k p) f -> p k f", p=128)
                    k = 0
                    for dj in range(KC):
                        wo = wga.next()
                        kb.dma(POOL, wo.t[:], wov[:, :, dj * 128:(dj + 1) * 128], (), wo.b)
                        for tt in range(4):
                            cs = slice(tt * 512, (tt + 1) * 512)
                            pb = bank[4 + (k % 2)]
                            k += 1
                            for kc in range(KC):
                                kb.mm(pb.t[:], wo.t[:, kc, :], yT.t[:, kc, cs], kc == 0, kc == KC - 1, wo.b + [yT.b[kc * 4 + tt]], pb.b)
                            kb.tt(xT.t[:, dj, cs], xT.t[:, dj, cs], pb.t[:], ALU.add, pb.b + [xb(dj, tt)], [xb(dj, tt)])
                if dbg:
                    dump("xT2", xT, xT.t[:], F32)

        def memattn():
            with Phase(kb) as pm:
                h3T = kb.sb(pm, [128, KC, S], BF16, nb=KC * 4, name="h3T")
                with Phase(kb) as pr_:
                    rmsnorm_fm(pr_, 2, h3T)
                omT = kb.sb(pm, [128, KC, S], BF16, nb=KC, name="omT")
                mem_nT = kb.sb(pm, [128, KC, MEM], BF16, name="memnT")
                kmT = kb.sb(pm, [128, KC, MEM], BF16, nb=KC, name="kmT")
                vm = kb.sb(pm, [128, 2, 4, 257], BF16, nb=2, name="vm")
                kb.memset(vm.t[:, :, :, 256:257], 1.0, vm.b)
                wr = Ring(kb, pm, [128, KC, 128], BF16, 5, "wm")
                with Phase(kb) as p1:
                    mt = kb.sb(p1, [128, 2, D], F32, name="mt")
                    kb.dma(SP, mt.t[:], mem_d.rearrange("(b p) d -> p b d", p=128), (), mt.b)
                    gkv = kb.sb(p1, [128, D], F32, name="gkv")
                    kb.dma(SP, gkv.t[:], gbc_d[:, 0, :], (), gkv.b)
                    sqj = kb.sb(p1, [128, D], F32, name="sqjm")
                    ss = kb.sb(p1, [128, 4], F32, name="ssm")
                    mn = kb.sb(p1, [128, 2, D], F32, name="mn")
                    for b in range(2):
                        kb.tt(sqj.t[:], mt.t[:, b, :], mt.t[:, b, :], ALU.mult, mt.b, sqj.b)
                        kb.op(DVE, lambda: nc.vector.reduce_sum(out=ss.t[:, b:b + 1], in_=sqj.t[:], axis=AX.X), sqj.b, ss.b)
                        kb.actv(ss.t[:, 2 + b:3 + b], ss.t[:, b:b + 1], AF.Sqrt, ss.b, ss.b, bias=EPS, scale=1.0 / D)
                        kb.op(DVE, lambda: nc.vector.reciprocal(out=ss.t[:, 2 + b:3 + b], in_=ss.t[:, 2 + b:3 + b]), ss.b, ss.b)
                        kb.stt(mn.t[:, b, :], mt.t[:, b, :], ss.t[:, 2 + b:3 + b], gkv.t[:], ALU.mult, ALU.mult, mt.b + ss.b + gkv.b, mn.b)
                        for h4 in range(2):
                            pb = bank[6 + (pjk[0] % 2)]
                            pjk[0] += 1
                            for k_ in range(4):
                                kc = h4 * 4 + k_
                                kb.tr(pb.t[:, k_ * 128:(k_ + 1) * 128], mn.t[:, b, kc * 128:(kc + 1) * 128], ident.t[:], mn.b + ident.b, pb.b)
                            evac(mem_nT.t[:, h4 * 4:h4 * 4 + 4, b * 128:(b + 1) * 128], pb.t[:].rearrange("p (j t) -> p j t", j=4), pb.b, mem_nT.b)
                wkvv = wkv_d.rearrange("(k p) f -> p k f", p=128)
                wqv = wq_d.rearrange("(k p) f -> p k f", p=128)
                wov = wo_d.rearrange("(k p) f -> p k f", p=128)
                for hc in range(KC):
                    w = wr.next()
                    kb.dma(POOL, w.t[:], wkvv[:, :, hc * 128:(hc + 1) * 128], (), w.b)
                    pb = bank[6 + (pjk[0] % 2)]
                    pjk[0] += 1
                    for kc in range(KC):
                        kb.mm(pb.t[:, 0:MEM], w.t[:, kc, :], mem_nT.t[:, kc, :], kc == 0, kc == KC - 1, w.b + mem_nT.b, pb.b)
                    evac(kmT.t[:, hc, :], pb.t[:, 0:MEM], pb.b, [kmT.b[hc]])
                for hv in range(4):
                    for c in range(2):
                        w = wr.next()
                        c0 = D + hv * 256 + c * 128
                        kb.dma(POOL, w.t[:], wkvv[:, :, c0:c0 + 128], (), w.b)
                        for b in range(2):
                            pb = bank[6 + (pjk[0] % 2)]
                            pjk[0] += 1
                            for kc in range(KC):
                                kb.mm(pb.t[:, 0:128], mem_nT.t[:, kc, b * 128:(b + 1) * 128], w.t[:, kc, :], kc == 0, kc == KC - 1, w.b + mem_nT.b, pb.b)
                            evac(vm.t[:, b, hv, c * 128:(c + 1) * 128], pb.t[:, 0:128], pb.b, [vm.b[b]])
                qmr = Ring(kb, pm, [128, 2, S], BF16, 2, "qm")
                oma = kb.sb(pm, [128, 16, 256], F32, nb=16, name="oma")
                pr = Ring(kb, pm, [128, 512], BF16, 10, "Pm", nb=4)
                smr = Ring(kb, pm, [128, 8], F32, 4, "smm")
                for hv in range(4):
                    qm = qmr.next()
                    for c in range(2):
                        w = wr.next()
                        c0 = hv * 256 + c * 128
                        kb.dma(POOL, w.t[:], wqv[:, :, c0:c0 + 128], (), w.b)
                        proj_fm(w, 128, h3T, lambda tt, cs, pb, c=c, qm=qm: evac(qm.t[:, c, cs], pb.t[:], pb.b, qm.b, scale=0.0625))

                    def qk_mem(sbk, kc, qt, lo, hi, hv=hv, qm=qm):
                        for c in range(2):
                            kb.mm(sbk.t[:, 0:512], kmT.t[:, hv * 2 + c, kc * 128:(kc + 1) * 128], qm.t[:, c, qt * 512:(qt + 1) * 512], c == 0, c == 1,
                                  [kmT.b[hv * 2 + c]] + qm.b, sbk.b)

                    def fin_mem(qt, qb, reg, rb):
                        tb = 4 * qt + qb
                        sm = smr.next()
                        kb.op(DVE, lambda: nc.vector.reciprocal(out=sm.t[:, 1:2], in_=reg[:, 256:257]), rb, sm.b)
                        kb.ts(oma.t[:, tb, :], reg[:, 0:256], sm.t[:, 1:2], None, ALU.mult, None, rb + sm.b, [oma.b[tb]])

                    attend(pr, lambda qt: [(0, 0, 512), (1, 0, 512)], qk_mem, lambda kc, qt, lo, hi: [("const", 0, 512, 0.0, [])],
                           lambda kc, hv=hv: (vm.t[:, kc, hv, :], [vm.b[kc]]), fin_mem, 128, 257, None)
                    pipe.flush()
                    for c in range(2):
                        for t4 in range(4):
                            pb = bank[6 + (pjk[0] % 2)]
                            pjk[0] += 1
                            for k_ in range(4):
                                tb = t4 * 4 + k_
                                kb.tr(pb.t[:, k_ * 128:(k_ + 1) * 128], oma.t[:, tb, c * 128:(c + 1) * 128], ident.t[:], [oma.b[tb]] + ident.b, pb.b)
                            evac(omT.t[:, hv * 2 + c, t4 * 512:(t4 + 1) * 512], pb.t[:], pb.b, [omT.b[hv * 2 + c]])
                k = 0
                for dj in range(KC):
                    wo = wr.next()
                    kb.dma(POOL, wo.t[:], wov[:, :, dj * 128:(dj + 1) * 128], (), wo.b)
                    for tt in range(4):
                        cs = slice(tt * 512, (tt + 1) * 512)
                        pb = bank[(k % 2)]
                        k += 1
                        for kc in range(KC):
                            kb.mm(pb.t[:], wo.t[:, kc, :], omT.t[:, kc, cs], kc == 0, kc == KC - 1, wo.b + [omT.b[kc]], pb.b)
                        kb.tt(xT.t[:, dj, cs], xT.t[:, dj, cs], pb.t[:], ALU.add, pb.b + [xb(dj, tt)], [xb(dj, tt)])

        if stage >= 2:
            mixer()
        if stage >= 3:
            memattn()
            if dbg:
                dump("xT3", xT, xT.t[:], F32)

        if stage >= 4:
            with Phase(kb) as ph:
                hT = kb.sb(ph, [128, KC, S], BF16, nb=KC * 4, name="hT2")
                pre = ffn_prefetch(ph, w_d["ffn2_g"], w_d["ffn2_u"], w_d["ffn2_d"])
                rmsnorm_fm(ph, 3, hT)
                ffn(ph, hT, w_d["ffn2_g"], w_d["ffn2_u"], w_d["ffn2_d"], pre)

        with Phase(kb) as ph:
            gbc = kb.sb(ph, [128, D], F32, name="gfin")
            kb.dma(SP, gbc.t[:], gbc_d[:, 1, :], (), gbc.b)
            xor_ = Ring(kb, ph, [128, D], F32, 4, "xo")
            sqj = kb.sb(ph, [128, D], F32, name="sqj")
            ssr = Ring(kb, ph, [128, 2], F32, 2, "ss")
            outs = []
            for tb in range(16):
                xo = xor_.next()
                for half in range(2):
                    pb = bank[6 + half]
                    for j in range(4):
                        kc = half * 4 + j
                        kb.tr(pb.t[:, j * 128:(j + 1) * 128], xT.t[:, kc, tb * 128:(tb + 1) * 128], ident.t[:],
                              [xb(kc, tb // 4)] + ident.b, pb.b)
                    kb.cp(xo.t[:, half * 512:(half + 1) * 512], pb.t[:], pb.b, xo.b, eng=(DVE if half == 0 else ACT))
                ss = ssr.next()
                kb.memset(ss.t[:, 0:1], 0.0, ss.b)
                kb.actv(sqj.t[:], xo.t[:], AF.Square, xo.b + ss.b, sqj.b + ss.b, accum_out=ss.t[:, 0:1])
                kb.actv(ss.t[:, 1:2], ss.t[:, 0:1], AF.Sqrt, ss.b, ss.b, bias=EPS, scale=1.0 / D)
                kb.op(DVE, lambda: nc.vector.reciprocal(out=ss.t[:, 1:2], in_=ss.t[:, 1:2]), ss.b, ss.b)
                kb.stt(xo.t[:], xo.t[:], ss.t[:, 1:2], gbc.t[:], ALU.mult, ALU.mult, xo.b + ss.b + gbc.b, xo.b)
                outs.append(kb.dma(SP, out_d[tb * 128:(tb + 1) * 128, :], xo.t[:], xo.b, ()))
            for t in outs + dump_toks:
                kb.wait(SP, t, True)
            import os
            if os.environ.get("ENGCOUNTS"):
                print("ENGCOUNTS pe", PE.count, "act", ACT.count, "dve", DVE.count, "pool", POOL.count, "sp", SP.count,
                      "dma_sp", sum(SP.dcnt), "dma_pool", sum(POOL.dcnt))
    return nc


_CACHE = {}


def _prep_shared(inp):
    sq = lambda a: np.ascontiguousarray(np.asarray(a, dtype=np.float32))
    fm = lambda g: np.asarray(g, np.float32).reshape(KC, 128).T
    gains = np.zeros((128, 5, KC), np.float32)
    gains[:, 0] = fm(inp["ffn1_norm"][0])
    gains[:, 1] = fm(inp["mix_norm"][0])
    gains[:, 2] = fm(inp["mem_q_norm"][0])
    gains[:, 3] = fm(inp["ffn2_norm"][0])
    gbc = np.zeros((128, 2, D), np.float32)
    gbc[:, 0] = np.asarray(inp["mem_kv_norm"][0], np.float32)[None, :]
    gbc[:, 1] = np.asarray(inp["final_norm"], np.float32)[None, :]
    sh = {
        "gains": gains,
        "gbc": gbc,
        "ident": np.eye(128, dtype=np.float32),
    }
    tbl = np.asarray(inp["rel_bias_table"], np.float32)

    def bucket(n):
        n = np.maximum(n, 0)
        nf = np.maximum(n, 1).astype(np.float32)
        large = 16 + (np.log(nf / np.float32(16)) / np.float32(math.log(128 / 16)) * np.float32(16)).astype(np.int32)
        large = np.minimum(large, 31)
        return np.where(n < 16, n, large)

    p_ = np.arange(128)[:, None]
    m_ = np.arange(256)[None, :]
    dist = m_ - p_
    bk = bucket(dist)
    bdiag = np.empty((8, 128, 256), np.float32)
    bwf = np.empty((8, 128, 128), np.float32)
    bcmp = np.empty((8, 4, 128, 512), np.float32)
    c_ = np.arange(128)[:, None]
    for h in range(8):
        bdiag[h] = np.where(dist >= 0, tbl[bk, h], np.float32(NEG))
        bwf[h] = np.where(p_ > np.arange(128)[None, :], tbl[31, h], np.float32(NEG))
        for qt in range(4):
            dc = qt * 512 + np.arange(512)[None, :] - (16 * c_ + 31)
            v = np.where(dc >= 0, tbl[bucket(dc), h], np.float32(NEG))
            v[127, :] = NEG
            bcmp[h, qt] = v
    sh["bdiag"] = bdiag
    sh["bwf"] = bwf
    sh["bcmp"] = bcmp
    sh["tbl31"] = np.ascontiguousarray(np.broadcast_to(tbl[31][None, :], (128, 8)))
    c0 = np.arange(127)[:, None] * 16
    s0 = np.arange(32)[None, :] * 64
    ov = np.clip(np.minimum(c0 + 32, s0 + 64) - np.maximum(c0, s0), 0, None) / 16
    ovaug = np.zeros((128, 33), np.float32)
    ovaug[:127, :32] = ov
    ovaug[:127, 32] = 1.0
    sh["ovaug"] = ovaug
    t_ = np.arange(S)[:, None]
    blk = np.arange(32)[None, :]
    cur = t_ // 64
    forced = (blk == 0) | (blk == cur) | (blk == cur - 1)
    valid = blk * 64 <= t_
    vm = valid.astype(np.float32)
    add2 = np.where(valid, np.where(forced, np.float32(1e4), np.float32(0.0)), np.float32(NEG)).astype(np.float32)
    impm = np.stack([vm, add2], 0).reshape(2, 16, 128, 32).transpose(2, 0, 1, 3)
    sh["impm"] = np.ascontiguousarray(impm)
    sh["esel"] = (np.arange(S)[None, :] // 64 == np.arange(32)[:, None]).astype(np.float32)
    sh["tri"] = (np.arange(128)[None, :] >= np.arange(128)[:, None]).astype(np.float32)
    sh["bforget"] = np.asarray(inp["mix_b_forget"][0], np.float32).reshape(8, 1)
    sh["posT"] = np.ascontiguousarray(np.stack([np.asarray(inp["cmp_pos_k"][0], np.float32).T, np.asarray(inp["cmp_pos_v"][0], np.float32).T], 0))
    for nm in ("cmp_k_w1", "cmp_v_w1", "cmp_k_w2", "cmp_v_w2", "w_up_nsa", "w_up_fox", "mix_w_out", "mem_w_q", "mem_w_kv", "mem_w_o"):
        sh[nm] = sq(inp[nm][0])
    sh["w_in"] = sq(inp["mix_w_in"][0])
    for nm in ("ffn1", "ffn2"):
        sh[nm + "_w_gate"] = sq(inp[nm + "_w_gate"][0])
        sh[nm + "_w_up"] = sq(inp[nm + "_w_up"][0])
        sh[nm + "_w_down"] = sq(inp[nm + "_w_down"][0])
    return sh


def kernel(_stage=9, _ncores=8, _dbg=False, **inp):
    key = ("prog", _stage, _dbg)
    if key not in _CACHE:
        _CACHE[key] = build_program(_stage, _dbg)
    nc = _CACHE[key]
    sh = _prep_shared(inp)
    x = np.asarray(inp["x"], np.float32)
    mem = np.asarray(inp["mem"], np.float32)
    in_maps = []
    for b in range(_ncores):
        m = dict(sh)
        m["x"] = np.ascontiguousarray(x[b])
        m["mem"] = np.ascontiguousarray(mem[b])
        in_maps.append(m)
    res = run_bass_kernel_spmd(nc, in_maps, core_ids=list(range(_ncores)))
    out = np.stack([np.asarray(r["out"], np.float32) for r in res.results], axis=0)
    if _dbg:
        return out, res.results
    return out
```

```python
import math
from contextlib import ExitStack
import numpy as np
import ml_dtypes
import concourse.bass as bass
import concourse.mybir as mybir
from concourse.bass_utils import run_bass_kernel_spmd

F32 = mybir.dt.float32
BF16 = mybir.dt.bfloat16
AF = mybir.ActivationFunctionType
ALU = mybir.AluOpType
AX = mybir.AxisListType

S = 2048
D = 1024
KC = 8
DFF = 2816
NF = 22
FGROUPS = [(0, 6), (6, 6), (12, 5), (17, 5)]
INW = 4896
MEM = 256
NEG = -1.0e30
EPS = 1e-6


class Tok:
    __slots__ = ("sem", "val")

    def __init__(self, sem, val):
        self.sem = sem
        self.val = val


class Buf:
    __slots__ = ("lw", "rd", "excl")

    def __init__(self):
        self.lw = None
        self.rd = {}
        self.excl = False


class Eng:
    def __init__(self, h, sem, is_pe=False):
        self.h = h
        self.sem = sem
        self.count = 0
        self.seen = {}
        self.is_pe = is_pe
        self.dsems = []
        self.dcnt = []
        self.dnext = 0


class Tile:
    def __init__(self, t, nb=1, init=None):
        self.t = t
        self.b = [Buf() for _ in range(nb)]
        if init:
            for b in self.b:
                b.rd = dict(init)


class Phase(ExitStack):
    def __init__(self, kb):
        super().__init__()
        self.kb = kb
        self.tiles = []

    def __exit__(self, *a):
        ft = self.kb.free_toks
        for t in self.tiles:
            for b in t.b:
                for tok in [b.lw] + list(b.rd.values()):
                    if tok is None:
                        continue
                    k = id(tok.sem)
                    if k not in ft or ft[k].val < tok.val:
                        ft[k] = tok
        return super().__exit__(*a)


class KB:
    def __init__(self, nc, es):
        self.nc = nc
        self.es = es
        sem = lambda n: es.enter_context(nc.semaphore(n))
        self.pe = Eng(nc.tensor, sem("s_pe"), True)
        self.act = Eng(nc.scalar, sem("s_act"))
        self.dve = Eng(nc.vector, sem("s_dve"))
        self.pool = Eng(nc.gpsimd, sem("s_pool"))
        self.sp = Eng(nc.sync, sem("s_sp"))
        for q, nm, n in ((self.sp, "dsp", 16), (self.pool, "dpl", 24)):
            q.dsems = [sem(f"{nm}{i}") for i in range(n)]
            q.dcnt = [0] * n
        self.banks = [Tile(es.enter_context(nc.psum_tensor(f"pb{i}", [128, 512], F32))) for i in range(8)]
        for t in self.banks:
            t.b[0].excl = True
        self.nalloc = 0
        self.free_toks = {}

    def sb(self, es, shape, dtype, nb=1, name=None):
        self.nalloc += 1
        t = es.enter_context(self.nc.sbuf_tensor(f"{name or 't'}_{self.nalloc}", list(shape), dtype))
        tl = Tile(t, nb, self.free_toks)
        if hasattr(es, "tiles"):
            es.tiles.append(tl)
        return tl

    def wait(self, eng, tok, raw):
        if tok is None:
            return
        if tok.sem is eng.sem and eng.is_pe:
            return
        k = id(tok.sem)
        if eng.seen.get(k, 0) >= tok.val:
            return
        eng.h.wait_ge(tok.sem, tok.val)
        eng.seen[k] = tok.val

    def _deps(self, eng, rd, wr):
        for b in rd:
            self.wait(eng, b.lw, True)
            if b.excl:
                for t in b.rd.values():
                    if t.sem is not eng.sem:
                        self.wait(eng, t, False)
        for b in wr:
            self.wait(eng, b.lw, False)
            for t in b.rd.values():
                self.wait(eng, t, False)

    def _commit(self, tok, rd, wr):
        k = id(tok.sem)
        for b in rd:
            b.rd[k] = tok
        for b in wr:
            b.lw = tok
            b.rd = {}

    def op(self, eng, fn, rd=(), wr=()):
        self._deps(eng, rd, wr)
        inst = fn()
        eng.count += 1
        inst.then_inc(eng.sem, 1)
        tok = Tok(eng.sem, eng.count)
        self._commit(tok, rd, wr)
        return tok

    def dma(self, q, out, in_, rd=(), wr=()):
        self._deps(q, rd, wr)
        i = q.dnext
        q.dnext = (i + 1) % len(q.dsems)
        s = q.dsems[i]
        if q.dcnt[i] > 0:
            self.wait(q, Tok(s, 16 * q.dcnt[i]), True)
        inst = q.h.dma_start(out=out, in_=in_)
        inst.then_inc(s, 16)
        q.dcnt[i] += 1
        tok = Tok(s, 16 * q.dcnt[i])
        self._commit(tok, rd, wr)
        return tok

    def barrier(self, bufs):
        pass

    def mm(self, out, lhsT, rhs, start, stop, rd, wr):
        return self.op(self.pe, lambda: self.nc.tensor.matmul(out, lhsT=lhsT, rhs=rhs, start=start, stop=stop), rd, wr)

    def tr(self, out, in_, ident, rd, wr):
        return self.op(self.pe, lambda: self.nc.tensor.transpose(out, in_, ident), rd, wr)

    def actv(self, out, in_, func, rd, wr, bias=0.0, scale=1.0, accum_out=None):
        if accum_out is not None:
            return self.op(self.act, lambda: self.nc.scalar.activation(out=out, in_=in_, func=func, bias=bias, scale=scale, accum_out=accum_out), rd, wr)
        return self.op(self.act, lambda: self.nc.scalar.activation(out=out, in_=in_, func=func, bias=bias, scale=scale), rd, wr)

    def tt(self, out, in0, in1, op, rd, wr, eng=None):
        eng = eng or self.dve
        return self.op(eng, lambda: eng.h.tensor_tensor(out=out, in0=in0, in1=in1, op=op), rd, wr)

    def ts(self, out, in0, s1, s2, op0, op1, rd, wr, eng=None):
        eng = eng or self.dve
        if s2 is None:
            return self.op(eng, lambda: eng.h.tensor_scalar(out=out, in0=in0, scalar1=s1, scalar2=None, op0=op0), rd, wr)
        return self.op(eng, lambda: eng.h.tensor_scalar(out=out, in0=in0, scalar1=s1, scalar2=s2, op0=op0, op1=op1), rd, wr)

    def stt(self, out, in0, scalar, in1, op0, op1, rd, wr, eng=None):
        eng = eng or self.dve
        return self.op(eng, lambda: eng.h.scalar_tensor_tensor(out=out, in0=in0, scalar=scalar, in1=in1, op0=op0, op1=op1), rd, wr)

    def cp(self, out, in_, rd, wr, eng=None):
        eng = eng or self.dve
        if eng is self.act:
            return self.actv(out, in_, AF.Copy, rd, wr)
        return self.op(eng, lambda: eng.h.tensor_copy(out=out, in_=in_), rd, wr)

    def memset(self, ap, val, wr, eng=None):
        eng = eng or self.dve
        return self.op(eng, lambda: eng.h.memset(ap, val), (), wr)


class Ring:
    def __init__(self, kb, es, shape, dtype, n, name, nb=1):
        self.tiles = [kb.sb(es, shape, dtype, nb, f"{name}{i}") for i in range(n)]
        self.i = 0

    def next(self):
        t = self.tiles[self.i]
        self.i = (self.i + 1) % len(self.tiles)
        return t


def build_program(stage=9, dbg=False):
    nc = bass.Bass("TRN2", target_bir_lowering=False)
    dram = lambda n, shp, dt=F32, kind="ExternalInput": nc.dram_tensor(n, list(shp), dt, kind=kind).ap()
    x_d = dram("x", [S, D])
    mem_d = dram("mem", [MEM, D])
    gains_d = dram("gains", [128, 5, KC])
    gbc_d = dram("gbc", [128, 2, D])
    ident_d = dram("ident", [128, 128])
    w_d = {}
    for nm in ("ffn1", "ffn2"):
        w_d[nm + "_g"] = dram(nm + "_w_gate", [D, DFF])
        w_d[nm + "_u"] = dram(nm + "_w_up", [D, DFF])
        w_d[nm + "_d"] = dram(nm + "_w_down", [DFF, D])
    win_d = dram("w_in", [D, INW])
    bdiag_d = dram("bdiag", [8, 128, 256])
    bwf_d = dram("bwf", [8, 128, 128])
    tbl31_d = dram("tbl31", [128, 8])
    bcmp_d = dram("bcmp", [8, 4, 128, 512])
    ovaug_d = dram("ovaug", [128, 33])
    impm_d = dram("impm", [128, 2, 16, 32])
    esel_d = dram("esel", [32, S])
    tri_d = dram("tri", [128, 128])
    bfg_d = dram("bforget", [8, 1])
    posT_d = dram("posT", [2, 64, 32])
    w1_d = [dram("cmp_k_w1", [2048, 256]), dram("cmp_v_w1", [2048, 256])]
    w2_d = [dram("cmp_k_w2", [256, 64]), dram("cmp_v_w2", [256, 64])]
    wupa_d = dram("w_up_nsa", [512, D])
    wupb_d = dram("w_up_fox", [512, D])
    wout_d = dram("mix_w_out", [D, D])
    wq_d = dram("mem_w_q", [D, D])
    wkv_d = dram("mem_w_kv", [D, 2 * D])
    wo_d = dram("mem_w_o", [D, D])
    out_d = dram("out", [S, D], kind="ExternalOutput")

    es = ExitStack()
    dump_toks = []

    def dump(name, tile, ap, dt):
        if not dbg:
            return
        d_ = nc.dram_tensor("dbg_" + name, list(ap.shape), dt, kind="ExternalOutput").ap()
        dump_toks.append(kb.dma(kb.sp, d_, ap, tile.b, ()))

    with es:
        kb = KB(nc, es)
        PE, ACT, DVE, POOL, SP = kb.pe, kb.act, kb.dve, kb.pool, kb.sp
        bank = kb.banks

        ident = kb.sb(es, [128, 128], F32, name="ident")
        kb.dma(SP, ident.t[:], ident_d, (), ident.b)
        gains = kb.sb(es, [128, 5, KC], F32, name="gains")
        kb.dma(SP, gains.t[:], gains_d, (), gains.b)
        ones_bf = kb.sb(es, [128, 128], BF16, name="ones")
        kb.memset(ones_bf.t[:], 1.0, ones_bf.b)

        xT = kb.sb(es, [128, KC, S], F32, nb=KC * 4, name="xT")
        xb = lambda kc, tt: xT.b[kc * 4 + tt]

        with Phase(kb) as ph:
            xst = Ring(kb, ph, [128, D], F32, 4, "xst")
            k = 0
            for tb in range(16):
                xs = xst.next()
                kb.dma(SP, xs.t[:], x_d[tb * 128:(tb + 1) * 128, :], (), xs.b)
                for half in range(2):
                    pb = bank[6 + (k % 2)]
                    k += 1
                    for j in range(4):
                        kc = half * 4 + j
                        kb.tr(pb.t[:, j * 128:(j + 1) * 128], xs.t[:, kc * 128:(kc + 1) * 128], ident.t[:], xs.b + ident.b, pb.b)
                    dst = xT.t[:, half * 4:half * 4 + 4, tb * 128:(tb + 1) * 128]
                    src = pb.t[:].rearrange("p (j t) -> p j t", j=4)
                    wr = [xb(half * 4 + j, tb // 4) for j in range(4)]
                    kb.cp(dst, src, pb.b, wr, eng=(DVE if half == 0 else ACT))

        def rmsnorm_fm(ph, gi, hT, rstd=None):
            sqr = Ring(kb, ph, [128, KC, 512], BF16, 2, "sq", nb=2)
            rsr = None if rstd is not None else Ring(kb, ph, [128, 512], F32, 2, "rstd")
            for tt in range(4):
                cs = slice(tt * 512, (tt + 1) * 512)
                sq = sqr.next()
                kb.actv(sq.t[:, 0:5, :], xT.t[:, 0:5, cs], AF.Square, [xb(kc, tt) for kc in range(5)], [sq.b[0]])
                kb.tt(sq.t[:, 5:8, :], xT.t[:, 5:8, cs], xT.t[:, 5:8, cs], ALU.mult, [xb(kc, tt) for kc in range(5, 8)], [sq.b[1]], eng=POOL)
                pb = bank[6 + (tt % 2)]
                for kc in range(KC):
                    kb.mm(pb.t[:], ones_bf.t[:], sq.t[:, kc, :], kc == 0, kc == KC - 1, [sq.b[0 if kc < 5 else 1]] + ones_bf.b, pb.b)
                if rstd is not None:
                    rs_ap, rs_b = rstd.t[:, cs], [rstd.b[tt]]
                else:
                    rs = rsr.next()
                    rs_ap, rs_b = rs.t[:], rs.b
                kb.actv(rs_ap, pb.t[:], AF.Ln, pb.b, rs_b, bias=EPS, scale=1.0 / D)
                kb.actv(rs_ap, rs_ap, AF.Exp, rs_b, rs_b, scale=-0.5)
                rmsnorm_apply(gi, hT, tt, rs_ap, rs_b)

        def rmsnorm_apply(gi, hT, tt, rs_ap, rs_b):
            cs = slice(tt * 512, (tt + 1) * 512)
            for kc in range(KC):
                kb.stt(hT.t[:, kc, cs], xT.t[:, kc, cs], gains.t[:, gi, kc:kc + 1], rs_ap, ALU.mult, ALU.mult,
                       [xb(kc, tt)] + rs_b + gains.b, [hT.b[kc * 4 + tt]])

        def ffn_prefetch(ph, wg_d, wu_d, wd_d):
            pre = {}
            pre["wgr"] = Ring(kb, ph, [128, KC, 128], BF16, 3, "wg")
            pre["wur"] = Ring(kb, ph, [128, KC, 128], BF16, 3, "wu")
            pre["wdr"] = Ring(kb, ph, [128, 6, D], BF16, 2, "wd")
            wgv = wg_d.rearrange("(k p) f -> p k f", p=128)
            wuv = wu_d.rearrange("(k p) f -> p k f", p=128)
            f0, nf = FGROUPS[0]
            wd = pre["wdr"].next()
            kb.dma(POOL, wd.t[:, 0:nf, :], wd_d[f0 * 128:(f0 + nf) * 128, :].rearrange("(f p) d -> p f d", p=128), (), wd.b)
            pre["wd0"] = wd
            for f in range(3):
                wg = pre["wgr"].next()
                wu = pre["wur"].next()
                kb.dma(POOL, wg.t[:], wgv[:, :, f * 128:(f + 1) * 128], (), wg.b)
                kb.dma(POOL, wu.t[:], wuv[:, :, f * 128:(f + 1) * 128], (), wu.b)
                pre[f] = (wg, wu)
            return pre

        def ffn(ph, hT, wg_d, wu_d, wd_d, pre):
            wgr, wur, wdr = pre["wgr"], pre["wur"], pre["wdr"]
            aT = kb.sb(ph, [128, 6, S], BF16, nb=6 * 4, name="aT")
            sgr = Ring(kb, ph, [128, 512], F32, 2, "sg")
            wgv = wg_d.rearrange("(k p) f -> p k f", p=128)
            wuv = wu_d.rearrange("(k p) f -> p k f", p=128)
            nb = 0
            for gi_, (f0, nf) in enumerate(FGROUPS):
                if gi_ == 0:
                    wd = pre["wd0"]
                else:
                    wd = wdr.next()
                    kb.dma(POOL, wd.t[:, 0:nf, :], wd_d[f0 * 128:(f0 + nf) * 128, :].rearrange("(f p) d -> p f d", p=128), (), wd.b)
                for fi in range(nf):
                    f = f0 + fi
                    if f in pre:
                        wg, wu = pre[f]
                    else:
                        wg = wgr.next()
                        wu = wur.next()
                        kb.dma(POOL, wg.t[:], wgv[:, :, f * 128:(f + 1) * 128], (), wg.b)
                        kb.dma(POOL, wu.t[:], wuv[:, :, f * 128:(f + 1) * 128], (), wu.b)
                    for tt in range(4):
                        cs = slice(tt * 512, (tt + 1) * 512)
                        pg = bank[(nb % 2) * 2]
                        pu = bank[(nb % 2) * 2 + 1]
                        nb += 1
                        for kc in range(KC):
                            kb.mm(pg.t[:], wg.t[:, kc, :], hT.t[:, kc, cs], kc == 0, kc == KC - 1, wg.b + [hT.b[kc * 4 + tt]], pg.b)
                        for kc in range(KC):
                            kb.mm(pu.t[:], wu.t[:, kc, :], hT.t[:, kc, cs], kc == 0, kc == KC - 1, wu.b + [hT.b[kc * 4 + tt]], pu.b)
                        sg = sgr.next()
                        kb.actv(sg.t[:], pg.t[:], AF.Silu, pg.b, sg.b)
                        kb.tt(aT.t[:, fi, cs], sg.t[:], pu.t[:], ALU.mult, sg.b + pu.b, [aT.b[fi * 4 + tt]])
                k = 0
                for dj in range(KC):
                    for tt in range(4):
                        cs = slice(tt * 512, (tt + 1) * 512)
                        pb = bank[4 + (k % 2)]
                        k += 1
                        for fi in range(nf):
                            kb.mm(pb.t[:], wd.t[:, fi, dj * 128:(dj + 1) * 128], aT.t[:, fi, cs], fi == 0, fi == nf - 1,
                                  wd.b + [aT.b[fi * 4 + tt]], pb.b)
                        kb.stt(xT.t[:, dj, cs], pb.t[:], 0.5, xT.t[:, dj, cs], ALU.mult, ALU.add,
                               pb.b + [xb(dj, tt)], [xb(dj, tt)])

        if stage >= 1:
            with Phase(kb) as ph:
                hT = kb.sb(ph, [128, KC, S], BF16, nb=KC * 4, name="hT")
                pre = ffn_prefetch(ph, w_d["ffn1_g"], w_d["ffn1_u"], w_d["ffn1_d"])
                rmsnorm_fm(ph, 0, hT)
                dump("hT", hT, hT.t[:], BF16)
                ffn(ph, hT, w_d["ffn1_g"], w_d["ffn1_u"], w_d["ffn1_d"], pre)
                dump("xT1", xT, xT.t[:], F32)


        winv = win_d.rearrange("(k p) f -> p k f", p=128)
        evk = [0]

        def evac(out, in_, rd, wr, scale=None):
            evk[0] += 1
            if evk[0] % 2 == 0:
                return kb.actv(out, in_, AF.Copy, rd, wr, scale=(1.0 if scale is None else scale))
            if scale is None:
                return kb.cp(out, in_, rd, wr)
            return kb.ts(out, in_, scale, None, ALU.mult, None, rd, wr)

        def wload(ring, c0, n, dup=False):
            w = ring.next()
            kb.dma(POOL, w.t[:, :, 0:n], winv[:, :, c0:c0 + n], (), w.b)
            if dup:
                kb.dma(POOL, w.t[:, :, n:2 * n], winv[:, :, c0:c0 + n], (), w.b)
            return w

        pjk = [0]

        def proj_fm(w, M, hT, fn):
            for tt in range(4):
                cs = slice(tt * 512, (tt + 1) * 512)
                pb = bank[6 + (pjk[0] % 2)]
                pjk[0] += 1
                for kc in range(KC):
                    kb.mm(pb.t[0:M, :], w.t[:, kc, 0:M], hT.t[:, kc, cs], kc == 0, kc == KC - 1, w.b + [hT.b[kc * 4 + tt]], pb.b)
                fn(tt, cs, pb)

        def proj_tm(w, N, hT, fn):
            for tb in range(16):
                pb = bank[6 + (pjk[0] % 2)]
                pjk[0] += 1
                for kc in range(KC):
                    kb.mm(pb.t[:, 0:N], hT.t[:, kc, tb * 128:(tb + 1) * 128], w.t[:, kc, 0:N], kc == 0, kc == KC - 1,
                          w.b + [hT.b[kc * 4 + tb // 4]], pb.b)
                fn(tb, pb)

        accb = [[Buf(), Buf()] for _ in range(4)]
        sk = [0]
        apar = [0]

        class Pipe:
            def __init__(self):
                self.q = []
                self.step = 0
                self.seq = 0

            def add(self, delay, fn, prio=1):
                self.q.append((self.step + delay, prio, self.seq, fn))
                self.seq += 1

            def tick(self):
                self.step += 1
                due = sorted([x for x in self.q if x[0] <= self.step])
                self.q = [x for x in self.q if x[0] > self.step]
                for x in due:
                    x[3]()

            def flush(self):
                while self.q:
                    self.tick()

        pipe = Pipe()
        NS = 2
        acck = [0]

        def attend(pr, chunks_fn, qk_fn, seg_fn, v_fn, fin_fn, krows, nv, tmpr):
            packed = 4 * nv <= 512
            ns = 4 if packed else 2
            for qt in range(4):
                chs = chunks_fn(qt)
                if not chs:
                    continue
                abanks = []
                if packed:
                    ab_ = bank[4 + acck[0] % 3]
                    acck[0] += 1
                    abanks = [ab_] * 4
                else:
                    for qb in range(4):
                        abanks.append(bank[2 + acck[0] % 5])
                        acck[0] += 1
                contrib = {qb: [i for i, (kc, lo, hi) in enumerate(chs) if lo <= qb * 128 and (qb + 1) * 128 <= hi] for qb in range(4)}
                lastc = max(max(v_) for v_ in contrib.values() if v_)
                started = [False]
                for i, (kc, lo, hi) in enumerate(chs):
                    pipe.tick()
                    sbk = bank[sk[0] % ns]
                    sk[0] += 1
                    qk_fn(sbk, kc, qt, lo, hi)
                    p = pr.next()

                    def stage_b(sbk=sbk, p=p, kc=kc, qt=qt, lo=lo, hi=hi):
                        for seg in seg_fn(kc, qt, lo, hi):
                            kind, c0, c1 = seg[0], seg[1], seg[2]
                            pbs = p.b[c0 // 128:(c1 + 127) // 128]
                            if kind == "const":
                                kb.actv(p.t[0:krows, c0:c1], sbk.t[0:krows, c0:c1], AF.Exp, sbk.b + seg[4], pbs, bias=seg[3])
                            elif kind == "tile":
                                tm = tmpr.next()
                                kb.tt(tm.t[0:krows, 0:c1 - c0], sbk.t[0:krows, c0:c1], seg[3], ALU.add, sbk.b + seg[4], tm.b)
                                kb.actv(p.t[0:krows, c0:c1], tm.t[0:krows, 0:c1 - c0], AF.Exp, tm.b, pbs)
                            elif kind == "mul":
                                kb.tt(p.t[0:krows, c0:c1], p.t[0:krows, c0:c1], seg[3], ALU.mult, pbs + seg[4], pbs)

                    def stage_c(i=i, kc=kc, p=p, qt=qt, contrib=contrib, abanks=abanks, started=started, lastc=lastc):
                        for qb in range(4):
                            if i in contrib[qb]:
                                first = contrib[qb][0] == i
                                last = contrib[qb][-1] == i
                                ab = abanks[qb]
                                v_ap, v_b = v_fn(kc)
                                if packed:
                                    reg = ab.t[:, qb * nv:(qb + 1) * nv]
                                    st = not started[0]
                                    started[0] = True
                                    kb.op(PE, lambda: nc.tensor.matmul(reg, lhsT=p.t[0:krows, qb * 128:(qb + 1) * 128], rhs=v_ap, start=st,
                                                                       stop=(i == lastc and last), skip_group_check=True),
                                          [p.b[qb]] + v_b, ab.b)
                                else:
                                    reg = ab.t[:, 0:nv]
                                    kb.mm(reg, p.t[0:krows, qb * 128:(qb + 1) * 128], v_ap, first, last, [p.b[qb]] + v_b, ab.b)
                                    if last:
                                        pipe.add(1, lambda qt=qt, qb=qb, reg=reg, ab=ab: fin_fn(qt, qb, reg, ab.b), prio=0)
                        if packed and i == lastc:
                            ab = abanks[0]
                            pipe.add(1, lambda qt=qt, ab=ab: fin_fn(qt, None, ab.t[:, 0:4 * nv].rearrange("p (q v) -> p q v", q=4), ab.b), prio=0)

                    pipe.add(1, stage_b, prio=1)
                    pipe.add(6, stage_c, prio=2)

        def transpose_o(oacc, dst, j):
            for t4 in range(4):
                pb = bank[6 + (pjk[0] % 2)]
                pjk[0] += 1
                for k_ in range(4):
                    tb = t4 * 4 + k_
                    kb.tr(pb.t[:, k_ * 128:(k_ + 1) * 128], oacc.t[:, tb, :, :].rearrange("p a b -> p (a b)"), ident.t[:],
                          [oacc.b[tb]] + ident.b, pb.b)
                evac(dst.t[:, j, t4 * 512:(t4 + 1) * 512], pb.t[:], pb.b, [dst.b[j]])

        def mixer():
            with Phase(kb) as pm:
                qno = kb.sb(pm, [128, 4, S], BF16, nb=4, name="qno")
                qfo = kb.sb(pm, [128, 4, S], BF16, nb=4, name="qfo")
                rstd_mix = kb.sb(pm, [128, S], F32, nb=4, name="rstdmix")
                tbl31 = kb.sb(pm, [128, 8], F32, name="tbl31")
                kb.dma(SP, tbl31.t[:], tbl31_d, (), tbl31.b)
                tri = kb.sb(pm, [128, 128], BF16, name="tri")
                kb.dma(POOL, tri.t[:], tri_d, (), tri.b)

                if stage >= 2:
                  with Phase(kb) as pn:
                    ksa = [kb.sb(pn, [128, S], BF16, name="ksa") for _ in range(2)]
                    kwT = [kb.sb(pn, [128, S], BF16, name="kwT") for _ in range(2)]
                    vs_tm = kb.sb(pn, [128, 16, 2, 65], BF16, nb=16, name="vs")
                    vw_tm = kb.sb(pn, [128, 16, 2, 65], BF16, nb=16, name="vw")
                    kcmpT = kb.sb(pn, [128, 2, 128], BF16, nb=2, name="kcmp")
                    vc_tm = kb.sb(pn, [128, 2, 65], BF16, nb=2, name="vc")
                    gs = kb.sb(pn, [128, 16, 24], F32, nb=16, name="gs")
                    ovaug = kb.sb(pn, [128, 33], BF16, name="ovaug")
                    kb.dma(POOL, ovaug.t[:], ovaug_d, (), ovaug.b)
                    kb.memset(vs_tm.t[:, :, :, 64:65], 1.0, vs_tm.b)
                    kb.memset(vw_tm.t[:, :, :, 64:65], 1.0, vw_tm.b)
                    kb.memset(vc_tm.t[:, :, 64:65], 1.0, vc_tm.b)
                    for g in range(2):
                        kb.memset(ksa[g].t[64:128, :], 0.0, ksa[g].b)
                        kb.memset(kwT[g].t[64:128, :], 0.0, kwT[g].b)
                        kb.dma(POOL, ksa[g].t[64:96, :], esel_d, (), ksa[g].b)
                    with Phase(kb) as pa:
                        h2T = kb.sb(pa, [128, KC, S], BF16, nb=KC * 4, name="h2T")
                        with Phase(kb) as pr_:
                            rmsnorm_fm(pr_, 1, h2T, rstd=rstd_mix)
                        wr = Ring(kb, pa, [128, KC, 128], BF16, 3, "wi")
                        kcT = kb.sb(pa, [128, 16, 128], BF16, name="kcT")
                        vcT = kb.sb(pa, [128, 16, 128], BF16, name="vcT")
                        for j in range(4):
                            w = wload(wr, 128 * j, 128)
                            proj_fm(w, 128, h2T, lambda tt, cs, pb, j=j: evac(qno.t[:, j, cs], pb.t[:], pb.b, [qno.b[j]], scale=0.125))
                        for g in range(2):
                            w = wload(wr, 768 + 64 * g, 64)
                            proj_fm(w, 64, h2T, lambda tt, cs, pb, g=g: evac(ksa[g].t[0:64, cs], pb.t[0:64, :], pb.b, ksa[g].b))
                            w = wload(wr, 1024 + 64 * g, 64)
                            proj_fm(w, 64, h2T, lambda tt, cs, pb, g=g: evac(kwT[g].t[0:64, cs], pb.t[0:64, :], pb.b, kwT[g].b))
                        w = wload(wr, 512, 128)
                        proj_fm(w, 128, h2T, lambda tt, cs, pb: evac(kcT.t[:, :, tt * 32:(tt + 1) * 32], pb.t[:].rearrange("p (c r) -> p r c", r=16), pb.b, kcT.b))
                        w = wload(wr, 640, 128)
                        proj_fm(w, 128, h2T, lambda tt, cs, pb: evac(vcT.t[:, :, tt * 32:(tt + 1) * 32], pb.t[:].rearrange("p (c r) -> p r c", r=16), pb.b, vcT.b))
                        w = wload(wr, 896, 128)
                        proj_tm(w, 128, h2T, lambda tb, pb: evac(vs_tm.t[:, tb, :, 0:64], pb.t[:, 0:128].rearrange("p (g d) -> p g d", g=2), pb.b, [vs_tm.b[tb]]))
                        w = wload(wr, 1152, 128)
                        proj_tm(w, 128, h2T, lambda tb, pb: evac(vw_tm.t[:, tb, :, 0:64], pb.t[:, 0:128].rearrange("p (g d) -> p g d", g=2), pb.b, [vw_tm.b[tb]]))
                        w = wload(wr, 1280, 24)
                        proj_tm(w, 24, h2T, lambda tb, pb: kb.actv(gs.t[:, tb, :], pb.t[:, 0:24], AF.Sigmoid, pb.b, [gs.b[tb]]))
                        with Phase(kb) as pc:
                            w1 = kb.sb(pc, [128, 32, 256], BF16, name="w1")
                            w2 = kb.sb(pc, [128, 2, 128], BF16, name="w2")
                            posT = kb.sb(pc, [128, 32], BF16, name="posT")
                            hid = kb.sb(pc, [128, 2, 128], BF16, nb=2, name="hid")
                            bsb = kb.sb(pc, [128, 1], F32, name="bsb")
                            zr = Ring(kb, pc, [128, 4, 128], F32, 2, "z")
                            for kv in range(2):
                                src = kcT if kv == 0 else vcT
                                w1v = w1_d[kv].rearrange("(i d) h -> d i h", d=64)
                                kb.dma(POOL, w1.t[0:64], w1v, (), w1.b)
                                kb.dma(POOL, w1.t[64:128], w1v, (), w1.b)
                                w2v = w2_d[kv].rearrange("(c p) d -> p c d", p=128)
                                kb.dma(POOL, w2.t[:, :, 0:64], w2v, (), w2.b)
                                kb.dma(POOL, w2.t[:, :, 64:128], w2v, (), w2.b)
                                kb.dma(POOL, posT.t[0:64, :], posT_d[kv], (), posT.b)
                                kb.dma(POOL, posT.t[64:128, :], posT_d[kv], (), posT.b)
                                for g in range(2):
                                    r0 = g * 64
                                    for hc in range(2):
                                        hs = slice(hc * 128, (hc + 1) * 128)
                                        pbb = bank[7]
                                        for i in range(32):
                                            kb.mm(pbb.t[:, 0:1], w1.t[r0:r0 + 64, i, hs], posT.t[r0:r0 + 64, i:i + 1], i == 0, i == 31,
                                                  w1.b + posT.b, pbb.b)
                                        kb.cp(bsb.t[:], pbb.t[:, 0:1], pbb.b, bsb.b)
                                        pbh = bank[6]
                                        for i in range(32):
                                            kb.mm(pbh.t[:, 0:127], w1.t[r0:r0 + 64, i, hs], src.t[r0:r0 + 64, i % 16, i // 16:i // 16 + 127], i == 0, i == 31,
                                                  w1.b + src.b, pbh.b)
                                        z = zr.next()
                                        kb.actv(z.t[:, 0, 0:127], pbh.t[:, 0:127], AF.Identity, pbh.b + bsb.b, z.b, bias=bsb.t[:, 0:1])
                                        kb.tt(z.t[:, 1, 0:127], z.t[:, 0, 0:127], z.t[:, 0, 0:127], ALU.mult, z.b, z.b)
                                        kb.ts(z.t[:, 1, 0:127], z.t[:, 1, 0:127], 0.044715, 1.0, ALU.mult, ALU.add, z.b, z.b)
                                        kb.tt(z.t[:, 2, 0:127], z.t[:, 1, 0:127], z.t[:, 0, 0:127], ALU.mult, z.b, z.b)
                                        kb.actv(z.t[:, 3, 0:127], z.t[:, 2, 0:127], AF.Sigmoid, z.b, z.b, scale=1.5957691216057308)
                                        kb.tt(hid.t[:, hc, 0:127], z.t[:, 3, 0:127], z.t[:, 0, 0:127], ALU.mult, z.b, [hid.b[hc]])
                                    pbo = bank[7]
                                    if kv == 0:
                                        for hc in range(2):
                                            kb.mm(pbo.t[:, 0:127], w2.t[:, hc, :], hid.t[:, hc, 0:127], hc == 0, hc == 1, w2.b + [hid.b[hc]], pbo.b)
                                        evac(kcmpT.t[:, g, 0:127], pbo.t[:, 0:127], pbo.b, [kcmpT.b[g]])
                                    else:
                                        for hc in range(2):
                                            kb.mm(pbo.t[0:127, 0:64], hid.t[:, hc, 0:127], w2.t[:, hc, 0:64], hc == 0, hc == 1, w2.b + [hid.b[hc]], pbo.b)
                                        evac(vc_tm.t[0:127, g, 0:64], pbo.t[0:127, 0:64], pbo.b, [vc_tm.b[g]])
                        if dbg:
                            dump("qno", qno, qno.t[:], BF16)
                            dump("kcmpT", kcmpT, kcmpT.t[:, :, 0:127], BF16)
                            dump("vc_tm", vc_tm, vc_tm.t[0:127], BF16)
                            dump("gs", gs, gs.t[:], F32)
                            dump("ksa0", ksa[0], ksa[0].t[:], BF16)
                            dump("vs_tm", vs_tm, vs_tm.t[:], BF16)
                    with Phase(kb) as pb_:
                      if stage >= 2.2:
                        qsa_g = [Ring(kb, pb_, [128, S], BF16, 2, "qsa"), Ring(kb, pb_, [128, S], BF16, 2, "qsb")]
                        for r_ in qsa_g:
                            for t_ in r_.tiles:
                                kb.memset(t_.t[64:128, :], 0.0, t_.b)
                        oacc = kb.sb(pb_, [128, 16, 2, 64], F32, nb=16, name="oacc")
                        impacc_g = [kb.sb(pb_, [128, 16, 32], F32, nb=16, name="impacc") for _ in range(2)]
                        impm = kb.sb(pb_, [128, 2, 16, 32], F32, name="impm")
                        kb.dma(SP, impm.t[:], impm_d, (), impm.b)
                        pr = Ring(kb, pb_, [128, 512], BF16, 12, "P", nb=4)
                        tmpr = Ring(kb, pb_, [128, 512], F32, 3, "tmp")
                        bdr = Ring(kb, pb_, [128, 256], F32, 2, "bd")
                        bwr = Ring(kb, pb_, [128, 128], F32, 2, "bw")
                        bcr = Ring(kb, pb_, [128, 512], F32, 2, "bc")
                        smr = Ring(kb, pb_, [128, 8], F32, 4, "sm")
                        o65 = Ring(kb, pb_, [128, 64], F32, 2, "o65")
                        m8r = Ring(kb, pb_, [128, 16], F32, 2, "m8")
                        ost = kb.sb(pb_, [128, 16, 3, 65], F32, nb=16, name="ost")
                        impst = kb.sb(pb_, [128, 16, 33], F32, nb=16, name="impst")
                        rdr = Ring(kb, pb_, [128, 16, 4], F32, 2, "rd")
                        vv = Ring(kb, pb_, [128, 4, 32], F32, 2, "vv")

                        def cmp_chunks(qt):
                            return [(0, 0, 512)]

                        def causal_chunks(qt):
                            return [(kc, max(0, 128 * kc - 512 * qt), 512) for kc in range(4 * qt + 4)]

                        wcnt = [0]

                        def win_chunks(qt):
                            out = []
                            for kc in range(max(0, 4 * qt - 4), 4 * qt + 4):
                                off = 512 * qt - 128 * kc
                                out.append((kc, max(0, -off), min(512, 640 - off)))
                            import os
                            wc = os.environ.get("WCUT", "")
                            if wc == "diag":
                                out = out[:1]
                            elif wc == "upper":
                                out = [o for o in out if o[0] >= 4 * qt]
                            elif wc == "lower":
                                out = [o for o in out if o[0] <= 4 * qt]
                            elif wc.startswith("n"):
                                n_ = int(wc[1:])
                                out = [o for o in out if o[0] <= 4 * qt]
                                keep = []
                                for o in out:
                                    if wcnt[0] < n_:
                                        keep.append(o)
                                        wcnt[0] += 1
                                out = keep
                            elif wc.startswith("off"):
                                k_ = int(wc[3:]) // 128
                                out = [o for o in out if o[0] == 4 * qt or o[0] == 4 * qt - k_]
                            return out

                        def group_part(g, part):
                            impacc = impacc_g[g]
                            qsa = qsa_g[g]
                            if part == 1:
                                for hj in range(4):
                                    h = g * 4 + hj
                                    j, half = h // 2, h % 2
                                    r0 = half * 64
                                    bct = {}

                                    def qk_cmp(sbk, kc, qt, lo, hi, g=g, j=j, r0=r0):
                                        kb.mm(sbk.t[0:127, 0:512], kcmpT.t[r0:r0 + 64, g, 0:127], qno.t[r0:r0 + 64, j, qt * 512:(qt + 1) * 512], True, True,
                                              [kcmpT.b[g], qno.b[j]], sbk.b)

                                    def seg_cmp(kc, qt, lo, hi, h=h):
                                        bc = bcr.next()
                                        kb.dma(SP, bc.t[:], bcmp_d[h, qt], (), bc.b)
                                        return [("tile", 0, 512, bc.t[0:127, :], bc.b)]

                                    def fin_imp(qt, qb, reg, rb, hj=hj):
                                        kb.cp(impst.t[:, 4 * qt:4 * qt + 4, :], reg, rb, impst.b[4 * qt:4 * qt + 4])

                                    attend(pr, cmp_chunks, qk_cmp, seg_cmp, lambda kc: (ovaug.t[0:127, 0:33], ovaug.b), fin_imp, 127, 33, tmpr)

                                    def norm_imp(hj=hj):
                                        rd = rdr.next()
                                        kb.ts(rd.t[:, :, 0:1], impst.t[:, :, 32:33], 1e-30, None, ALU.max, None, impst.b, rd.b)
                                        kb.op(DVE, lambda: nc.vector.reciprocal(out=rd.t[:, :, 0:1], in_=rd.t[:, :, 0:1]), rd.b, rd.b)
                                        rb_ = rd.t[:, :, 0:1].to_broadcast([128, 16, 32])
                                        if hj == 0:
                                            kb.tt(impacc.t[:], impst.t[:, :, 0:32], rb_, ALU.mult, impst.b + rd.b, impacc.b)
                                        else:
                                            kb.tt(impst.t[:, :, 0:32], impst.t[:, :, 0:32], rb_, ALU.mult, impst.b + rd.b, impst.b)
                                            kb.tt(impacc.t[:], impacc.t[:], impst.t[:, :, 0:32], ALU.add, impst.b + impacc.b, impacc.b)

                                    pipe.add(8, norm_imp, prio=-1)
                                pipe.flush()
                                return
                            if part == 2:
                                if stage < 2.22:
                                    return
                                for tb in range(16):
                                    def mask_tb(g=g, tb=tb):
                                        v = vv.next()
                                        kb.tt(v.t[:, 1, :], impacc_g[g].t[:, tb, :], impm.t[:, 0, tb, :], ALU.mult, [impacc_g[g].b[tb]] + impm.b, v.b)
                                        kb.tt(v.t[:, 2, :], v.t[:, 1, :], impm.t[:, 1, tb, :], ALU.add, v.b + impm.b, v.b)
                                        m8 = m8r.next()
                                        kb.op(DVE, lambda: nc.vector.max(out=m8.t[:, 0:8], in_=v.t[:, 2, :]), v.b, m8.b)
                                        kb.op(DVE, lambda: nc.vector.match_replace(out=v.t[:, 3, :], in_to_replace=m8.t[:, 0:8], in_values=v.t[:, 2, :],
                                                                                   imm_value=-3.0e38), v.b + m8.b, v.b)
                                        kb.op(DVE, lambda: nc.vector.max(out=m8.t[:, 8:16], in_=v.t[:, 3, :]), v.b, m8.b)
                                        kb.ts(v.t[:, 0, :], v.t[:, 2, :], m8.t[:, 15:16], -30000.0, ALU.is_lt, ALU.mult, v.b + m8.b, v.b)
                                        pbt = bank[7]
                                        kb.tr(pbt.t[:, 0:128], v.t[:].rearrange("p a b -> p (a b)"), ident.t[:], v.b + ident.b, pbt.b)
                                        for t_ in qsa_g[g].tiles:
                                            kb.cp(t_.t[64:96, tb * 128:(tb + 1) * 128], pbt.t[0:32, 0:128], pbt.b, t_.b)
                                    pipe.add(1 + tb * (1 if g == 0 else 10), mask_tb, prio=3)
                                return
                            if stage < 2.23:
                                return
                            def prep_nsa(h):
                                bd = bdr.next()
                                kb.dma(SP, bd.t[:], bdiag_d[h], (), bd.b)
                                bw = bwr.next()
                                kb.dma(SP, bw.t[:], bwf_d[h], (), bw.b)
                                qa = qsa.next()
                                kb.cp(qa.t[0:64, :], qno.t[(h % 2) * 64:(h % 2) * 64 + 64, h // 2, :], [qno.b[h // 2]], qa.b)
                                return bd, bw, qa

                            preps = {g * 4: prep_nsa(g * 4)}
                            for pj in range(2):
                                j = g * 2 + pj
                                for half in range(2):
                                    h = 2 * j + half
                                    r0 = half * 64
                                    bd, bw, qa = preps.pop(h)
                                    if h + 1 < g * 4 + 4 and half == 0:
                                        preps[h + 1] = prep_nsa(h + 1)

                                    def mkfin(br, first, h=h, half=half):
                                        def fin(qt, qb, reg, rb):
                                            kb.cp(ost.t[:, 4 * qt:4 * qt + 4, br, :], reg, rb, ost.b[4 * qt:4 * qt + 4])
                                        return fin

                                    def norm_nsa(h=h, half=half):
                                        rd = rdr.next()
                                        kb.ts(rd.t[:, :, 0:3], ost.t[:, :, :, 64], 1e-30, None, ALU.max, None, ost.b, rd.b)
                                        kb.op(DVE, lambda: nc.vector.reciprocal(out=rd.t[:, :, 0:3], in_=rd.t[:, :, 0:3]), rd.b, rd.b)
                                        kb.tt(rd.t[:, :, 0:3], rd.t[:, :, 0:3], gs.t[:, :, h:h + 17:8], ALU.mult, rd.b + gs.b, rd.b)
                                        dst = oacc.t[:, :, half, :]
                                        for br in range(3):
                                            cb = rd.t[:, :, br:br + 1].to_broadcast([128, 16, 64])
                                            if br == 0:
                                                kb.tt(dst, ost.t[:, :, 0, 0:64], cb, ALU.mult, ost.b + rd.b, oacc.b)
                                            else:
                                                kb.tt(ost.t[:, :, br, 0:64], ost.t[:, :, br, 0:64], cb, ALU.mult, ost.b + rd.b, ost.b)
                                                kb.tt(dst, dst, ost.t[:, :, br, 0:64], ALU.add, ost.b + oacc.b, oacc.b)

                                    def qk_cmp(sbk, kc, qt, lo, hi, g=g, j=j, r0=r0):
                                        kb.mm(sbk.t[0:127, 0:512], kcmpT.t[r0:r0 + 64, g, 0:127], qno.t[r0:r0 + 64, j, qt * 512:(qt + 1) * 512], True, True,
                                              [kcmpT.b[g], qno.b[j]], sbk.b)

                                    def seg_cmp(kc, qt, lo, hi, h=h):
                                        bc = bcr.next()
                                        kb.dma(SP, bc.t[:], bcmp_d[h, qt], (), bc.b)
                                        return [("tile", 0, 512, bc.t[0:127, :], bc.b)]

                                    attend(pr, cmp_chunks, qk_cmp, seg_cmp, lambda kc, g=g: (vc_tm.t[0:127, g, :], [vc_tm.b[g]]), mkfin(0, True), 127, 65, tmpr)

                                    def qk_sel(sbk, kc, qt, lo, hi, g=g, qa=qa):
                                        kb.mm(sbk.t[:, lo:hi], ksa[g].t[:, kc * 128:(kc + 1) * 128], qa.t[:, qt * 512 + lo:qt * 512 + hi], True, True,
                                              ksa[g].b + qa.b, sbk.b)

                                    def seg_diag(kc, qt, lo, hi, h=h, bd=bd, bw=bw):
                                        off = 512 * qt - 128 * kc
                                        segs = []
                                        for (m0, m1, kind) in ((0, 256, "bd"), (256, 512, "c"), (512, 640, "bw")):
                                            c0, c1 = max(lo, m0 - off), min(hi, m1 - off)
                                            if c1 <= c0:
                                                continue
                                            if kind == "bd":
                                                segs.append(("tile", c0, c1, bd.t[:, c0 + off:c1 + off], bd.b))
                                            elif kind == "c":
                                                segs.append(("const", c0, c1, tbl31.t[:, h:h + 1], tbl31.b))
                                            else:
                                                segs.append(("tile", c0, c1, bw.t[:, c0 + off - 512:c1 + off - 512], bw.b))
                                        return segs

                                    def seg_sel(kc, qt, lo, hi, h=h, bd=bd):
                                        off = 512 * qt - 128 * kc
                                        segs = []
                                        c0, c1 = max(lo, -off), min(hi, 256 - off)
                                        if c1 > c0:
                                            segs.append(("tile", c0, c1, bd.t[:, c0 + off:c1 + off], bd.b))
                                        c0 = max(lo, 256 - off)
                                        if hi > c0:
                                            segs.append(("const", c0, hi, tbl31.t[:, h:h + 1], tbl31.b))
                                        return segs

                                    if stage >= 2.24:
                                      attend(pr, causal_chunks, qk_sel, seg_sel, lambda kc, g=g: (vs_tm.t[:, kc, g, :], [vs_tm.b[kc]]), mkfin(1, False), 128, 65, tmpr)

                                    def qk_win(sbk, kc, qt, lo, hi, g=g, qa=qa):
                                        kb.mm(sbk.t[:, lo:hi], kwT[g].t[:, kc * 128:(kc + 1) * 128], qa.t[:, qt * 512 + lo:qt * 512 + hi], True, True,
                                              kwT[g].b + qa.b, sbk.b)

                                    if stage >= 2.25:
                                      attend(pr, win_chunks, qk_win, seg_diag, lambda kc, g=g: (vw_tm.t[:, kc, g, :], [vw_tm.b[kc]]), mkfin(2, False), 128, 65, tmpr)
                                    pipe.add(8, norm_nsa, prio=-1)
                                if dbg and j == 0:
                                    pipe.flush()
                                    dump("oacc0", oacc, oacc.t[:], F32)
                                if pj == 0:
                                    preps[2 * j + 2] = prep_nsa(2 * j + 2)
                                pipe.flush()
                                if stage >= 2.26:
                                    transpose_o(oacc, qno, j)


                        group_part(0, 1)
                        group_part(0, 2)
                        group_part(1, 1)
                        group_part(1, 2)
                        group_part(0, 3)
                        pipe.flush()
                        group_part(1, 3)
                        pipe.flush()
                if stage >= 2.3:
                  with Phase(kb) as pf:
                    kfT = kb.sb(pf, [128, 4, S], BF16, nb=4, name="kfT")
                    vf_tm = kb.sb(pf, [128, 16, 8, 65], BF16, nb=16, name="vf")
                    negc_tm = kb.sb(pf, [128, 16, 8], F32, name="negc")
                    c3 = kb.sb(pf, [8, 3, S], BF16, name="c3")
                    kb.memset(vf_tm.t[:, :, :, 64:65], 1.0, vf_tm.b)
                    with Phase(kb) as pa:
                        h2T = kb.sb(pa, [128, KC, S], BF16, nb=KC * 4, name="h2Tf")
                        for tt_ in range(4):
                            rmsnorm_apply(1, h2T, tt_, rstd_mix.t[:, tt_ * 512:(tt_ + 1) * 512], [rstd_mix.b[tt_]])
                        wr = Ring(kb, pa, [128, KC, 128], BF16, 3, "wi")
                        w = wload(wr, 2840, 8)
                        bfg = kb.sb(pa, [8, 2], F32, name="bfg")
                        kb.dma(SP, bfg.t[:, 0:1], bfg_d, (), bfg.b)
                        kb.ts(bfg.t[:, 1:2], bfg.t[:, 0:1], -1.0, None, ALU.mult, None, bfg.b, bfg.b)
                        ones8 = kb.sb(pa, [8, 512], F32, name="ones8")
                        kb.memset(ones8.t[:], 1.0, ones8.b)
                        etr = Ring(kb, pa, [8, 512], F32, 2, "et")
                        ncr = Ring(kb, pa, [8, 512], F32, 2, "nct")
                        r1r = Ring(kb, pa, [8, 512], F32, 2, "r1")
                        prev = [None]

                        def fgate(tt, cs, pb):
                            e = etr.next()
                            kb.actv(e.t[:], pb.t[0:8, :], AF.Exp, pb.b + bfg.b, e.b, bias=bfg.t[:, 1:2], scale=-1.0)
                            kb.actv(e.t[:], e.t[:], AF.Ln, e.b, e.b, bias=1.0)
                            nct = ncr.next()
                            init = 0.0 if prev[0] is None else prev[0].t[:, 511:512]
                            rdp = [] if prev[0] is None else prev[0].b
                            kb.op(DVE, lambda: nc.vector.tensor_tensor_scan(out=nct.t[:], data0=ones8.t[:], data1=e.t[:], initial=init,
                                                                            op0=ALU.mult, op1=ALU.add), e.b + ones8.b + rdp, nct.b)
                            prev[0] = nct
                            r1 = r1r.next()
                            kb.ts(c3.t[:, 0, cs], nct.t[:], -1.0, None, ALU.mult, None, nct.b, c3.b)
                            kb.stt(r1.t[:], nct.t[:], -1.0, c3.t[:, 0, cs], ALU.mult, ALU.subtract, nct.b + c3.b, r1.b)
                            kb.cp(c3.t[:, 1, cs], r1.t[:], r1.b, c3.b)
                            kb.tt(r1.t[:], r1.t[:], c3.t[:, 1, cs], ALU.subtract, r1.b + c3.b, r1.b)
                            kb.cp(c3.t[:, 2, cs], r1.t[:], r1.b, c3.b)
                            pbt = bank[6 + (pjk[0] % 2)]
                            pjk[0] += 1
                            for k_ in range(4):
                                kb.tr(pbt.t[:, k_ * 8:(k_ + 1) * 8], nct.t[0:8, k_ * 128:(k_ + 1) * 128], ident.t[0:8, 0:8], nct.b + ident.b, pbt.b)
                            evac(negc_tm.t[:, 4 * tt:4 * tt + 4, :], pbt.t[:, 0:32].rearrange("p (a b) -> p a b", a=4), pbt.b, negc_tm.b)

                        proj_fm(w, 8, h2T, fgate)
                        for j in range(4):
                            w = wload(wr, 1304 + 128 * j, 128)
                            proj_fm(w, 128, h2T, lambda tt, cs, pb, j=j: evac(qfo.t[:, j, cs], pb.t[:], pb.b, [qfo.b[j]], scale=0.125))
                        for j in range(4):
                            w = wload(wr, 1816 + 128 * j, 128)
                            proj_fm(w, 128, h2T, lambda tt, cs, pb, j=j: evac(kfT.t[:, j, cs], pb.t[:], pb.b, [kfT.b[j]]))
                        for j in range(4):
                            w = wload(wr, 2328 + 128 * j, 128)
                            proj_tm(w, 128, h2T, lambda tb, pb, j=j: evac(vf_tm.t[:, tb, 2 * j:2 * j + 2, 0:64],
                                                                         pb.t[:, 0:128].rearrange("p (g d) -> p g d", g=2), pb.b, [vf_tm.b[tb]]))
                        if dbg:
                            dump("c3", c3, c3.t[:], BF16)
                            dump("negc", negc_tm, negc_tm.t[:], F32)
                    with Phase(kb) as pb_:
                      if stage >= 2.35:
                        qfa = Ring(kb, pb_, [128, S], BF16, 2, "qfa")
                        kfa = Ring(kb, pb_, [128, S], BF16, 2, "kfa")
                        for t_ in qfa.tiles + kfa.tiles:
                            kb.memset(t_.t[64:128, :], 0.0, t_.b)
                        for t_ in kfa.tiles:
                            kb.memset(t_.t[64:67, :], 1.0, t_.b)
                        oacc = kb.sb(pb_, [128, 16, 2, 64], F32, nb=16, name="oaccf")
                        pr = Ring(kb, pb_, [128, 512], BF16, 12, "Pf", nb=4)
                        smr = Ring(kb, pb_, [128, 8], F32, 4, "smf")
                        ostf = kb.sb(pb_, [128, 16, 65], F32, nb=16, name="ostf")
                        rdf = Ring(kb, pb_, [128, 16, 2], F32, 2, "rdf")

                        def causal_chunks(qt):
                            return [(kc, max(0, 128 * kc - 512 * qt), 512) for kc in range(4 * qt + 4)]

                        def prep_fox(h):
                            r0_, j_ = (h % 2) * 64, h // 2
                            qa = qfa.next()
                            kb.cp(qa.t[0:64, :], qfo.t[r0_:r0_ + 64, j_, :], [qfo.b[j_]], qa.b)
                            for i in range(3):
                                kb.dma(SP, qa.t[64 + i:65 + i, :], c3.t[h:h + 1, i, :], c3.b, qa.b)
                            ka = kfa.next()
                            kb.cp(ka.t[0:64, :], kfT.t[r0_:r0_ + 64, j_, :], [kfT.b[j_]], ka.b)
                            return qa, ka

                        fpre = {0: prep_fox(0)}
                        for j in range(4):
                            for half in range(2):
                                h = 2 * j + half
                                r0 = half * 64
                                qa, ka = fpre.pop(h)
                                if h + 1 < 8:
                                    fpre[h + 1] = prep_fox(h + 1)

                                def qk_fox(sbk, kc, qt, lo, hi, qa=qa, ka=ka):
                                    kb.mm(sbk.t[:, lo:hi], ka.t[:, kc * 128:(kc + 1) * 128], qa.t[:, qt * 512 + lo:qt * 512 + hi], True, True,
                                          ka.b + qa.b, sbk.b)

                                def seg_fox(kc, qt, lo, hi, h=h):
                                    off = 512 * qt - 128 * kc
                                    segs = [("const", lo, hi, negc_tm.t[:, kc, h:h + 1], negc_tm.b)]
                                    if off <= 0:
                                        segs.append(("mul", -off, -off + 128, tri.t[:], tri.b))
                                    return segs

                                def fin_fox(qt, qb, reg, rb, half=half):
                                    kb.cp(ostf.t[:, 4 * qt:4 * qt + 4, :], reg, rb, ostf.b[4 * qt:4 * qt + 4])

                                def norm_fox(half=half):
                                    rd = rdf.next()
                                    kb.op(DVE, lambda: nc.vector.reciprocal(out=rd.t[:, :, 0:1], in_=ostf.t[:, :, 64:65]), ostf.b, rd.b)
                                    kb.tt(oacc.t[:, :, half, :], ostf.t[:, :, 0:64], rd.t[:, :, 0:1].to_broadcast([128, 16, 64]), ALU.mult,
                                          ostf.b + rd.b, oacc.b)

                                attend(pr, causal_chunks, qk_fox, seg_fox, lambda kc, h=h: (vf_tm.t[:, kc, h, :], [vf_tm.b[kc]]), fin_fox, 128, 65, None)
                                pipe.add(8, norm_fox, prio=-1)
                            if dbg and j == 0:
                                dump("oaccf0", oacc, oacc.t[:], F32)
                            pipe.flush()
                            transpose_o(oacc, qfo, j)

                with Phase(kb) as pg:
                  if stage >= 2.4:
                    h2T = kb.sb(pg, [128, KC, S], BF16, nb=KC * 4, name="h2Tg")
                    for tt_ in range(4):
                        rmsnorm_apply(1, h2T, tt_, rstd_mix.t[:, tt_ * 512:(tt_ + 1) * 512], [rstd_mix.b[tt_]])
                    yT = kb.sb(pg, [128, KC, S], BF16, nb=KC * 4, name="yT")
                    wga = Ring(kb, pg, [128, KC, 128], BF16, 2, "wga")
                    wgb = Ring(kb, pg, [128, KC, 128], BF16, 2, "wgb")
                    wua = Ring(kb, pg, [128, 4, 128], BF16, 2, "wua")
                    wub = Ring(kb, pg, [128, 4, 128], BF16, 2, "wub")
                    sgr = Ring(kb, pg, [128, 512], F32, 3, "sgg")
                    y1r = Ring(kb, pg, [128, 512], F32, 2, "y1")
                    wupav = wupa_d.rearrange("(k p) f -> p k f", p=128)
                    wupbv = wupb_d.rearrange("(k p) f -> p k f", p=128)
                    for dj in range(KC):
                        ds_ = slice(dj * 128, (dj + 1) * 128)
                        ga = wload(wga, 2848 + 128 * dj, 128)
                        gb = wload(wgb, 3872 + 128 * dj, 128)
                        ua = wua.next()
                        kb.dma(POOL, ua.t[:], wupav[:, :, ds_], (), ua.b)
                        ub = wub.next()
                        kb.dma(POOL, ub.t[:], wupbv[:, :, ds_], (), ub.b)
                        for tt in range(4):
                            cs = slice(tt * 512, (tt + 1) * 512)
                            for kc in range(KC):
                                kb.mm(bank[0].t[:], ga.t[:, kc, :], h2T.t[:, kc, cs], kc == 0, kc == KC - 1, ga.b + [h2T.b[kc * 4 + tt]], bank[0].b)
                            sa = sgr.next()
                            kb.actv(sa.t[:], bank[0].t[:], AF.Sigmoid, bank[0].b, sa.b)
                            for k_ in range(4):
                                kb.mm(bank[1].t[:], ua.t[:, k_, :], qno.t[:, k_, cs], k_ == 0, k_ == 3, ua.b + [qno.b[k_]], bank[1].b)
                            y1 = y1r.next()
                            kb.tt(y1.t[:], sa.t[:], bank[1].t[:], ALU.mult, sa.b + bank[1].b, y1.b)
                            for kc in range(KC):
                                kb.mm(bank[2].t[:], gb.t[:, kc, :], h2T.t[:, kc, cs], kc == 0, kc == KC - 1, gb.b + [h2T.b[kc * 4 + tt]], bank[2].b)
                            sb_ = sgr.next()
                            kb.actv(sb_.t[:], bank[2].t[:], AF.Sigmoid, bank[2].b, sb_.b)
                            for k_ in range(4):
                                kb.mm(bank[3].t[:], ub.t[:, k_, :], qfo.t[:, k_, cs], k_ == 0, k_ == 3, ub.b + [qfo.b[k_]], bank[3].b)
                            kb.tt(sb_.t[:], sb_.t[:], bank[3].t[:], ALU.mult, sb_.b + bank[3].b, sb_.b)
                            kb.tt(yT.t[:, dj, cs], y1.t[:], sb_.t[:], ALU.add, y1.b + sb_.b, [yT.b[dj * 4 + tt]])
                    if dbg:
                        dump("yT", yT, yT.t[:], BF16)
                    wov = wout_d.rearrange("(k p) f -> p k f", p=128)
                    k = 0
                    for dj in range(KC):
                        wo = wga.next()
                        kb.dma(POOL, wo.t[:], wov[:, :, dj * 128:(dj + 1) * 128], (), wo.b)
                        for tt in range(4):
                            cs = slice(tt * 512, (tt + 1) * 512)
                            pb = bank[4 + (k % 2)]
                            k += 1
                            for kc in range(KC):
                                kb.mm(pb.t[:], wo.t[:, kc, :], yT.t[:, kc, cs], kc == 0, kc == KC - 1, wo.b + [yT.b[kc * 4 + tt]], pb.b)
                            kb.tt(xT.t[:, dj, cs], xT.t[:, dj, cs], pb.t[:], ALU.add, pb.b + [xb(dj, tt)], [xb(dj, tt)])
                if dbg:
                    dump("xT2", xT, xT.t[:], F32)

        def memattn():
            with Phase(kb) as pm:
                h3T = kb.sb(pm, [128, KC, S], BF16, nb=KC * 4, name="h3T")
                with Phase(kb) as pr_:
                    rmsnorm_fm(pr_, 2, h3T)
                omT = kb.sb(pm, [128, KC, S], BF16, nb=KC, name="omT")
                mem_nT = kb.sb(pm, [128, KC, MEM], BF16, name="memnT")
                kmT = kb.sb(pm, [128, KC, MEM], BF16, nb=KC, name="kmT")
                vm = kb.sb(pm, [128, 2, 4, 257], BF16, nb=2, name="vm")
                kb.memset(vm.t[:, :, :, 256:257], 1.0, vm.b)
                wr = Ring(kb, pm, [128, KC, 128], BF16, 5, "wm")
                with Phase(kb) as p1:
                    mt = kb.sb(p1, [128, 2, D], F32, name="mt")
                    kb.dma(SP, mt.t[:], mem_d.rearrange("(b p) d -> p b d", p=128), (), mt.b)
                    gkv = kb.sb(p1, [128, D], F32, name="gkv")
                    kb.dma(SP, gkv.t[:], gbc_d[:, 0, :], (), gkv.b)
                    sqj = kb.sb(p1, [128, D], F32, name="sqjm")
                    ss = kb.sb(p1, [128, 4], F32, name="ssm")
                    mn = kb.sb(p1, [128, 2, D], F32, name="mn")
                    for b in range(2):
                        kb.tt(sqj.t[:], mt.t[:, b, :], mt.t[:, b, :], ALU.mult, mt.b, sqj.b)
                        kb.op(DVE, lambda: nc.vector.reduce_sum(out=ss.t[:, b:b + 1], in_=sqj.t[:], axis=AX.X), sqj.b, ss.b)
                        kb.actv(ss.t[:, 2 + b:3 + b], ss.t[:, b:b + 1], AF.Sqrt, ss.b, ss.b, bias=EPS, scale=1.0 / D)
                        kb.op(DVE, lambda: nc.vector.reciprocal(out=ss.t[:, 2 + b:3 + b], in_=ss.t[:, 2 + b:3 + b]), ss.b, ss.b)
                        kb.stt(mn.t[:, b, :], mt.t[:, b, :], ss.t[:, 2 + b:3 + b], gkv.t[:], ALU.mult, ALU.mult, mt.b + ss.b + gkv.b, mn.b)
                        for h4 in range(2):
                            pb = bank[6 + (pjk[0] % 2)]
                            pjk[0] += 1
                            for k_ in range(4):
                                kc = h4 * 4 + k_
                                kb.tr(pb.t[:, k_ * 128:(k_ + 1) * 128], mn.t[:, b, kc * 128:(kc + 1) * 128], ident.t[:], mn.b + ident.b, pb.b)
                            evac(mem_nT.t[:, h4 * 4:h4 * 4 + 4, b * 128:(b + 1) * 128], pb.t[:].rearrange("p (j t) -> p j t", j=4), pb.b, mem_nT.b)
                wkvv = wkv_d.rearrange("(k p) f -> p k f", p=128)
                wqv = wq_d.rearrange("(k p) f -> p k f", p=128)
                wov = wo_d.rearrange("(k p) f -> p k f", p=128)
                for hc in range(KC):
                    w = wr.next()
                    kb.dma(POOL, w.t[:], wkvv[:, :, hc * 128:(hc + 1) * 128], (), w.b)
                    pb = bank[6 + (pjk[0] % 2)]
                    pjk[0] += 1
                    for kc in range(KC):
                        kb.mm(pb.t[:, 0:MEM], w.t[:, kc, :], mem_nT.t[:, kc, :], kc == 0, kc == KC - 1, w.b + mem_nT.b, pb.b)
                    evac(kmT.t[:, hc, :], pb.t[:, 0:MEM], pb.b, [kmT.b[hc]])
                for hv in range(4):
                    for c in range(2):
                        w = wr.next()
                        c0 = D + hv * 256 + c * 128
                        kb.dma(POOL, w.t[:], wkvv[:, :, c0:c0 + 128], (), w.b)
                        for b in range(2):
                            pb = bank[6 + (pjk[0] % 2)]
                            pjk[0] += 1
                            for kc in range(KC):
                                kb.mm(pb.t[:, 0:128], mem_nT.t[:, kc, b * 128:(b + 1) * 128], w.t[:, kc, :], kc == 0, kc == KC - 1, w.b + mem_nT.b, pb.b)
                            evac(vm.t[:, b, hv, c * 128:(c + 1) * 128], pb.t[:, 0:128], pb.b, [vm.b[b]])
                qmr = Ring(kb, pm, [128, 2, S], BF16, 2, "qm")
                oma = kb.sb(pm, [128, 16, 256], F32, nb=16, name="oma")
                pr = Ring(kb, pm, [128, 512], BF16, 12, "Pm", nb=4)
                smr = Ring(kb, pm, [128, 8], F32, 4, "smm")
                for hv in range(4):
                    qm = qmr.next()
                    for c in range(2):
                        w = wr.next()
                        c0 = hv * 256 + c * 128
                        kb.dma(POOL, w.t[:], wqv[:, :, c0:c0 + 128], (), w.b)
                        proj_fm(w, 128, h3T, lambda tt, cs, pb, c=c, qm=qm: evac(qm.t[:, c, cs], pb.t[:], pb.b, qm.b, scale=0.0625))

                    def qk_mem(sbk, kc, qt, lo, hi, hv=hv, qm=qm):
                        for c in range(2):
                            kb.mm(sbk.t[:, 0:512], kmT.t[:, hv * 2 + c, kc * 128:(kc + 1) * 128], qm.t[:, c, qt * 512:(qt + 1) * 512], c == 0, c == 1,
                                  [kmT.b[hv * 2 + c]] + qm.b, sbk.b)

                    def fin_mem(qt, qb, reg, rb):
                        tb = 4 * qt + qb
                        sm = smr.next()
                        kb.op(DVE, lambda: nc.vector.reciprocal(out=sm.t[:, 1:2], in_=reg[:, 256:257]), rb, sm.b)
                        kb.ts(oma.t[:, tb, :], reg[:, 0:256], sm.t[:, 1:2], None, ALU.mult, None, rb + sm.b, [oma.b[tb]])

                    attend(pr, lambda qt: [(0, 0, 512), (1, 0, 512)], qk_mem, lambda kc, qt, lo, hi: [("const", 0, 512, 0.0, [])],
                           lambda kc, hv=hv: (vm.t[:, kc, hv, :], [vm.b[kc]]), fin_mem, 128, 257, None)
                    pipe.flush()
                    for c in range(2):
                        for t4 in range(4):
                            pb = bank[6 + (pjk[0] % 2)]
                            pjk[0] += 1
                            for k_ in range(4):
                                tb = t4 * 4 + k_
                                kb.tr(pb.t[:, k_ * 128:(k_ + 1) * 128], oma.t[:, tb, c * 128:(c + 1) * 128], ident.t[:], [oma.b[tb]] + ident.b, pb.b)
                            evac(omT.t[:, hv * 2 + c, t4 * 512:(t4 + 1) * 512], pb.t[:], pb.b, [omT.b[hv * 2 + c]])
                k = 0
                for dj in range(KC):
                    wo = wr.next()
                    kb.dma(POOL, wo.t[:], wov[:, :, dj * 128:(dj + 1) * 128], (), wo.b)
                    for tt in range(4):
                        cs = slice(tt * 512, (tt + 1) * 512)
                        pb = bank[(k % 2)]
                        k += 1
                        for kc in range(KC):
                            kb.mm(pb.t[:], wo.t[:, kc, :], omT.t[:, kc, cs], kc == 0, kc == KC - 1, wo.b + [omT.b[kc]], pb.b)
                        kb.tt(xT.t[:, dj, cs], xT.t[:, dj, cs], pb.t[:], ALU.add, pb.b + [xb(dj, tt)], [xb(dj, tt)])

        if stage >= 2:
            mixer()
        if stage >= 3:
            memattn()
            if dbg:
                dump("xT3", xT, xT.t[:], F32)

        if stage >= 4:
            with Phase(kb) as ph:
                hT = kb.sb(ph, [128, KC, S], BF16, nb=KC * 4, name="hT2")
                pre = ffn_prefetch(ph, w_d["ffn2_g"], w_d["ffn2_u"], w_d["ffn2_d"])
                rmsnorm_fm(ph, 3, hT)
                ffn(ph, hT, w_d["ffn2_g"], w_d["ffn2_u"], w_d["ffn2_d"], pre)

        with Phase(kb) as ph:
            gbc = kb.sb(ph, [128, D], F32, name="gfin")
            kb.dma(SP, gbc.t[:], gbc_d[:, 1, :], (), gbc.b)
            xor_ = Ring(kb, ph, [128, D], F32, 4, "xo")
            sqj = kb.sb(ph, [128, D], F32, name="sqj")
            ssr = Ring(kb, ph, [128, 2], F32, 2, "ss")
            outs = []
            for tb in range(16):
                xo = xor_.next()
                for half in range(2):
                    pb = bank[6 + half]
                    for j in range(4):
                        kc = half * 4 + j
                        kb.tr(pb.t[:, j * 128:(j + 1) * 128], xT.t[:, kc, tb * 128:(tb + 1) * 128], ident.t[:],
                              [xb(kc, tb // 4)] + ident.b, pb.b)
                    kb.cp(xo.t[:, half * 512:(half + 1) * 512], pb.t[:], pb.b, xo.b, eng=(DVE if half == 0 else ACT))
                ss = ssr.next()
                kb.memset(ss.t[:, 0:1], 0.0, ss.b)
                kb.actv(sqj.t[:], xo.t[:], AF.Square, xo.b + ss.b, sqj.b + ss.b, accum_out=ss.t[:, 0:1])
                kb.actv(ss.t[:, 1:2], ss.t[:, 0:1], AF.Sqrt, ss.b, ss.b, bias=EPS, scale=1.0 / D)
                kb.op(DVE, lambda: nc.vector.reciprocal(out=ss.t[:, 1:2], in_=ss.t[:, 1:2]), ss.b, ss.b)
                kb.stt(xo.t[:], xo.t[:], ss.t[:, 1:2], gbc.t[:], ALU.mult, ALU.mult, xo.b + ss.b + gbc.b, xo.b)
                outs.append(kb.dma(SP, out_d[tb * 128:(tb + 1) * 128, :], xo.t[:], xo.b, ()))
            for t in outs + dump_toks:
                kb.wait(SP, t, True)
            import os
            if os.environ.get("ENGCOUNTS"):
                print("ENGCOUNTS pe", PE.count, "act", ACT.count, "dve", DVE.count, "pool", POOL.count, "sp", SP.count,
                      "dma_sp", sum(SP.dcnt), "dma_pool", sum(POOL.dcnt))
    return nc


_CACHE = {}


def _prep_shared(inp):
    sq = lambda a: np.ascontiguousarray(np.asarray(a, dtype=np.float32))
    fm = lambda g: np.asarray(g, np.float32).reshape(KC, 128).T
    gains = np.zeros((128, 5, KC), np.float32)
    gains[:, 0] = fm(inp["ffn1_norm"][0])
    gains[:, 1] = fm(inp["mix_norm"][0])
    gains[:, 2] = fm(inp["mem_q_norm"][0])
    gains[:, 3] = fm(inp["ffn2_norm"][0])
    gbc = np.zeros((128, 2, D), np.float32)
    gbc[:, 0] = np.asarray(inp["mem_kv_norm"][0], np.float32)[None, :]
    gbc[:, 1] = np.asarray(inp["final_norm"], np.float32)[None, :]
    sh = {
        "gains": gains,
        "gbc": gbc,
        "ident": np.eye(128, dtype=np.float32),
    }
    tbl = np.asarray(inp["rel_bias_table"], np.float32)

    def bucket(n):
        n = np.maximum(n, 0)
        nf = np.maximum(n, 1).astype(np.float32)
        large = 16 + (np.log(nf / np.float32(16)) / np.float32(math.log(128 / 16)) * np.float32(16)).astype(np.int32)
        large = np.minimum(large, 31)
        return np.where(n < 16, n, large)

    p_ = np.arange(128)[:, None]
    m_ = np.arange(256)[None, :]
    dist = m_ - p_
    bk = bucket(dist)
    bdiag = np.empty((8, 128, 256), np.float32)
    bwf = np.empty((8, 128, 128), np.float32)
    bcmp = np.empty((8, 4, 128, 512), np.float32)
    c_ = np.arange(128)[:, None]
    for h in range(8):
        bdiag[h] = np.where(dist >= 0, tbl[bk, h], np.float32(NEG))
        bwf[h] = np.where(p_ > np.arange(128)[None, :], tbl[31, h], np.float32(NEG))
        for qt in range(4):
            dc = qt * 512 + np.arange(512)[None, :] - (16 * c_ + 31)
            v = np.where(dc >= 0, tbl[bucket(dc), h], np.float32(NEG))
            v[127, :] = NEG
            bcmp[h, qt] = v
    sh["bdiag"] = bdiag
    sh["bwf"] = bwf
    sh["bcmp"] = bcmp
    sh["tbl31"] = np.ascontiguousarray(np.broadcast_to(tbl[31][None, :], (128, 8)))
    c0 = np.arange(127)[:, None] * 16
    s0 = np.arange(32)[None, :] * 64
    ov = np.clip(np.minimum(c0 + 32, s0 + 64) - np.maximum(c0, s0), 0, None) / 16
    ovaug = np.zeros((128, 33), np.float32)
    ovaug[:127, :32] = ov
    ovaug[:127, 32] = 1.0
    sh["ovaug"] = ovaug
    t_ = np.arange(S)[:, None]
    blk = np.arange(32)[None, :]
    cur = t_ // 64
    forced = (blk == 0) | (blk == cur) | (blk == cur - 1)
    valid = blk * 64 <= t_
    vm = valid.astype(np.float32)
    add2 = np.where(valid, np.where(forced, np.float32(1e4), np.float32(0.0)), np.float32(NEG)).astype(np.float32)
    impm = np.stack([vm, add2], 0).reshape(2, 16, 128, 32).transpose(2, 0, 1, 3)
    sh["impm"] = np.ascontiguousarray(impm)
    sh["esel"] = (np.arange(S)[None, :] // 64 == np.arange(32)[:, None]).astype(np.float32)
    sh["tri"] = (np.arange(128)[None, :] >= np.arange(128)[:, None]).astype(np.float32)
    sh["bforget"] = np.asarray(inp["mix_b_forget"][0], np.float32).reshape(8, 1)
    sh["posT"] = np.ascontiguousarray(np.stack([np.asarray(inp["cmp_pos_k"][0], np.float32).T, np.asarray(inp["cmp_pos_v"][0], np.float32).T], 0))
    for nm in ("cmp_k_w1", "cmp_v_w1", "cmp_k_w2", "cmp_v_w2", "w_up_nsa", "w_up_fox", "mix_w_out", "mem_w_q", "mem_w_kv", "mem_w_o"):
        sh[nm] = sq(inp[nm][0])
    sh["w_in"] = sq(inp["mix_w_in"][0])
    for nm in ("ffn1", "ffn2"):
        sh[nm + "_w_gate"] = sq(inp[nm + "_w_gate"][0])
        sh[nm + "_w_up"] = sq(inp[nm + "_w_up"][0])
        sh[nm + "_w_down"] = sq(inp[nm + "_w_down"][0])
    return sh


def kernel(_stage=9, _ncores=8, _dbg=False, **inp):
    key = ("prog", _stage, _dbg)
    if key not in _CACHE:
        _CACHE[key] = build_program(_stage, _dbg)
    nc = _CACHE[key]
    sh = _prep_shared(inp)
    x = np.asarray(inp["x"], np.float32)
    mem = np.asarray(inp["mem"], np.float32)
    in_maps = []
    for b in range(_ncores):
        m = dict(sh)
        m["x"] = np.ascontiguousarray(x[b])
        m["mem"] = np.ascontiguousarray(mem[b])
        in_maps.append(m)
    res = run_bass_kernel_spmd(nc, in_maps, core_ids=list(range(_ncores)))
    out = np.stack([np.asarray(r["out"], np.float32) for r in res.results], axis=0)
    if _dbg:
        return out, res.results
    return out
```
